# Optimizing a Trainium2 kernel written in Bass

```python
import math
import jax, jax.numpy as jnp
from jax import lax
import numpy as np

D_MODEL = 1024
BATCH = 4
SEQ = 4096
DEPTH = 4

D_MIX = D_MODEL
LRU_WIDTH = D_MIX // 2
LRU_BLOCKS = 8
LRU_BLOCK_W = LRU_WIDTH // LRU_BLOCKS
LRU_C = 8.0
CONV_WIDTH = 4
CONV_PAD = (CONV_WIDTH // 2, CONV_WIDTH - 1 - CONV_WIDTH // 2)
N_HEADS = 8
N_KV_HEADS = 2
KV_GROUP = N_HEADS // N_KV_HEADS
HEAD_DIM = (D_MIX - LRU_WIDTH) // N_HEADS
ATT_WIDTH = N_HEADS * HEAD_DIM
KV_WIDTH = N_KV_HEADS * HEAD_DIM
WINDOW = 128
BLOCK = 128
N_BUCKETS = 32
MAX_DISTANCE = 128
D_FF = ((8 * D_MODEL // 3 + 255) // 256) * 256
FFN_RES = 0.5
EPS = 1e-6
NEG_INF = -1e30
D_IN = 2 * LRU_WIDTH + ATT_WIDTH + 2 * KV_WIDTH
SPLITS = (LRU_WIDTH, 2 * LRU_WIDTH, 2 * LRU_WIDTH + ATT_WIDTH, 2 * LRU_WIDTH + ATT_WIDTH + KV_WIDTH)

kernel_name = 'hymba_style_rglru_swa_macaron_encoder'


def rms_norm(x, g):
    x32 = x.astype(jnp.float32)
    y = x32 * lax.rsqrt(jnp.mean(x32 * x32, axis=-1, keepdims=True) + EPS)
    return (y * g.astype(jnp.float32)).astype(x.dtype)


def swiglu(x, w_gate, w_up, w_down):
    return (jax.nn.silu(x @ w_gate) * (x @ w_up)) @ w_down


def t5_buckets(rel):
    half = N_BUCKETS // 2
    max_exact = half // 2
    ret = (rel > 0).astype(jnp.int32) * half
    n = jnp.abs(rel)
    n_f = jnp.maximum(n, 1).astype(jnp.float32)
    large = max_exact + (jnp.log(n_f / max_exact) / math.log(MAX_DISTANCE / max_exact) * (half - max_exact)).astype(jnp.int32)
    large = jnp.minimum(large, half - 1)
    return ret + jnp.where(n < max_exact, n, large)


def band_layout(seq):
    nb = seq // BLOCK
    n_idx = jnp.arange(nb)[:, None, None]
    t = jnp.arange(BLOCK)[None, :, None]
    j = jnp.arange(3 * BLOCK)[None, None, :]
    rel = j - BLOCK - t
    key_pos = (n_idx - 1) * BLOCK + j
    mask = (jnp.abs(rel) <= WINDOW) & (key_pos >= 0) & (key_pos < seq)
    return t5_buckets(rel[0]), mask


def band_windows(t, nb):
    b = t.shape[0]
    tp = jnp.pad(t, ((0, 0), (BLOCK, BLOCK), (0, 0), (0, 0)))
    tb = tp.reshape(b, nb + 2, BLOCK, N_KV_HEADS, HEAD_DIM)
    return jnp.concatenate([tb[:, :-2], tb[:, 1:-1], tb[:, 2:]], axis=2)


def windowed_gqa(q, k, v, sink, rel_bias):
    b, s = q.shape[0], q.shape[1]
    nb = s // BLOCK
    qb = q.reshape(b, nb, BLOCK, N_KV_HEADS, KV_GROUP, HEAD_DIM) * (HEAD_DIM ** -0.5)
    kw = band_windows(k.reshape(b, s, N_KV_HEADS, HEAD_DIM), nb)
    vw = band_windows(v.reshape(b, s, N_KV_HEADS, HEAD_DIM), nb)
    buckets, mask = band_layout(s)
    bias = rel_bias[buckets].astype(jnp.float32)
    bias = jnp.transpose(bias, (2, 0, 1)).reshape(N_KV_HEADS, KV_GROUP, BLOCK, 3 * BLOCK)
    logits = jnp.einsum('bnqkgd,bnjkd->bnkgqj', qb, kw).astype(jnp.float32) + bias
    logits = jnp.where(mask[None, :, None, None], logits, NEG_INF)
    sink32 = sink.astype(jnp.float32).reshape(1, 1, N_KV_HEADS, KV_GROUP, 1, 1)
    m = jnp.maximum(jnp.max(logits, axis=-1, keepdims=True), sink32)
    p = jnp.exp(logits - m)
    p = p / (jnp.sum(p, axis=-1, keepdims=True) + jnp.exp(sink32 - m))
    o = jnp.einsum('bnkgqj,bnjkd->bnqkgd', p.astype(vw.dtype), vw)
    return o.reshape(b, s, ATT_WIDTH)


def _linear_combine(left, right):
    a_l, b_l = left
    a_r, b_r = right
    return a_l * a_r, a_r * b_l + b_r


def rg_lru_direction(xc, w_a, b_a, w_x, b_x, lam, reverse):
    b, s = xc.shape[0], xc.shape[1]
    xb = xc.reshape(b, s, LRU_BLOCKS, LRU_BLOCK_W)
    r = jax.nn.sigmoid(jnp.einsum('bsnc,ncd->bsnd', xb, w_a).reshape(b, s, LRU_WIDTH).astype(jnp.float32) + b_a.astype(jnp.float32))
    i = jax.nn.sigmoid(jnp.einsum('bsnc,ncd->bsnd', xb, w_x).reshape(b, s, LRU_WIDTH).astype(jnp.float32) + b_x.astype(jnp.float32))
    log_a = -LRU_C * jax.nn.softplus(-lam.astype(jnp.float32)) * r
    a = jnp.exp(log_a)
    u = jnp.sqrt(-jnp.expm1(2.0 * log_a)) * (i * xc.astype(jnp.float32))
    _, h = lax.associative_scan(_linear_combine, (a, u), axis=1, reverse=reverse)
    return h


def recurrent_group(xr, gate, conv_w, conv_b, w_a, b_a, w_x, b_x, lam):
    xc = lax.conv_general_dilated(xr, conv_w[:, None, :], window_strides=(1,), padding=[CONV_PAD],
                                  dimension_numbers=('NWC', 'WIO', 'NWC'), feature_group_count=LRU_WIDTH) + conv_b
    h = (rg_lru_direction(xc, w_a[0], b_a[0], w_x[0], b_x[0], lam[0], False)
         + rg_lru_direction(xc, w_a[1], b_a[1], w_x[1], b_x[1], lam[1], True))
    return (jax.nn.gelu(gate.astype(jnp.float32)) * h).astype(xr.dtype)


def setup_inputs(seed: int = 0) -> dict:
    key = jax.random.key(seed)
    ks = jax.random.split(key, 32)
    f32 = jnp.float32

    def nrm(k, shape, fan_in):
        return jax.random.normal(k, shape, f32) * (fan_in ** -0.5)

    def gain(k, shape):
        return 1.0 + 0.02 * jax.random.normal(k, shape, f32)

    def small(k, shape, scale=0.01):
        return scale * jax.random.normal(k, shape, f32)

    u = jax.random.uniform(ks[14], (DEPTH, 2, LRU_WIDTH), f32, minval=0.9, maxval=0.999)
    a0 = u ** (1.0 / LRU_C)
    lru_lambda = jnp.log(a0) - jnp.log1p(-a0)
    return {
        'x': jax.random.normal(ks[0], (BATCH, SEQ, D_MODEL), f32),
        'ffn1_norm': gain(ks[1], (DEPTH, D_MODEL)),
        'ffn1_w_gate': nrm(ks[2], (DEPTH, D_MODEL, D_FF), D_MODEL),
        'ffn1_w_up': nrm(ks[3], (DEPTH, D_MODEL, D_FF), D_MODEL),
        'ffn1_w_down': nrm(ks[4], (DEPTH, D_FF, D_MODEL), D_FF),
        'mix_norm': gain(ks[5], (DEPTH, D_MODEL)),
        'w_in': nrm(ks[6], (DEPTH, D_MODEL, D_IN), D_MODEL),
        'conv_w': nrm(ks[7], (DEPTH, CONV_WIDTH, LRU_WIDTH), CONV_WIDTH),
        'conv_b': small(ks[8], (DEPTH, LRU_WIDTH)),
        'lru_w_a': nrm(ks[9], (DEPTH, 2, LRU_BLOCKS, LRU_BLOCK_W, LRU_BLOCK_W), LRU_BLOCK_W),
        'lru_b_a': small(ks[10], (DEPTH, 2, LRU_WIDTH), 0.1),
        'lru_w_x': nrm(ks[11], (DEPTH, 2, LRU_BLOCKS, LRU_BLOCK_W, LRU_BLOCK_W), LRU_BLOCK_W),
        'lru_b_x': small(ks[12], (DEPTH, 2, LRU_WIDTH), 0.1),
        'lru_lambda': lru_lambda,
        'attn_sink': 0.5 * jax.random.normal(ks[15], (DEPTH, N_HEADS), f32),
        'rel_bias': 0.2 * jax.random.normal(ks[16], (N_BUCKETS, N_HEADS), f32),
        'lru_out_norm': gain(ks[17], (DEPTH, LRU_WIDTH)),
        'attn_out_norm': gain(ks[18], (DEPTH, ATT_WIDTH)),
        'w_out': nrm(ks[19], (DEPTH, D_MIX, D_MODEL), D_MIX),
        'ffn2_norm': gain(ks[20], (DEPTH, D_MODEL)),
        'ffn2_w_gate': nrm(ks[21], (DEPTH, D_MODEL, D_FF), D_MODEL),
        'ffn2_w_up': nrm(ks[22], (DEPTH, D_MODEL, D_FF), D_MODEL),
        'ffn2_w_down': nrm(ks[23], (DEPTH, D_FF, D_MODEL), D_FF),
        'final_norm': gain(ks[24], (D_MODEL,)),
    }


def reference(x, ffn1_norm, ffn1_w_gate, ffn1_w_up, ffn1_w_down, mix_norm, w_in, conv_w, conv_b,
              lru_w_a, lru_b_a, lru_w_x, lru_b_x, lru_lambda, attn_sink, rel_bias,
              lru_out_norm, attn_out_norm, w_out, ffn2_norm, ffn2_w_gate, ffn2_w_up, ffn2_w_down,
              final_norm):
    for l in range(DEPTH):
        x = x + FFN_RES * swiglu(rms_norm(x, ffn1_norm[l]), ffn1_w_gate[l], ffn1_w_up[l], ffn1_w_down[l])
        h = rms_norm(x, mix_norm[l])
        proj = h @ w_in[l]
        xr, gate, q, k, v = jnp.split(proj, SPLITS, axis=-1)
        y_rec = recurrent_group(xr, gate, conv_w[l], conv_b[l], lru_w_a[l], lru_b_a[l],
                                lru_w_x[l], lru_b_x[l], lru_lambda[l])
        y_att = windowed_gqa(q, k, v, attn_sink[l], rel_bias)
        y = jnp.concatenate([rms_norm(y_rec, lru_out_norm[l]), rms_norm(y_att, attn_out_norm[l])], axis=-1)
        x = x + y @ w_out[l]
        x = x + FFN_RES * swiglu(rms_norm(x, ffn2_norm[l]), ffn2_w_gate[l], ffn2_w_up[l], ffn2_w_down[l])
    return rms_norm(x, final_norm)
```

```python
import math
from contextlib import ExitStack
import numpy as np
import concourse.bass as bass
import concourse.mybir as mybir
from concourse.bass_utils import run_bass_kernel_spmd

F32 = mybir.dt.float32
BF16 = mybir.dt.bfloat16
AF = mybir.ActivationFunctionType
ALU = mybir.AluOpType
AX = mybir.AxisListType

NCORES = 8
DEPTH = 4
T = 2048
D = 1024
DC = 8
DFF = 2816
FG = 11
DIN = 1792
NTG = 4
NB = 16
EPS = 1e-6
ENGS = ("pe", "act", "dve", "pool", "sp")


class _Ins:
    __slots__ = ("eng", "fn", "waits", "signal", "dma_sem", "ctr", "is_dma", "inc", "idx", "epoch")
    _epoch = 0
    _n = 0

    def __init__(self, eng, fn, is_dma=False, dma_sem=None, inc=16):
        self.eng = eng
        self.fn = fn
        self.waits = []
        self.signal = False
        self.dma_sem = dma_sem
        self.ctr = None
        self.is_dma = is_dma
        self.inc = inc
        _Ins._n += 1
        self.idx = _Ins._n
        self.epoch = _Ins._epoch


class Prog:
    def __init__(self, nc, stack):
        self.nc = nc
        self.stack = stack
        self.ins = []
        self.res = {}
        self.dma_tot = {}
        self.dma_sems = {}
        _Ins._epoch = 0
        self.eng_sems = {(e, 0): stack.enter_context(nc.semaphore("ctr_%s_0" % e)) for e in ENGS}
        self.last = {e: None for e in ENGS}

    def new_epoch(self):
        _Ins._epoch += 1
        for e in ENGS:
            self.eng_sems[(e, _Ins._epoch)] = self.stack.enter_context(self.nc.semaphore("ctr_%s_%d" % (e, _Ins._epoch)))

    def sbuf(self, name, shape, dt):
        return self.stack.enter_context(self.nc.sbuf_tensor(name, list(shape), dt))

    def psum(self, name, shape, dt=F32):
        return self.stack.enter_context(self.nc.psum_tensor(name, list(shape), dt))

    def dsem(self, name):
        if name not in self.dma_sems:
            self.dma_sems[name] = self.stack.enter_context(self.nc.semaphore("d_" + name))
            self.dma_tot[name] = 0
        return name

    def _deps(self, ins, reads, writes):
        evs = []
        for r in reads:
            st = self.res.setdefault(r, [[], []])
            evs.extend(st[0])
        for w in writes:
            st = self.res.setdefault(w, [[], []])
            for ev in st[0] + st[1]:
                if ev[0] == "dma" or ins.is_dma or ev[1].eng != ins.eng:
                    evs.append(ev)
        best = {}
        for ev in evs:
            if ev[0] == "dma":
                best[("d", ev[1])] = ("dma", ev[1], self.dma_tot[ev[1]])
            else:
                key = ("e", ev[1].eng, ev[1].epoch)
                if key not in best or ev[1].idx > best[key][1].idx:
                    best[key] = ev
        ins.waits = list(best.values())

    def _commit(self, ev, reads, writes):
        for r in reads:
            self.res[r][1].append(ev)
        for w in writes:
            self.res[w] = [[ev], []]

    def op(self, eng, fn, reads=(), writes=()):
        ins = _Ins(eng, fn)
        self._deps(ins, reads, writes)
        self.ins.append(ins)
        self._commit(("eng", ins), reads, writes)
        self.last[eng] = ins
        return ins

    def dma(self, q, sem, fn, reads=(), writes=(), inc=16):
        ins = _Ins(q, fn, is_dma=True, dma_sem=sem, inc=inc)
        self._deps(ins, reads, writes)
        self.dma_tot[sem] += inc
        ev = ("dma", sem, self.dma_tot[sem])
        self.ins.append(ins)
        self._commit(ev, reads, writes)
        return ev

    def I(self, eng, name, reads=(), writes=(), **kw):
        return self.op(eng, lambda e: getattr(e, name)(**kw), reads, writes)

    def D(self, q, sem, reads=(), writes=(), **kw):
        return self.dma(q, sem, lambda e: e.dma_start(**kw), reads, writes)

    def wait_all(self, eng, resources):
        ins = _Ins(eng, None)
        self._deps(ins, list(resources), [])
        self.ins.append(ins)

    def barrier(self, skip_sems=()):
        lasts = dict(self.last)
        dmas = [("dma", s, v) for s, v in self.dma_tot.items() if v > 0 and s not in skip_sems]
        for e in ENGS:
            ins = _Ins(e, None)
            ins.waits = [("eng", l) for e2, l in lasts.items() if l is not None and e2 != e] + dmas
            self.ins.append(ins)

    def emit(self):
        nc = self.nc
        for ins in self.ins:
            for ev in ins.waits:
                if ev[0] == "eng":
                    ev[1].signal = True
        cnt = {}
        for ins in self.ins:
            if ins.signal and not ins.is_dma:
                k_ = (ins.eng, ins.epoch)
                cnt[k_] = cnt.get(k_, 0) + 1
                ins.ctr = cnt[k_]
        self.counts = cnt
        per = {e: [i for i in self.ins if i.eng == e] for e in ENGS}

        def run(eng_name, eh):
            known = {}
            for ins in per[eng_name]:
                need = {}
                for ev in ins.waits:
                    if ev[0] == "eng":
                        key = ("e", ev[1].eng, ev[1].epoch)
                        val = ev[1].ctr
                    else:
                        key = ("d", ev[1])
                        val = ev[2]
                    if val > need.get(key, 0):
                        need[key] = val
                for key, val in need.items():
                    if known.get(key, 0) >= val:
                        continue
                    known[key] = val
                    sem = self.eng_sems[(key[1], key[2])] if key[0] == "e" else self.dma_sems[key[1]]
                    eh.wait_ge(sem, val)
                if ins.fn is None:
                    continue
                r = ins.fn(eh)
                if ins.is_dma:
                    if ins.inc == 16:
                        r.then_inc(self.dma_sems[ins.dma_sem], 16)
                    else:
                        r.then_inc(self.dma_sems[ins.dma_sem])
                elif ins.signal:
                    r.then_inc(self.eng_sems[(eng_name, ins.epoch)], 1)

        with nc.Block() as block:
            @block.tensor
            def _(e):
                run("pe", e)

            @block.scalar
            def _(e):
                run("act", e)

            @block.vector
            def _(e):
                run("dve", e)

            @block.gpsimd
            def _(e):
                run("pool", e)

            @block.sync
            def _(e):
                run("sp", e)


V_F1G, V_MIXG, V_F2G = 0, 8, 16
V_CONV = 24
V_CONVB = 44
V_BA = 48
V_BX = 56
V_LAM = 64
V_RECG = 72
V_ATTG = 76
V_SINK = 84
VL = 92
V_FINAL = DEPTH * VL
NV = V_FINAL + 8


V_ATTG4 = V_ATTG


class K:
    def __init__(self, layers, last, debug=False, phases=("f1", "mx", "f2")):
        self.layers = list(layers)
        self.last = last
        self.debug = debug
        self.phases = phases
        self.nc = bass.Bass("TRN2", target_bir_lowering=False)
        self.dbg = []
        self.bank = 0
        self.ffn_pref = set()
        self.in_names = []
        import os
        self.mx_stop = os.environ.get("MXSTOP", "")

    def dram_in(self, name, shape, dt=F32):
        self.in_names.append(name)
        return self.nc.dram_tensor(name, list(shape), dt, kind="ExternalInput").ap()

    def build(self):
        nc = self.nc
        self.x_in = self.dram_in("x_in", [T, D])
        self.vecs_in = self.dram_in("vecs", [128, NV])
        self.sel_in = self.dram_in("sel", [128, 2])
        self.bias_in = self.dram_in("bias", [128, 4096])
        self.w = {}
        for l in self.layers:
            if "f1" in self.phases:
                self.w[(l, "wg", 1)] = self.dram_in("f1_wg_%d" % l, [D, DFF])
                self.w[(l, "wu", 1)] = self.dram_in("f1_wu_%d" % l, [D, DFF])
                self.w[(l, "wd", 1)] = self.dram_in("f1_wd_%d" % l, [DFF, D])
            if "f2" in self.phases:
                self.w[(l, "wg", 2)] = self.dram_in("f2_wg_%d" % l, [D, DFF])
                self.w[(l, "wu", 2)] = self.dram_in("f2_wu_%d" % l, [D, DFF])
                self.w[(l, "wd", 2)] = self.dram_in("f2_wd_%d" % l, [DFF, D])
            if "mx" in self.phases:
                self.w[(l, "win")] = self.dram_in("w_in_%d" % l, [D, DIN])
                self.w[(l, "wout")] = self.dram_in("w_out_%d" % l, [D, D])
                self.w[(l, "lru")] = self.dram_in("lru_w_%d" % l, [128, 2048])
        self.y_out = nc.dram_tensor("y_out", [T, D], F32, kind="ExternalOutput").ap()
        self.bias_hl = nc.dram_tensor("bias_hl", [128, 8192], BF16)
        self.ex1_in = nc.dram_tensor("ex1_in", [128, 136], F32)
        self.ex1_out = nc.dram_tensor("ex1_out", [256, 136], F32)
        self.ex2_in = [nc.dram_tensor("ex2_in%d" % i, [128, 2], F32) for i in range(2)]
        self.ex2_out = [nc.dram_tensor("ex2_out%d" % i, [256, 2], F32) for i in range(2)]
        with ExitStack() as st:
            self.P = Prog(nc, st)
            self.alloc()
            self.setup()
            if "f1" in self.phases:
                self.ffn_prefetch(self.layers[0], 1)
            self.load_x()
            if self.debug:
                self.dump("dbg_x0")
            for li_, l in enumerate(self.layers):
                if li_ > 0:
                    self.P.new_epoch()
                if "f1" in self.phases:
                    self.ffn(l, 1)
                    if self.debug:
                        self.dump("dbg_f1_%d" % l)
                if "mx" in self.phases:
                    self.mixer(l)
                    if self.debug:
                        self.dump("dbg_mx_%d" % l)
                if "f2" in self.phases:
                    self.ffn(l, 2)
                    if self.debug:
                        self.dump("dbg_f2_%d" % l)
            self.store(self.y_out, "y_out", norm=self.last)
            self.P.wait_all("sp", ["y_out"] + self.dbg)
            self.P.emit()
        return nc

    def alloc(self):
        P = self.P
        self.xT = P.sbuf("xT", [128, DC, T], F32)
        self.Rxn = P.sbuf("Rxn", [128, 16384], BF16)
        self.Rw = P.sbuf("Rw", [128, 12288], BF16)
        self.Rh = P.sbuf("Rh", [128, 10240], BF16)
        self.Rxr = P.sbuf("Rxr", [128, 16416], BF16)
        self.Rgg = P.sbuf("Rgg", [128, 8192], BF16)
        self.Rwo = P.sbuf("Rwo", [128, 4096], BF16)
        self.vecs = P.sbuf("vecs_s", [128, NV], F32)
        self.dv = P.sbuf("dv", [128, 64], F32)
        self.sel = P.sbuf("sel_s", [128, 2], F32)
        self.ident = P.sbuf("ident", [128, 128], F32)
        self.identb = P.sbuf("identb", [128, 128], BF16)
        self.onesb = P.sbuf("onesb", [128, 128], BF16)
        self.lruw = P.sbuf("lruw", [128, 16, 128], BF16)
        self.xrh = P.sbuf("xrh", [128, 8], F32)
        self.xrhalo = P.sbuf("xrhalo", [128, 4, 2], F32)
        self.hinit = P.sbuf("hinit", [128, 4], F32)
        self.ex1s = P.sbuf("ex1s", [128, 2, 136], F32)
        self.ex1t = P.sbuf("ex1t", [128, 136], F32)
        self.ex2s = P.sbuf("ex2s", [128, 2, 4], F32)
        self.ex2t = P.sbuf("ex2t", [128, 4], F32)
        self.hAend = P.sbuf("hAend", [128, 4], F32)
        self.vhalo = P.sbuf("vhalo", [128, 128], BF16)
        self.ps = [P.psum("ps%d" % i, [128, 512], F32) for i in range(8)]
        Rxn, Rw, Rh, Rxr, Rgg, Rwo = self.Rxn, self.Rw, self.Rh, self.Rxr, self.Rgg, self.Rwo
        self.xn = Rxn[:, :].rearrange("p (c t) -> p c t", c=DC)
        self.xc = Rxn[:, :].bitcast(F32).rearrange("p (c t) -> p c t", c=4)
        self.yrecb = Rxn[:, :].rearrange("p (c t) -> p c t", c=4)
        self.xf = [Rxn[:, s * 8192:(s + 1) * 8192].bitcast(F32).rearrange("p (c t) -> p c t", c=8) for s in range(2)]
        self.bF = Rxn[:, 0:8192].bitcast(F32)
        self.bH = Rxn[:, 8192:12288]
        self.bL = Rxn[:, 12288:16384]
        self.wg_s = [Rw[:, s * 6144:s * 6144 + 2048].rearrange("p (k f) -> p k f", k=8) for s in range(2)]
        self.wu_s = [Rw[:, s * 6144 + 2048:s * 6144 + 4096].rearrange("p (k f) -> p k f", k=8) for s in range(2)]
        self.wd_s = [Rw[:, s * 6144 + 4096:s * 6144 + 6144].rearrange("p (c d) -> p c d", c=2) for s in range(2)]
        self.win_xg = Rw[:, 0:8192].rearrange("p (k f) -> p k f", k=8)
        self.oN = Rw[:, 8192:12288].bitcast(F32).rearrange("p (c t) -> p c t", c=4)
        self.hT = [Rh[:, s * 4096:(s + 1) * 4096].rearrange("p (c t) -> p c t", c=2) for s in range(2)]
        self.sg = [Rh[:, 8192 + s * 1024:8192 + (s + 1) * 1024].bitcast(F32) for s in range(2)]
        self.sq8 = [Rh[:, s * 4096:(s + 1) * 4096].rearrange("p (c t) -> p c t", c=8) for s in range(2)]
        self.stage = [Rgg[:, s * 2048:(s + 1) * 2048].bitcast(F32) for s in range(2)]
        self.kT = Rh[:, 0:2176]
        self.Vd = [Rh[:, 2176:4352].rearrange("p (b f) -> p b f", b=17),
                   Rh[:, 4352:6528].rearrange("p (b f) -> p b f", b=17)]
        self.yatt = Rh[:, 6528:8576].rearrange("p (c t) -> p c t", c=4)
        self.den = [Rh[:, 8576:9600].bitcast(F32), Rxr[:, 12288:13312].bitcast(F32)]
        self.qT = Rgg[:, :].rearrange("p (b g t) -> p b g t", b=16, g=4)
        self.gg = Rgg[:, :].rearrange("p (c t) -> p c t", c=4)
        self.biasb = Rxr[:, 0:8192].rearrange("p (l t h q) -> p l t h q", l=2, t=4, h=8)
        self.win_qkv = Rxr[:, 8192:14336].rearrange("p (k f) -> p k f", k=8)
        self.pT = Rxr[:, 8192:11264].rearrange("p (b q) -> p b q", b=6)
        self.asd = Rxr[:, 11264:12288].bitcast(F32)
        self.sqa = Rxr[:, 14336:16384].rearrange("p (c t) -> p c t", c=4)
        self.xr = Rxr[:, :].bitcast(F32).rearrange("p (c t) -> p c t", c=4)
        self.wo = Rwo[:, :].rearrange("p (c d) -> p c d", c=4)

        def lt(i):
            return Rh[:, i * 1024:(i + 1) * 1024].bitcast(F32)
        self.tR = [lt(0), lt(1)]
        self.tI = [lt(2), lt(3)]
        self.tS = [lt(4), lt(5)]
        self.xcb = [Rh[:, 6144 + s * 512:6144 + (s + 1) * 512] for s in range(2)]
        self.sq4 = Rh[:, 7168:9216].rearrange("p (c t) -> p c t", c=4)
        self.rsd = Rh[:, 9216:10240].bitcast(F32)

    def nb(self):
        b = self.bank
        self.bank = (self.bank + 1) % 8
        return b

    def setup(self):
        P = self.P
        for s in ("vecs", "sel", "bias", "biashl", "stg0", "stg1", "out0", "out1", "wgu0", "wgu1", "wd0", "wd1",
                  "mixw", "mixb", "winxg", "worec", "ex1a", "ex1b", "ex1c", "ex2a0", "ex2b0", "ex2c0", "ex2a1", "ex2b1", "ex2c1", "dbgsb"):
            P.dsem(s)
        P.D("sp", "vecs", out=self.vecs[:], in_=self.vecs_in, writes=["vecs"])
        P.D("sp", "sel", out=self.sel[:], in_=self.sel_in, writes=["sel"])
        P.D("sp", "bias", out=self.bF, in_=self.bias_in, writes=["bF"])
        P.I("pool", "memset", ap=self.ident[:], constant=0.0, writes=["ident"])
        P.I("pool", "affine_select", out=self.ident[:], in_=self.ident[:], pattern=[[-1, 128]],
            compare_op=ALU.not_equal, fill=1.0, base=0, channel_multiplier=1, reads=["ident"], writes=["ident"])
        P.I("pool", "tensor_copy", out=self.identb[:], in_=self.ident[:], reads=["ident"], writes=["identb"])
        P.I("pool", "memset", ap=self.onesb[:], constant=1.0, writes=["onesb"])
        P.I("dve", "tensor_copy", out=self.bH, in_=self.bF, reads=["bF"], writes=["bH"])
        P.I("dve", "tensor_tensor", out=self.bL, in0=self.bF, in1=self.bH, op=ALU.subtract, reads=["bF", "bH"], writes=["bL"])
        P.D("sp", "biashl", out=self.bias_hl.ap()[:, 0:4096], in_=self.bH, reads=["bH"], writes=["bias_hl"])
        P.D("sp", "biashl", out=self.bias_hl.ap()[:, 4096:8192], in_=self.bL, reads=["bL"], writes=["bias_hl"])
        P.barrier()

    def load_x(self):
        P = self.P
        for tt in range(NB):
            s = tt % 2
            P.D("sp", "stg%d" % s, out=self.stage[s], in_=self.x_in[tt * 128:(tt + 1) * 128, :], writes=["stage%d" % s])
            for half in range(2):
                b = self.nb()
                for j in range(4):
                    dc = half * 4 + j
                    P.I("pe", "transpose", out=self.ps[b][:, j * 128:(j + 1) * 128],
                        in_=self.stage[s][:, dc * 128:(dc + 1) * 128], identity=self.ident[:],
                        reads=["stage%d" % s, "ident"], writes=["ps%d" % b])
                src = self.ps[b][:, :].rearrange("p (c t) -> p c t", c=4)
                dst = self.xT[:, half * 4:half * 4 + 4, tt * 128:(tt + 1) * 128]
                wr = ["xT%d_%d" % (dc, tt // 4) for dc in range(half * 4, half * 4 + 4)]
                if (tt + half) % 2 == 0:
                    P.I("act", "copy", out=dst, in_=src, reads=["ps%d" % b], writes=wr)
                else:
                    P.I("dve", "tensor_copy", out=dst, in_=src, reads=["ps%d" % b], writes=wr)

    def store(self, dst_dram, resname, norm=False):
        P = self.P
        P.barrier()
        gcol = V_FINAL
        for tg in range(NTG):
            ts = slice(tg * 512, (tg + 1) * 512)
            if norm:
                self.rstd_tg(DC, [self.xT[:, dc, ts] for dc in range(DC)], ["xT%d_%d" % (dc, tg) for dc in range(DC)],
                             1.0 / D, self.sq8[tg % 2], "sq8_%d" % (tg % 2), self.sg[tg % 2], "sg%d" % (tg % 2))
                xf = self.xf[tg % 2]
                for dc in range(DC):
                    P.I("dve", "scalar_tensor_tensor", out=xf[:, dc, :], in0=self.xT[:, dc, ts],
                        scalar=self.vecs[:, gcol + dc:gcol + dc + 1], in1=self.sg[tg % 2], op0=ALU.mult, op1=ALU.mult,
                        reads=["xT%d_%d" % (dc, tg), "sg%d" % (tg % 2), "vecs"], writes=["xf%d_%d" % (tg % 2, dc)])
            for ttl in range(4):
                tt = tg * 4 + ttl
                s = tt % 2
                for half in range(2):
                    b = self.nb()
                    for j in range(4):
                        dc = half * 4 + j
                        if norm:
                            src = self.xf[tg % 2][:, dc, ttl * 128:(ttl + 1) * 128]
                            rd = "xf%d_%d" % (tg % 2, dc)
                        else:
                            src = self.xT[:, dc, tt * 128:(tt + 1) * 128]
                            rd = "xT%d_%d" % (dc, tg)
                        P.I("pe", "transpose", out=self.ps[b][:, j * 128:(j + 1) * 128], in_=src, identity=self.ident[:],
                            reads=[rd, "ident"], writes=["ps%d" % b])
                    dst = self.stage[s][:, half * 512:(half + 1) * 512]
                    if half == 0:
                        P.I("act", "copy", out=dst, in_=self.ps[b][:, :], reads=["ps%d" % b], writes=["stage%d_%d" % (s, half)])
                    else:
                        P.I("dve", "tensor_copy", out=dst, in_=self.ps[b][:, :], reads=["ps%d" % b], writes=["stage%d_%d" % (s, half)])
                P.D("sp", "out%d" % s, out=dst_dram[tt * 128:(tt + 1) * 128, :], in_=self.stage[s],
                    reads=["stage%d_0" % s, "stage%d_1" % s], writes=[resname])
        P.barrier()

    def dbg_sb(self, name, ap, shape, dt, reads):
        d = self.nc.dram_tensor(name, list(shape), dt, kind="ExternalOutput").ap()
        self.dbg.append(name)
        self.P.D("sp", "dbgsb", out=d, in_=ap, reads=reads, writes=[name])

    def dump(self, name):
        d = self.nc.dram_tensor(name, [T, D], F32, kind="ExternalOutput").ap()
        self.dbg.append(name)
        self.store(d, name, norm=False)

    def rstd_tg(self, nch, srcs, srcnames, inv_n, sqbuf, sqname, outbuf, outname, bank=None):
        P = self.P
        for c in range(nch):
            P.I("act", "activation", out=sqbuf[:, c, :], in_=srcs[c], func=AF.Square,
                reads=[srcnames[c]], writes=["%s_%d" % (sqname, c)])
        b = self.nb() if bank is None else bank
        for c in range(nch):
            P.I("pe", "matmul", out=self.ps[b][:, :], lhsT=self.onesb[:], rhs=sqbuf[:, c, :], start=(c == 0), stop=(c == nch - 1),
                reads=["%s_%d" % (sqname, c), "onesb"], writes=["ps%d" % b])
        P.I("act", "activation", out=outbuf, in_=self.ps[b][:, :], func=AF.Ln, scale=inv_n, bias=EPS,
            reads=["ps%d" % b], writes=[outname])
        P.I("act", "activation", out=outbuf, in_=outbuf, func=AF.Exp, scale=-0.5, reads=[outname], writes=[outname])

    def rmsnorm_xn(self, gcol):
        P = self.P
        for tg in range(NTG):
            s = tg % 2
            ts = slice(tg * 512, (tg + 1) * 512)
            self.rstd_tg(DC, [self.xT[:, dc, ts] for dc in range(DC)], ["xT%d_%d" % (dc, tg) for dc in range(DC)],
                         1.0 / D, self.sq8[s], "sq8_%d" % s, self.sg[s], "sg%d" % s)
            for dc in range(DC):
                P.I("dve", "scalar_tensor_tensor", out=self.xn[:, dc, ts], in0=self.xT[:, dc, ts],
                    scalar=self.vecs[:, gcol + dc:gcol + dc + 1], in1=self.sg[s], op0=ALU.mult, op1=ALU.mult,
                    reads=["xT%d_%d" % (dc, tg), "sg%d" % s, "vecs"], writes=["xn%d_%d" % (dc, tg)])

    def ffn_load(self, l, k, gi, part):
        P = self.P
        s = gi % 2
        f0 = gi * 256
        if part == "gu":
            wg = self.w[(l, "wg", k)].rearrange("(kc p) f -> p kc f", p=128)[:, :, f0:f0 + 256]
            wu = self.w[(l, "wu", k)].rearrange("(kc p) f -> p kc f", p=128)[:, :, f0:f0 + 256]
            P.D("pool", "wgu%d" % s, out=self.wg_s[s], in_=wg, writes=["wg%d" % s])
            P.D("pool", "wgu%d" % s, out=self.wu_s[s], in_=wu, writes=["wu%d" % s])
        else:
            wd = self.w[(l, "wd", k)].rearrange("(fc p) d -> p fc d", p=128)[:, gi * 2:gi * 2 + 2, :]
            P.D("pool", "wd%d" % s, out=self.wd_s[s], in_=wd, writes=["wd%d" % s])

    def ffn_prefetch(self, l, k):
        if (l, k) in self.ffn_pref:
            return
        self.ffn_pref.add((l, k))
        for gi in (0, 1):
            self.ffn_load(l, k, gi, "gu")
            self.ffn_load(l, k, gi, "d")

    def ffn_gu(self, gi, tgs=range(NTG)):
        P = self.P
        s = gi % 2
        for fc in range(2):
            fs = slice(fc * 128, (fc + 1) * 128)
            for tg in tgs:
                ts = slice(tg * 512, (tg + 1) * 512)
                bg = self.gub
                bu = self.gub + 1
                self.gub = (self.gub + 2) % 4
                for kc in range(DC):
                    P.I("pe", "matmul", out=self.ps[bg][:, :], lhsT=self.wg_s[s][:, kc, fs], rhs=self.xn[:, kc, ts],
                        start=(kc == 0), stop=(kc == DC - 1), reads=["wg%d" % s, "xn%d_%d" % (kc, tg)], writes=["ps%d" % bg])
                for kc in range(DC):
                    P.I("pe", "matmul", out=self.ps[bu][:, :], lhsT=self.wu_s[s][:, kc, fs], rhs=self.xn[:, kc, ts],
                        start=(kc == 0), stop=(kc == DC - 1), reads=["wu%d" % s, "xn%d_%d" % (kc, tg)], writes=["ps%d" % bu])
                sgi = self.sgi
                self.sgi ^= 1
                P.I("act", "activation", out=self.sg[sgi], in_=self.ps[bg][:, :], func=AF.Silu,
                    reads=["ps%d" % bg], writes=["sg%d" % sgi])
                P.I("dve", "tensor_tensor", out=self.hT[s][:, fc, ts], in0=self.ps[bu][:, :], in1=self.sg[sgi], op=ALU.mult,
                    reads=["ps%d" % bu, "sg%d" % sgi], writes=["hT%d_%d_%d" % (s, fc, tg)])

    def ffn_d(self, gi):
        P = self.P
        s = gi % 2
        for dc in range(DC):
            ds_ = slice(dc * 128, (dc + 1) * 128)
            for tg in range(NTG):
                ts = slice(tg * 512, (tg + 1) * 512)
                b = 4 + self.db
                self.db = (self.db + 1) % 4
                for fc in range(2):
                    P.I("pe", "matmul", out=self.ps[b][:, :], lhsT=self.wd_s[s][:, fc, ds_], rhs=self.hT[s][:, fc, ts],
                        start=(fc == 0), stop=(fc == 1), reads=["wd%d" % s, "hT%d_%d_%d" % (s, fc, tg)], writes=["ps%d" % b])
                P.I("dve", "scalar_tensor_tensor", out=self.xT[:, dc, ts], in0=self.ps[b][:, :], scalar=0.5,
                    in1=self.xT[:, dc, ts], op0=ALU.mult, op1=ALU.add,
                    reads=["ps%d" % b, "xT%d_%d" % (dc, tg)], writes=["xT%d_%d" % (dc, tg)])

    def ffn(self, l, k):
        P = self.P
        self.gub = 0
        self.db = 0
        self.sgi = 0
        self.ffn_prefetch(l, k)
        if k == 1 and "mx" in self.phases:
            self.mixer_prefetch(l)
        self.rmsnorm_xn(l * VL + (V_F1G if k == 1 else V_F2G))
        for tg in range(NTG):
            self.ffn_gu(0, [tg])
            self.ffn_gu(1, [tg])
        for gi in range(FG):
            if gi + 2 < FG:
                self.ffn_load(l, k, gi + 2, "gu")
            self.ffn_d(gi)
            if gi + 2 < FG:
                self.ffn_load(l, k, gi + 2, "d")
                self.ffn_gu(gi + 2)
        P.barrier()

    def mixer_prefetch(self, l):
        P = self.P
        win = self.w[(l, "win")].rearrange("(kc p) f -> p kc f", p=128)
        P.D("sp", "mixb", out=self.Rxr[:, 0:8192], in_=self.bias_hl.ap(), reads=["bias_hl"], writes=["biasb"])
        for g in range(4):
            for kv in range(2):
                c0 = 1024 + kv * 256 + g * 64
                P.D("pool", "mixw", out=self.win_qkv[:, :, g * 128 + kv * 64:g * 128 + kv * 64 + 64], in_=win[:, :, c0:c0 + 64],
                    writes=["win_qkv"])
        P.D("pool", "mixw", out=self.win_qkv[:, :, 512:768], in_=win[:, :, 1536:1792], writes=["win_qkv"])
        P.D("pool", "mixw", out=self.wo, in_=self.w[(l, "wout")][512:1024, :].rearrange("(c p) d -> p c d", p=128), writes=["wo"])
        P.D("pool", "mixw", out=self.lruw[:, :, :], in_=self.w[(l, "lru")].rearrange("p (i m) -> p i m", i=16), writes=["lruw"])

    def evac(self, i, out, in_, reads, writes, scale=None):
        P = self.P
        if i % 2 == 0:
            if scale is None:
                P.I("act", "copy", out=out, in_=in_, reads=reads, writes=writes)
            else:
                P.I("act", "mul", out=out, in_=in_, mul=scale, reads=reads, writes=writes)
        else:
            if scale is None:
                P.I("dve", "tensor_copy", out=out, in_=in_, reads=reads, writes=writes)
            else:
                P.I("dve", "tensor_scalar", out=out, in0=in_, scalar1=scale, scalar2=None, op0=ALU.mult, reads=reads, writes=writes)

    def mixer(self, l):
        P = self.P
        o = l * VL
        dv = self.dv
        vecs = self.vecs
        if not ("f1" in self.phases):
            self.mixer_prefetch(l)
        z = dv[:, 40:48]
        p = dv[:, 48:56]
        P.I("act", "activation", out=z, in_=vecs[:, o + V_LAM:o + V_LAM + 8], func=AF.Exp, scale=-1.0, reads=["vecs"], writes=["dvz"])
        P.I("dve", "tensor_scalar", out=p, in0=z, scalar1=-1.0 / 8, scalar2=1.0 / 7, op0=ALU.mult, op1=ALU.add, reads=["dvz"], writes=["dvp"])
        for n in (6, 5, 4, 3, 2, 1):
            P.I("dve", "tensor_tensor", out=p, in0=p, in1=z, op=ALU.mult, reads=["dvp", "dvz"], writes=["dvp"])
            P.I("dve", "tensor_scalar", out=p, in0=p, scalar1=-1.0, scalar2=1.0 / n, op0=ALU.mult, op1=ALU.add, reads=["dvp"], writes=["dvp"])
        P.I("dve", "tensor_tensor", out=p, in0=p, in1=z, op=ALU.mult, reads=["dvp", "dvz"], writes=["dvp"])
        P.I("dve", "tensor_scalar", out=dv[:, 0:8], in0=p, scalar1=-8.0, scalar2=None, op0=ALU.mult, reads=["dvp"], writes=["dvkk"])
        P.I("dve", "tensor_scalar", out=dv[:, 8:16], in0=p, scalar1=-4.0, scalar2=None, op0=ALU.mult, reads=["dvp"], writes=["dvkk"])
        P.I("dve", "tensor_scalar", out=dv[:, 16:24], in0=vecs[:, o + V_BA:o + V_BA + 8], scalar1=0.5, scalar2=None, op0=ALU.mult, reads=["vecs"], writes=["dvkk"])
        P.I("dve", "tensor_scalar", out=dv[:, 24:32], in0=vecs[:, o + V_BX:o + V_BX + 8], scalar1=0.5, scalar2=None, op0=ALU.mult, reads=["vecs"], writes=["dvkk"])
        P.I("act", "activation", out=dv[:, 32:40], in_=vecs[:, o + V_SINK:o + V_SINK + 8], func=AF.Exp, reads=["vecs"], writes=["dvkk"])
        win = self.w[(l, "win")].rearrange("(kc p) f -> p kc f", p=128)
        P.D("pool", "winxg", out=self.win_xg, in_=win[:, :, 0:1024], writes=["win_xg"])
        if self.mx_stop == "pre":
            P.barrier()
            return
        self.rmsnorm_xn(o + V_MIXG)
        if self.mx_stop == "norm":
            P.barrier()
            return
        ei = 0
        for tg in range(NTG):
            ts = slice(tg * 512, (tg + 1) * 512)
            for g in range(4):
                b = self.nb()
                for kc in range(DC):
                    lh = self.win_qkv[:, kc, g * 128:(g + 1) * 128]
                    P.I("pe", "matmul", out=self.ps[b][:, :], lhsT=lh, rhs=self.xn[:, kc, ts], start=(kc == 0), stop=(kc == DC - 1),
                        reads=["win_qkv", "xn%d_%d" % (kc, tg)], writes=["ps%d" % b])
                sl = (g % 2) * 2 + g // 2
                self.evac(ei, self.qT[:, tg * 4:(tg + 1) * 4, sl, :], self.ps[b][:, :].rearrange("p (b t) -> p b t", b=4),
                          ["ps%d" % b], ["qT%d_%d" % (g, tg)], scale=0.125)
                ei += 1
            b = self.nb()
            for kc in range(DC):
                P.I("pe", "matmul", out=self.ps[b][:, :], lhsT=self.win_qkv[:, kc, 512:640], rhs=self.xn[:, kc, ts],
                    start=(kc == 0), stop=(kc == DC - 1), reads=["win_qkv", "xn%d_%d" % (kc, tg)], writes=["ps%d" % b])
            self.evac(ei, self.kT[:, ts], self.ps[b][:, :], ["ps%d" % b], ["kT%d" % tg])
            ei += 1
            b = self.nb()
            for bl in range(4):
                blk = tg * 4 + bl
                for kc in range(DC):
                    P.I("pe", "matmul", out=self.ps[b][:, bl * 128:(bl + 1) * 128], lhsT=self.xn[:, kc, blk * 128:(blk + 1) * 128],
                        rhs=self.win_qkv[:, kc, 640:768], start=(kc == 0), stop=(kc == DC - 1),
                        reads=["win_qkv", "xn%d_%d" % (kc, tg)], writes=["ps%d" % b])
            src = self.ps[b][:, :].rearrange("p (b f) -> p b f", b=4)
            bs = slice(tg * 4, tg * 4 + 4)
            self.evac(0, self.Vd[0][:, bs, 0:64], src[:, :, 0:64], ["ps%d" % b], ["Vd0a%d" % tg])
            self.evac(0, self.Vd[0][:, bs, 64:128], src[:, :, 0:64], ["ps%d" % b], ["Vd0b%d" % tg])
            self.evac(0, self.Vd[1][:, bs, 0:64], src[:, :, 64:128], ["ps%d" % b], ["Vd1a%d" % tg])
            self.evac(0, self.Vd[1][:, bs, 64:128], src[:, :, 64:128], ["ps%d" % b], ["Vd1b%d" % tg])
        if self.mx_stop in ("v", "v1"):
            P.barrier()
            return
        b = self.nb()
        for cc in range(4):
            for kc in range(DC):
                P.I("pe", "matmul", out=self.ps[b][:, cc * 2:cc * 2 + 2], lhsT=self.win_xg[:, kc, cc * 128:(cc + 1) * 128],
                    rhs=self.xn[:, kc, 2046:2048], start=(kc == 0), stop=(kc == DC - 1),
                    reads=["win_xg", "xn%d_3" % kc], writes=["ps%d" % b])
        P.I("dve", "tensor_copy", out=self.xrh[:, :], in_=self.ps[b][:, 0:8], reads=["ps%d" % b], writes=["xrh"])
        if self.mx_stop == "p1":
            P.barrier()
            return
        e1 = self.ex1_in.ap()
        P.D("sp", "ex1a", out=e1[:, 0:64], in_=self.kT[:, 1920:2048].bitcast(F32), reads=["kT3"], writes=["ex1_in"])
        P.D("sp", "ex1a", out=e1[:, 64:96], in_=self.Vd[0][:, 15, 0:64].bitcast(F32), reads=["Vd0a3"], writes=["ex1_in"])
        P.D("sp", "ex1a", out=e1[:, 96:128], in_=self.Vd[1][:, 15, 0:64].bitcast(F32), reads=["Vd1a3"], writes=["ex1_in"])
        P.D("sp", "ex1a", out=e1[:, 128:136], in_=self.xrh[:, :], reads=["xrh"], writes=["ex1_in"])
        P.dma("pool", "ex1c", lambda e: e.collective_compute(
            "AllGather", ALU.bypass, replica_groups=[[0, 1], [2, 3], [4, 5], [6, 7]],
            ins=[self.ex1_in.ap().opt()], outs=[self.ex1_out.ap().opt()]), reads=["ex1_in"], writes=["ex1_out"], inc=1)
        self.attention(l)
        P.barrier()
        if self.mx_stop == "p2":
            return
        P.I("dve", "memset", ap=self.xr[:, :, 0:2], constant=0.0, writes=["xrpad"])
        P.I("dve", "tensor_copy", out=self.xr[:, :, 2050:2052], in_=self.xrhalo[:, :, ::-1], reads=["xrhalo"], writes=["xrhal"])
        ei = 0
        for cc in range(4):
            for tg in range(NTG):
                ts = slice(tg * 512, (tg + 1) * 512)
                b = self.nb()
                for kc in range(DC):
                    P.I("pe", "matmul", out=self.ps[b][:, :], lhsT=self.win_xg[:, kc, cc * 128:(cc + 1) * 128], rhs=self.xn[:, kc, ts],
                        start=(kc == 0), stop=(kc == DC - 1), reads=["win_xg", "xn%d_%d" % (kc, tg)], writes=["ps%d" % b])
                self.evac(1, self.xr[:, cc, 2 + tg * 512:2 + (tg + 1) * 512], self.ps[b][:, :], ["ps%d" % b], ["xr%d_%d" % (cc, tg)])
                b = self.nb()
                for kc in range(DC):
                    P.I("pe", "matmul", out=self.ps[b][:, :], lhsT=self.win_xg[:, kc, 512 + cc * 128:512 + (cc + 1) * 128], rhs=self.xn[:, kc, ts],
                        start=(kc == 0), stop=(kc == DC - 1), reads=["win_xg", "xn%d_%d" % (kc, tg)], writes=["ps%d" % b])
                P.I("act", "activation", out=self.gg[:, cc, ts], in_=self.ps[b][:, :], func=AF.Gelu, reads=["ps%d" % b], writes=["gg%d_%d" % (cc, tg)])
        P.barrier()
        if self.mx_stop == "p3":
            return
        if "f2" in self.phases:
            self.ffn_prefetch(l, 2)
        P.D("pool", "worec", out=self.wo, in_=self.w[(l, "wout")][0:512, :].rearrange("(c p) d -> p c d", p=128), writes=["wo"])
        self.lru(l)
        P.barrier()

    def ex1_receive(self):
        P = self.P
        P.D("sp", "ex1b", out=self.ex1s[:, :, :], in_=self.ex1_out.ap().rearrange("(r p) w -> p r w", p=128),
            reads=["ex1_out"], writes=["ex1s"])
        s0 = self.sel[:, 0:1]
        s1 = self.sel[:, 1:2]
        e0b = self.ex1s[:, 0, 0:128].bitcast(BF16)
        e1b = self.ex1s[:, 1, 0:128].bitcast(BF16)
        etb = self.ex1t[:, 0:128].bitcast(BF16)
        P.I("dve", "tensor_scalar", out=etb, in0=e1b, scalar1=s1, scalar2=None, op0=ALU.mult,
            reads=["ex1s", "sel"], writes=["ex1t"])
        P.I("dve", "scalar_tensor_tensor", out=self.kT[:, 2048:2176], in0=e0b[:, 0:128], scalar=s0, in1=etb[:, 0:128],
            op0=ALU.mult, op1=ALU.add, reads=["ex1s", "ex1t", "sel"], writes=["kT4"])
        P.I("dve", "scalar_tensor_tensor", out=self.vhalo[:, :], in0=e0b[:, 128:256], scalar=s0, in1=etb[:, 128:256],
            op0=ALU.mult, op1=ALU.add, reads=["ex1s", "ex1t", "sel"], writes=["vhalo"])
        for kv in range(2):
            for hf, nm in ((0, "a"), (1, "b")):
                P.I("dve", "tensor_copy", out=self.Vd[kv][:, 16, hf * 64:(hf + 1) * 64], in_=self.vhalo[:, kv * 64:(kv + 1) * 64],
                    reads=["vhalo"], writes=["Vd%d%s4" % (kv, nm)])
        xh0 = self.ex1s[:, 0, 128:136]
        xh1 = self.ex1s[:, 1, 128:136]
        xht = self.ex1t[:, 128:136]
        P.I("dve", "tensor_scalar", out=xht, in0=xh1, scalar1=s1, scalar2=None, op0=ALU.mult, reads=["ex1s", "sel"], writes=["ex1tx"])
        P.I("dve", "scalar_tensor_tensor", out=self.xrhalo[:, :, :].rearrange("p c t -> p (c t)"), in0=xh0, scalar=s0, in1=xht,
            op0=ALU.mult, op1=ALU.add, reads=["ex1s", "ex1tx", "sel"], writes=["xrhalo"])

    def attention(self, l):
        P = self.P
        o = l * VL
        psL = (0, 1, 2)
        esink = self.dv[:, 32:40]
        it = 0
        for n in range(NB):
            tg = n // 4
            nq = n % 4
            if n == 8:
                self.ex1_receive()
            for kvh in range(2):
                ks = slice(kvh * 64, (kvh + 1) * 64)
                kbs = []
                if n > 0:
                    kbs.append((0, n - 1))
                kbs.append((1, n))
                kbs.append((2, n + 1) if n < NB - 1 else (3, 16))
                pset = it % 2
                bO = 3 + pset
                bS = 5
                it += 1
                for i, (tb, blk) in enumerate(kbs):
                    b = psL[i]
                    kname = "kT%d" % (blk // 4) if blk < 16 else "kT4"
                    P.I("pe", "matmul", out=self.ps[b][:, :], lhsT=self.kT[ks, blk * 128:(blk + 1) * 128],
                        rhs=self.Rgg[ks, n * 512:(n + 1) * 512], start=True, stop=False,
                        reads=[kname] + ["qT%d_%d" % (g, tg) for g in range(4)], writes=["ps%d" % b])
                    P.I("pe", "matmul", out=self.ps[b][:, :], lhsT=self.identb[:], rhs=self.Rxr[:, tb * 1024 + kvh * 512:tb * 1024 + kvh * 512 + 512],
                        start=False, stop=True, reads=["identb", "biasb"], writes=["ps%d" % b])
                    P.I("act", "activation", out=self.pT[:, pset * 3 + i, :], in_=self.ps[b][:, :], func=AF.Exp,
                        reads=["ps%d" % b], writes=["pT%d_%d" % (pset, i)])
                psO = self.ps[bO]
                nk = len(kbs)
                for i, (tb, blk) in enumerate(kbs):
                    pv = self.pT[:, pset * 3 + i, :]
                    bn = blk // 4 if blk < 16 else 4
                    rd = ["pT%d_%d" % (pset, i)]
                    P.I("pe", "matmul", out=psO[:, :], lhsT=self.Vd[kvh][:, blk, :], rhs=pv,
                        start=(i == 0), stop=(i == nk - 1), reads=rd + ["Vd%da%d" % (kvh, bn), "Vd%db%d" % (kvh, bn)], writes=["ps%d" % bO])
                for i, (tb, blk) in enumerate(kbs):
                    pv = self.pT[:, pset * 3 + i, :]
                    P.I("pe", "matmul", out=self.ps[bS][:, :], lhsT=self.onesb[:], rhs=pv,
                        start=(i == 0), stop=(i == nk - 1), reads=["pT%d_%d" % (pset, i), "onesb"], writes=["ps%d" % bS])
                dbuf = self.den[pset]
                dn = "den%d" % pset
                den = dbuf.rearrange("p (g t) -> p g t", g=4)
                P.I("dve", "tensor_tensor", out=den, in0=self.ps[bS][:, :].rearrange("p (g t) -> p g t", g=4),
                    in1=esink[:, kvh * 4:(kvh + 1) * 4].unsqueeze(2).broadcast_to([128, 4, 128]), op=ALU.add,
                    reads=["ps%d" % bS, "dvkk"], writes=[dn])
                P.I("act", "activation", out=dbuf, in_=dbuf, func=AF.Ln, reads=[dn], writes=[dn])
                P.I("act", "activation", out=dbuf, in_=dbuf, func=AF.Exp, scale=-1.0, reads=[dn], writes=[dn])
                on = self.oN[:, kvh * 2:kvh * 2 + 2, nq * 128:(nq + 1) * 128]
                P.I("dve", "tensor_tensor", out=on[0:64], in0=psO[0:64, 0:256].rearrange("p (j t) -> p j t", j=2),
                    in1=dbuf[0:64, 0:256].rearrange("p (j t) -> p j t", j=2), op=ALU.mult,
                    reads=["ps%d" % bO, dn], writes=["oNe%d_%d" % (kvh, nq)])
                P.I("dve", "tensor_tensor", out=on[64:128], in0=psO[64:128, 256:512].rearrange("p (j t) -> p j t", j=2),
                    in1=dbuf[64:128, 256:512].rearrange("p (j t) -> p j t", j=2), op=ALU.mult,
                    reads=["ps%d" % bO, dn], writes=["oNo%d_%d" % (kvh, nq)])
            if nq == 3 and tg == 0 and self.debug == 3:
                self.dbg_sb("dbg_oN", self.Rw[:, 8192:12288], [128, 4096], BF16,
                            ["oNe%d_%d" % (k_, q_) for k_ in range(2) for q_ in range(4)] + ["oNo%d_%d" % (k_, q_) for k_ in range(2) for q_ in range(4)])
                self.dbg_sb("dbg_qT", self.Rgg[:, :], [128, 8192], BF16, ["qT%d_%d" % (g_, t_) for g_ in range(4) for t_ in range(4)])
                self.dbg_sb("dbg_kv", self.Rh[:, 0:6528], [128, 6528], BF16, ["kT%d" % i for i in range(5)])
                self.dbg_sb("dbg_pT", self.Rxr[:, 8192:11264], [128, 3072], BF16, ["pT%d_%d" % (a_, b_) for a_ in range(2) for b_ in range(3)])
            if nq == 3:
                ts = slice(tg * 512, (tg + 1) * 512)
                onames = [["oNe%d_%d" % (c // 2, q) for q in range(4)] + ["oNo%d_%d" % (c // 2, q) for q in range(4)] for c in range(4)]
                P_ = self.P
                for c in range(4):
                    P_.I("act", "activation", out=self.sqa[:, c, :], in_=self.oN[:, c, :], func=AF.Square, reads=onames[c], writes=["sqa%d" % c])
                b = 6
                for c in range(4):
                    P_.I("pe", "matmul", out=self.ps[b][:, :], lhsT=self.onesb[:], rhs=self.sqa[:, c, :], start=(c == 0), stop=(c == 3),
                         reads=["sqa%d" % c, "onesb"], writes=["ps%d" % b])
                P_.I("act", "activation", out=self.asd, in_=self.ps[b][:, :], func=AF.Ln, scale=1.0 / 512, bias=EPS, reads=["ps%d" % b], writes=["asd"])
                P_.I("act", "activation", out=self.asd, in_=self.asd, func=AF.Exp, scale=-0.5, reads=["asd"], writes=["asd"])
                for c in range(4):
                    P_.I("dve", "scalar_tensor_tensor", out=self.yatt[:, c, :], in0=self.oN[:, c, :],
                         scalar=self.vecs[:, o + V_ATTG + c:o + V_ATTG + c + 1], in1=self.asd, op0=ALU.mult, op1=ALU.mult,
                         reads=onames[c] + ["asd", "vecs"], writes=["yatt%d" % c])
                for dc in range(DC):
                    b = 6 + (dc + 1) % 2
                    for c in range(4):
                        P_.I("pe", "matmul", out=self.ps[b][:, :], lhsT=self.wo[:, c, dc * 128:(dc + 1) * 128], rhs=self.yatt[:, c, :],
                             start=(c == 0), stop=(c == 3), reads=["wo", "yatt%d" % c], writes=["ps%d" % b])
                    P_.I("dve", "tensor_tensor", out=self.xT[:, dc, ts], in0=self.ps[b][:, :], in1=self.xT[:, dc, ts], op=ALU.add,
                         reads=["ps%d" % b, "xT%d_%d" % (dc, tg)], writes=["xT%d_%d" % (dc, tg)])

    def lru_T(self, l, di, cc, tt, ui):
        P = self.P
        dv = self.dv
        s = ui % 2
        ts = slice(tt * 512, (tt + 1) * 512)
        xc = self.xc[:, cc, ts]
        xcn = "xc%d_%d" % (cc, tt)
        col = di * 4 + cc
        P.I("act", "copy", out=self.xcb[s], in_=xc, reads=[xcn], writes=["xcb%d" % s])
        bR = (ui % 2) * 2
        bI = bR + 1
        P.I("pe", "matmul", out=self.ps[bR][:, :], lhsT=self.lruw[:, (di * 2 + 0) * 4 + cc, :], rhs=self.xcb[s], start=True, stop=True,
            reads=["lruw", "xcb%d" % s], writes=["ps%d" % bR])
        P.I("pe", "matmul", out=self.ps[bI][:, :], lhsT=self.lruw[:, (di * 2 + 1) * 4 + cc, :], rhs=self.xcb[s], start=True, stop=True,
            reads=["lruw", "xcb%d" % s], writes=["ps%d" % bI])
        tR, tI, tS = self.tR[s], self.tI[s], self.tS[s]
        P.I("act", "activation", out=tR, in_=self.ps[bR][:, :], func=AF.Tanh, scale=0.5, bias=dv[:, 16 + col:17 + col],
            reads=["ps%d" % bR, "dvkk"], writes=["tR%d" % s])
        P.I("act", "activation", out=tI, in_=self.ps[bI][:, :], func=AF.Tanh, scale=0.5, bias=dv[:, 24 + col:25 + col],
            reads=["ps%d" % bI, "dvkk"], writes=["tI%d" % s])
        P.I("act", "activation", out=tS, in_=tR, func=AF.Exp, scale=dv[:, col:col + 1], bias=dv[:, col:col + 1],
            reads=["tR%d" % s, "dvkk"], writes=["tS%d" % s])
        P.I("act", "activation", out=tR, in_=tR, func=AF.Exp, scale=dv[:, 8 + col:9 + col], bias=dv[:, 8 + col:9 + col],
            reads=["tR%d" % s, "dvkk"], writes=["tR%d" % s])
        return s

    def lru_S(self, s):
        self.P.I("act", "activation", out=self.tS[s], in_=self.tS[s], func=AF.Sqrt, scale=-0.25, bias=0.25,
                 reads=["tS%d" % s], writes=["tS%d" % s])

    def lru_U(self, s, cc, tt):
        P = self.P
        xc = self.xc[:, cc, tt * 512:(tt + 1) * 512]
        tI, tS = self.tI[s], self.tS[s]
        P.I("dve", "scalar_tensor_tensor", out=tI, in0=tI, scalar=1.0, in1=xc, op0=ALU.add, op1=ALU.mult,
            reads=["tI%d" % s, "xc%d_%d" % (cc, tt)], writes=["tI%d" % s])
        P.I("dve", "tensor_tensor", out=tI, in0=tI, in1=tS, op=ALU.mult, reads=["tI%d" % s, "tS%d" % s], writes=["tI%d" % s])

    def conv_tile(self, l, cc, tt):
        P = self.P
        o = l * VL
        vecs = self.vecs
        t0 = tt * 512
        out = self.xc[:, cc, t0:t0 + 512]
        rd = ["xr%d_%d" % (cc, tt), "vecs"]
        rd.append("xr%d_%d" % (cc, tt - 1) if tt > 0 else "xrpad")
        rd.append("xr%d_%d" % (cc, tt + 1) if tt < NTG - 1 else "xrhal")
        wn = "xc%d_%d" % (cc, tt)
        wc = o + V_CONV + cc * 5
        bcol = vecs[:, o + V_CONVB + cc:o + V_CONVB + cc + 1]
        if cc < 4:
            P.I("dve", "tensor_scalar", out=out, in0=self.xr[:, cc, t0:t0 + 512], scalar1=vecs[:, wc:wc + 1],
                scalar2=bcol, op0=ALU.mult, op1=ALU.add, reads=rd, writes=[wn])
            for j in range(1, 5):
                P.I("dve", "scalar_tensor_tensor", out=out, in0=self.xr[:, cc, t0 + j:t0 + j + 512], scalar=vecs[:, wc + j:wc + j + 1],
                    in1=out, op0=ALU.mult, op1=ALU.add, reads=rd + [wn], writes=[wn])
        else:
            tmp = self.rsd
            P.I("pool", "tensor_scalar", out=out, in0=self.xr[:, cc, t0:t0 + 512], scalar1=vecs[:, wc:wc + 1],
                scalar2=bcol, op0=ALU.mult, op1=ALU.add, reads=rd, writes=[wn])
            for j in range(1, 5):
                P.I("pool", "tensor_scalar", out=tmp, in0=self.xr[:, cc, t0 + j:t0 + j + 512], scalar1=vecs[:, wc + j:wc + j + 1],
                    scalar2=0.0, op0=ALU.mult, op1=ALU.add, reads=rd, writes=["rsd"])
                P.I("pool", "tensor_tensor", out=out, in0=out, in1=tmp, op=ALU.add, reads=[wn, "rsd"], writes=[wn])

    def lru(self, l):
        P = self.P
        o = l * VL
        vecs = self.vecs
        groups = [[0, 1], [2, 3], [4, 5], [6, 7]]
        for cc in (0, 1):
            self.conv_tile(l, cc, 0)
        ui = 0
        for cp in range(2):
            ccs = (2 * cp, 2 * cp + 1)
            for tt in range(NTG):
                ss = []
                for cc in ccs:
                    ss.append(self.lru_T(l, 0, cc, tt, ui))
                    ui += 1
                for s in ss:
                    self.lru_S(s)
                for s, cc in zip(ss, ccs):
                    self.lru_U(s, cc, tt)
                    if tt + 1 < NTG:
                        self.conv_tile(l, cc, tt + 1)
                    elif cp == 0:
                        self.conv_tile(l, cc + 2, 0)
                    t0 = tt * 512
                    hn = "xr%d_%d" % (cc, tt)
                    init = 0.0 if tt == 0 else self.xr[:, cc, 2 + t0 - 1:2 + t0]
                    rd = ["tR%d" % s, "tI%d" % s] + ([] if tt == 0 else ["xr%d_%d" % (cc, tt - 1)])
                    P.I("dve", "tensor_tensor_scan", out=self.xr[:, cc, 2 + t0:2 + t0 + 512], data0=self.tR[s], data1=self.tI[s],
                        initial=init, op0=ALU.mult, op1=ALU.add, reads=rd, writes=[hn])
            hae = self.hAend[:, 2 * cp:2 * cp + 2]
            P.I("dve", "tensor_copy", out=hae, in_=self.xr[:, 2 * cp:2 * cp + 2, 2049], reads=["xr%d_3" % cc for cc in ccs], writes=["hAend%d" % cp])
            P.D("sp", "ex2a%d" % cp, out=self.ex2_in[cp].ap(), in_=hae, reads=["hAend%d" % cp], writes=["ex2_in%d" % cp])
            P.dma("pool", "ex2c%d" % cp, (lambda cp_: lambda e: e.collective_compute(
                "AllGather", ALU.bypass, replica_groups=groups,
                ins=[self.ex2_in[cp_].ap().opt()], outs=[self.ex2_out[cp_].ap().opt()]))(cp),
                reads=["ex2_in%d" % cp], writes=["ex2_out%d" % cp], inc=1)
        for cp in range(2):
            ccs = (2 * cp, 2 * cp + 1)
            cs = slice(2 * cp, 2 * cp + 2)
            P.D("sp", "ex2b%d" % cp, out=self.ex2s[:, :, cs], in_=self.ex2_out[cp].ap().rearrange("(r p) w -> p r w", p=128),
                reads=["ex2_out%d" % cp], writes=["ex2s%d" % cp])
            P.I("dve", "tensor_scalar", out=self.ex2t[:, cs], in0=self.ex2s[:, 1, cs], scalar1=self.sel[:, 1:2], scalar2=None, op0=ALU.mult,
                reads=["ex2s%d" % cp, "sel"], writes=["ex2t%d" % cp])
            P.I("dve", "scalar_tensor_tensor", out=self.hinit[:, cs], in0=self.ex2s[:, 0, cs], scalar=self.sel[:, 0:1], in1=self.ex2t[:, cs],
                op0=ALU.mult, op1=ALU.add, reads=["ex2s%d" % cp, "ex2t%d" % cp, "sel"], writes=["hinit%d" % cp])
            for tt in range(NTG - 1, -1, -1):
                t0 = tt * 512
                ts = slice(t0, t0 + 512)
                ss = []
                for cc in ccs:
                    ss.append(self.lru_T(l, 1, cc, tt, ui))
                    ui += 1
                for s in ss:
                    self.lru_S(s)
                for s, cc in zip(ss, ccs):
                    self.lru_U(s, cc, tt)
                    xcn = "xc%d_%d" % (cc, tt)
                    hn = "xr%d_%d" % (cc, tt)
                    if tt == NTG - 1:
                        init = self.hinit[:, cc:cc + 1]
                        rd = ["hinit%d" % cp]
                    else:
                        init = self.xc[:, cc, t0 + 512:t0 + 513]
                        rd = ["xc%d_%d" % (cc, tt + 1)]
                    P.I("dve", "tensor_tensor_scan", out=self.xc[:, cc, ts][:, ::-1], data0=self.tR[s][:, ::-1], data1=self.tI[s][:, ::-1],
                        initial=init, op0=ALU.mult, op1=ALU.add, reads=["tR%d" % s, "tI%d" % s] + rd, writes=[xcn])
                    hA = self.xr[:, cc, 2 + t0:2 + t0 + 512]
                    P.I("dve", "tensor_tensor", out=hA, in0=hA, in1=self.xc[:, cc, ts], op=ALU.add, reads=[hn, xcn], writes=[hn])
                    P.I("dve", "tensor_tensor", out=hA, in0=hA, in1=self.gg[:, cc, ts], op=ALU.mult, reads=[hn, "gg%d_%d" % (cc, tt)], writes=[hn])
                if cp == 0:
                    continue
                ysrc = [self.xr[:, cc, 2 + t0:2 + t0 + 512] for cc in range(4)]
                ynames = ["xr%d_%d" % (cc, tt) for cc in range(4)]
                self.rstd_tg(4, ysrc, ynames, 1.0 / 512, self.sq4, "sq4", self.rsd, "rsd", bank=4)
                for cc in range(4):
                    yb = self.yrecb[:, cc, tt * 1024 + 512:tt * 1024 + 1024]
                    P.I("dve", "scalar_tensor_tensor", out=yb, in0=ysrc[cc], scalar=vecs[:, o + V_RECG + cc:o + V_RECG + cc + 1], in1=self.rsd,
                        op0=ALU.mult, op1=ALU.mult, reads=[ynames[cc], "rsd", "vecs"], writes=["yrb%d_%d" % (cc, tt)])
                for dc in range(DC):
                    b = 5 + dc % 3
                    for cc in range(4):
                        yb = self.yrecb[:, cc, tt * 1024 + 512:tt * 1024 + 1024]
                        P.I("pe", "matmul", out=self.ps[b][:, :], lhsT=self.wo[:, cc, dc * 128:(dc + 1) * 128], rhs=yb,
                            start=(cc == 0), stop=(cc == 3), reads=["wo", "yrb%d_%d" % (cc, tt)], writes=["ps%d" % b])
                    P.I("dve", "tensor_tensor", out=self.xT[:, dc, ts], in0=self.ps[b][:, :], in1=self.xT[:, dc, ts], op=ALU.add,
                        reads=["ps%d" % b, "xT%d_%d" % (dc, tt)], writes=["xT%d_%d" % (dc, tt)])


_HPERM = [0, 2, 1, 3, 4, 6, 5, 7]
_N_BUCKETS = 32
_MAX_DIST = 128


def _t5_bucket(rel):
    half = _N_BUCKETS // 2
    max_exact = half // 2
    ret = (rel > 0).astype(np.int64) * half
    n = np.abs(rel)
    n_f = np.maximum(n, 1).astype(np.float32)
    large = max_exact + (np.log(n_f / np.float32(max_exact)) / np.float32(math.log(_MAX_DIST / max_exact))
                         * np.float32(half - max_exact)).astype(np.int32)
    large = np.minimum(large, half - 1)
    return ret + np.where(n < max_exact, n, large)


def _bias_tables(rel_bias, flip):
    j = np.arange(128)[:, None]
    t = np.arange(128)[None, :]
    out = np.empty((128, 4, 8, 128), np.float32)
    rels = [(-128 + j - t), (j - t), (128 + j - t), (255 - j - t)]
    for ti, rel in enumerate(rels):
        rel_true = -rel if flip else rel
        b = _t5_bucket(rel_true)
        tab = rel_bias[b]
        tab = np.transpose(tab, (0, 2, 1))[:, _HPERM, :]
        mask = (np.abs(rel) <= 128)[:, None, :]
        out[:, ti] = np.where(mask, tab, np.float32(-1e30))
    return out.reshape(128, 4096)


def _vecs(inp, r):
    v = np.zeros((128, NV), np.float32)

    def chunks(a, n):
        return np.ascontiguousarray(a.reshape(n, 128).T)
    for l in range(DEPTH):
        o = l * VL
        v[:, o + V_F1G:o + V_F1G + 8] = chunks(inp["ffn1_norm"][l], 8)
        v[:, o + V_MIXG:o + V_MIXG + 8] = chunks(inp["mix_norm"][l], 8)
        v[:, o + V_F2G:o + V_F2G + 8] = chunks(inp["ffn2_norm"][l], 8)
        cw = inp["conv_w"][l]
        w5 = np.zeros((5, 512), np.float32)
        if r == 0:
            w5[0:4] = cw
        else:
            w5[1:5] = cw[::-1]
        for cc in range(4):
            v[:, o + V_CONV + cc * 5:o + V_CONV + cc * 5 + 5] = w5[:, cc * 128:(cc + 1) * 128].T
        v[:, o + V_CONVB:o + V_CONVB + 4] = chunks(inp["conv_b"][l], 4)
        dirs = (0, 1) if r == 0 else (1, 0)
        for di, dsrc in enumerate(dirs):
            v[:, o + V_BA + di * 4:o + V_BA + di * 4 + 4] = chunks(inp["lru_b_a"][l, dsrc], 4)
            v[:, o + V_BX + di * 4:o + V_BX + di * 4 + 4] = chunks(inp["lru_b_x"][l, dsrc], 4)
            v[:, o + V_LAM + di * 4:o + V_LAM + di * 4 + 4] = chunks(inp["lru_lambda"][l, dsrc], 4)
        v[:, o + V_RECG:o + V_RECG + 4] = chunks(inp["lru_out_norm"][l], 4)
        v[:, o + V_ATTG:o + V_ATTG + 4] = chunks(inp["attn_out_norm"][l], 4)
        v[:, o + V_SINK:o + V_SINK + 8] = inp["attn_sink"][l][_HPERM][None, :]
    v[:, V_FINAL:V_FINAL + 8] = chunks(inp["final_norm"], 8)
    return v


def _lru_w(inp, l, r):
    out = np.zeros((128, 16, 128), np.float32)
    dirs = (0, 1) if r == 0 else (1, 0)
    for di, dsrc in enumerate(dirs):
        for ki, key in enumerate(("lru_w_a", "lru_w_x")):
            wsrc = inp[key][l, dsrc]
            for cc in range(4):
                idx = (di * 2 + ki) * 4 + cc
                out[0:64, idx, 0:64] = wsrc[2 * cc]
                out[64:128, idx, 64:128] = wsrc[2 * cc + 1]
    return out.reshape(128, 2048)


_CACHE = {}


def _get_nc(layers, last, debug, phases):
    key = (tuple(layers), last, debug, tuple(phases))
    if key not in _CACHE:
        _CACHE[key] = K(layers, last, debug, phases)
        _CACHE[key].build()
    return _CACHE[key]


def _core_inputs(inp, xs, layers, names):
    f32 = lambda a: np.ascontiguousarray(a, dtype=np.float32)
    shared = {}
    for r in (0, 1):
        shared[("vecs", r)] = _vecs(inp, r)
        shared[("bias", r)] = _bias_tables(f32(inp["rel_bias"]), r == 1)
        shared[("sel", r)] = np.tile(np.array([[0.0, 1.0]] if r == 0 else [[1.0, 0.0]], np.float32), (128, 1))
        for l in layers:
            if ("lru_w_%d" % l) in names:
                shared[("lru_w_%d" % l, r)] = _lru_w(inp, l, r)
    wmap = {"f1_wg": "ffn1_w_gate", "f1_wu": "ffn1_w_up", "f1_wd": "ffn1_w_down", "f2_wg": "ffn2_w_gate",
            "f2_wu": "ffn2_w_up", "f2_wd": "ffn2_w_down", "w_in": "w_in", "w_out": "w_out"}
    maps = []
    for c in range(NCORES):
        r = c % 2
        m = {}
        for n in names:
            if n == "x_in":
                m[n] = xs[c]
            elif (n, r) in shared:
                m[n] = shared[(n, r)]
            else:
                base, l = n.rsplit("_", 1)
                m[n] = f32(inp[wmap[base]][int(l)])
        maps.append(m)
    return maps


def _shard_x(x):
    xs = []
    for c in range(NCORES):
        b, r = c // 2, c % 2
        if r == 0:
            xs.append(np.ascontiguousarray(x[b, 0:T]))
        else:
            xs.append(np.ascontiguousarray(x[b, T:2 * T][::-1]))
    return xs


def _unshard(ys):
    out = np.empty((4, 2 * T, D), np.float32)
    for c in range(NCORES):
        b, r = c // 2, c % 2
        if r == 0:
            out[b, 0:T] = ys[c]
        else:
            out[b, T:2 * T] = ys[c][::-1]
    return out


def run_layers(inp, xs, layers, last, debug=False, phases=("f1", "mx", "f2")):
    k = _get_nc(layers, last, debug, phases)
    maps = _core_inputs(inp, xs, layers, k.in_names)
    res = run_bass_kernel_spmd(k.nc, maps, core_ids=list(range(NCORES)))
    return res.results, k


LAUNCH_GROUPS = [[0, 1, 2, 3]]


def kernel(**inputs):
    inp = {k: np.asarray(v) for k, v in inputs.items()}
    xs = _shard_x(np.ascontiguousarray(inp["x"], dtype=np.float32))
    for gi, layers in enumerate(LAUNCH_GROUPS):
        last = gi == len(LAUNCH_GROUPS) - 1
        results, _ = run_layers(inp, xs, layers, last)
        xs = [results[c]["y_out"] for c in range(NCORES)]
    return _unshard(xs)
```

```python
import math
from contextlib import ExitStack
import numpy as np
import concourse.bass as bass
import concourse.mybir as mybir
from concourse.bass_utils import run_bass_kernel_spmd

F32 = mybir.dt.float32
BF16 = mybir.dt.bfloat16
AF = mybir.ActivationFunctionType
ALU = mybir.AluOpType
AX = mybir.AxisListType

NCORES = 8
DEPTH = 4
T = 2048
D = 1024
DC = 8
DFF = 2816
FG = 11
DIN = 1792
NTG = 4
NB = 16
EPS = 1e-6
ENGS = ("pe", "act", "dve", "pool", "sp")


class _Ins:
    __slots__ = ("eng", "fn", "waits", "signal", "dma_sem", "ctr", "is_dma", "inc", "idx", "epoch")
    _epoch = 0
    _n = 0

    def __init__(self, eng, fn, is_dma=False, dma_sem=None, inc=16):
        self.eng = eng
        self.fn = fn
        self.waits = []
        self.signal = False
        self.dma_sem = dma_sem
        self.ctr = None
        self.is_dma = is_dma
        self.inc = inc
        _Ins._n += 1
        self.idx = _Ins._n
        self.epoch = _Ins._epoch


class Prog:
    def __init__(self, nc, stack):
        self.nc = nc
        self.stack = stack
        self.ins = []
        self.res = {}
        self.dma_tot = {}
        self.dma_sems = {}
        _Ins._epoch = 0
        self.eng_sems = {(e, 0): stack.enter_context(nc.semaphore("ctr_%s_0" % e)) for e in ENGS}
        self.last = {e: None for e in ENGS}

    def new_epoch(self):
        _Ins._epoch += 1
        for e in ENGS:
            self.eng_sems[(e, _Ins._epoch)] = self.stack.enter_context(self.nc.semaphore("ctr_%s_%d" % (e, _Ins._epoch)))

    def sbuf(self, name, shape, dt):
        return self.stack.enter_context(self.nc.sbuf_tensor(name, list(shape), dt))

    def psum(self, name, shape, dt=F32):
        return self.stack.enter_context(self.nc.psum_tensor(name, list(shape), dt))

    def dsem(self, name):
        if name not in self.dma_sems:
            self.dma_sems[name] = self.stack.enter_context(self.nc.semaphore("d_" + name))
            self.dma_tot[name] = 0
        return name

    def _deps(self, ins, reads, writes):
        evs = []
        for r in reads:
            st = self.res.setdefault(r, [[], []])
            evs.extend(st[0])
        for w in writes:
            st = self.res.setdefault(w, [[], []])
            for ev in st[0] + st[1]:
                if ev[0] == "dma" or ins.is_dma or ev[1].eng != ins.eng:
                    evs.append(ev)
        best = {}
        for ev in evs:
            if ev[0] == "dma":
                best[("d", ev[1])] = ("dma", ev[1], self.dma_tot[ev[1]])
            else:
                key = ("e", ev[1].eng, ev[1].epoch)
                if key not in best or ev[1].idx > best[key][1].idx:
                    best[key] = ev
        ins.waits = list(best.values())

    def _commit(self, ev, reads, writes):
        for r in reads:
            self.res[r][1].append(ev)
        for w in writes:
            self.res[w] = [[ev], []]

    def op(self, eng, fn, reads=(), writes=()):
        ins = _Ins(eng, fn)
        self._deps(ins, reads, writes)
        self.ins.append(ins)
        self._commit(("eng", ins), reads, writes)
        self.last[eng] = ins
        return ins

    def dma(self, q, sem, fn, reads=(), writes=(), inc=16):
        ins = _Ins(q, fn, is_dma=True, dma_sem=sem, inc=inc)
        self._deps(ins, reads, writes)
        self.dma_tot[sem] += inc
        ev = ("dma", sem, self.dma_tot[sem])
        self.ins.append(ins)
        self._commit(ev, reads, writes)
        return ev

    def I(self, eng, name, reads=(), writes=(), **kw):
        return self.op(eng, lambda e: getattr(e, name)(**kw), reads, writes)

    def D(self, q, sem, reads=(), writes=(), **kw):
        return self.dma(q, sem, lambda e: e.dma_start(**kw), reads, writes)

    def wait_all(self, eng, resources):
        ins = _Ins(eng, None)
        self._deps(ins, list(resources), [])
        self.ins.append(ins)

    def barrier(self, skip_sems=()):
        lasts = dict(self.last)
        dmas = [("dma", s, v) for s, v in self.dma_tot.items() if v > 0 and s not in skip_sems]
        for e in ENGS:
            ins = _Ins(e, None)
            ins.waits = [("eng", l) for e2, l in lasts.items() if l is not None and e2 != e] + dmas
            self.ins.append(ins)

    def emit(self):
        nc = self.nc
        for ins in self.ins:
            for ev in ins.waits:
                if ev[0] == "eng":
                    ev[1].signal = True
        cnt = {}
        for ins in self.ins:
            if ins.signal and not ins.is_dma:
                k_ = (ins.eng, ins.epoch)
                cnt[k_] = cnt.get(k_, 0) + 1
                ins.ctr = cnt[k_]
        self.counts = cnt
        per = {e: [i for i in self.ins if i.eng == e] for e in ENGS}

        def run(eng_name, eh):
            known = {}
            for ins in per[eng_name]:
                need = {}
                for ev in ins.waits:
                    if ev[0] == "eng":
                        key = ("e", ev[1].eng, ev[1].epoch)
                        val = ev[1].ctr
                    else:
                        key = ("d", ev[1])
                        val = ev[2]
                    if val > need.get(key, 0):
                        need[key] = val
                for key, val in need.items():
                    if known.get(key, 0) >= val:
                        continue
                    known[key] = val
                    sem = self.eng_sems[(key[1], key[2])] if key[0] == "e" else self.dma_sems[key[1]]
                    eh.wait_ge(sem, val)
                if ins.fn is None:
                    continue
                r = ins.fn(eh)
                if ins.is_dma:
                    if ins.inc == 16:
                        r.then_inc(self.dma_sems[ins.dma_sem], 16)
                    else:
                        r.then_inc(self.dma_sems[ins.dma_sem])
                elif ins.signal:
                    r.then_inc(self.eng_sems[(eng_name, ins.epoch)], 1)

        with nc.Block() as block:
            @block.tensor
            def _(e):
                run("pe", e)

            @block.scalar
            def _(e):
                run("act", e)

            @block.vector
            def _(e):
                run("dve", e)

            @block.gpsimd
            def _(e):
                run("pool", e)

            @block.sync
            def _(e):
                run("sp", e)


V_F1G, V_MIXG, V_F2G = 0, 8, 16
V_CONV = 24
V_CONVB = 44
V_BA = 48
V_BX = 56
V_LAM = 64
V_RECG = 72
V_ATTG = 76
V_SINK = 84
VL = 92
V_FINAL = DEPTH * VL
NV = V_FINAL + 8


V_ATTG4 = V_ATTG


class K:
    def __init__(self, layers, last, debug=False, phases=("f1", "mx", "f2")):
        self.layers = list(layers)
        self.last = last
        self.debug = debug
        self.phases = phases
        self.nc = bass.Bass("TRN2", target_bir_lowering=False)
        self.dbg = []
        self.bank = 0
        self.ffn_pref = set()
        self.in_names = []
        import os
        self.mx_stop = os.environ.get("MXSTOP", "")

    def dram_in(self, name, shape, dt=F32):
        self.in_names.append(name)
        return self.nc.dram_tensor(name, list(shape), dt, kind="ExternalInput").ap()

    def build(self):
        nc = self.nc
        self.x_in = self.dram_in("x_in", [T, D])
        self.vecs_in = self.dram_in("vecs", [128, NV])
        self.sel_in = self.dram_in("sel", [128, 2])
        self.bias_in = self.dram_in("bias", [128, 4096])
        self.w = {}
        for l in self.layers:
            if "f1" in self.phases:
                self.w[(l, "wg", 1)] = self.dram_in("f1_wg_%d" % l, [D, DFF])
                self.w[(l, "wu", 1)] = self.dram_in("f1_wu_%d" % l, [D, DFF])
                self.w[(l, "wd", 1)] = self.dram_in("f1_wd_%d" % l, [DFF, D])
            if "f2" in self.phases:
                self.w[(l, "wg", 2)] = self.dram_in("f2_wg_%d" % l, [D, DFF])
                self.w[(l, "wu", 2)] = self.dram_in("f2_wu_%d" % l, [D, DFF])
                self.w[(l, "wd", 2)] = self.dram_in("f2_wd_%d" % l, [DFF, D])
            if "mx" in self.phases:
                self.w[(l, "win")] = self.dram_in("w_in_%d" % l, [D, DIN])
                self.w[(l, "wout")] = self.dram_in("w_out_%d" % l, [D, D])
                self.w[(l, "lru")] = self.dram_in("lru_w_%d" % l, [128, 2048])
        self.y_out = nc.dram_tensor("y_out", [T, D], F32, kind="ExternalOutput").ap()
        self.bias_hl = nc.dram_tensor("bias_hl", [128, 8192], BF16)
        self.ex1_in = nc.dram_tensor("ex1_in", [128, 136], F32)
        self.ex1_out = nc.dram_tensor("ex1_out", [256, 136], F32)
        self.ex2_in = [nc.dram_tensor("ex2_in%d" % i, [128, 2], F32) for i in range(2)]
        self.ex2_out = [nc.dram_tensor("ex2_out%d" % i, [256, 2], F32) for i in range(2)]
        with ExitStack() as st:
            self.P = Prog(nc, st)
            self.alloc()
            self.setup()
            if "f1" in self.phases:
                self.ffn_prefetch(self.layers[0], 1)
                if "mx" in self.phases:
                    self.mixer_prefetch(self.layers[0])
                    self.mx_pref_done = self.layers[0]
            self.load_x()
            if self.debug:
                self.dump("dbg_x0")
            for li_, l in enumerate(self.layers):
                if li_ > 0:
                    self.P.new_epoch()
                if "f1" in self.phases:
                    self.ffn(l, 1)
                    if self.debug:
                        self.dump("dbg_f1_%d" % l)
                if "mx" in self.phases:
                    self.mixer(l)
                    if self.debug:
                        self.dump("dbg_mx_%d" % l)
                if "f2" in self.phases:
                    self.ffn(l, 2)
                    if self.debug:
                        self.dump("dbg_f2_%d" % l)
            self.store(self.y_out, "y_out", norm=self.last)
            self.P.wait_all("sp", ["y_out"] + self.dbg)
            self.P.emit()
        return nc

    def alloc(self):
        P = self.P
        self.xT = P.sbuf("xT", [128, DC, T], F32)
        self.Rxn = P.sbuf("Rxn", [128, 16384], BF16)
        self.Rw = P.sbuf("Rw", [128, 12288], BF16)
        self.Rh = P.sbuf("Rh", [128, 10240], BF16)
        self.Rxr = P.sbuf("Rxr", [128, 16416], BF16)
        self.Rgg = P.sbuf("Rgg", [128, 8192], BF16)
        self.Rwo = P.sbuf("Rwo", [128, 4096], BF16)
        self.vecs = P.sbuf("vecs_s", [128, NV], F32)
        self.dv = P.sbuf("dv", [128, 64], F32)
        self.sel = P.sbuf("sel_s", [128, 2], F32)
        self.ident = P.sbuf("ident", [128, 128], F32)
        self.identb = P.sbuf("identb", [128, 128], BF16)
        self.onesb = P.sbuf("onesb", [128, 128], BF16)
        self.lruw = P.sbuf("lruw", [128, 16, 128], BF16)
        self.xrh = P.sbuf("xrh", [128, 8], F32)
        self.xrhalo = P.sbuf("xrhalo", [128, 4, 2], F32)
        self.hinit = P.sbuf("hinit", [128, 4], F32)
        self.ex1s = P.sbuf("ex1s", [128, 2, 136], F32)
        self.ex1t = P.sbuf("ex1t", [128, 136], F32)
        self.ex2s = P.sbuf("ex2s", [128, 2, 4], F32)
        self.ex2t = P.sbuf("ex2t", [128, 4], F32)
        self.hAend = P.sbuf("hAend", [128, 4], F32)
        self.vhalo = P.sbuf("vhalo", [128, 128], BF16)
        self.ps = [P.psum("ps%d" % i, [128, 512], F32) for i in range(8)]
        Rxn, Rw, Rh, Rxr, Rgg, Rwo = self.Rxn, self.Rw, self.Rh, self.Rxr, self.Rgg, self.Rwo
        self.xn = Rxn[:, :].rearrange("p (c t) -> p c t", c=DC)
        self.xc = Rxn[:, :].bitcast(F32).rearrange("p (c t) -> p c t", c=4)
        self.yrecb = Rxn[:, :].rearrange("p (c t) -> p c t", c=4)
        self.xf = [Rxn[:, s * 8192:(s + 1) * 8192].bitcast(F32).rearrange("p (c t) -> p c t", c=8) for s in range(2)]
        self.bF = Rxn[:, 0:8192].bitcast(F32)
        self.bH = Rxn[:, 8192:12288]
        self.bL = Rxn[:, 12288:16384]
        self.wg_s = [Rw[:, s * 6144:s * 6144 + 2048].rearrange("p (k f) -> p k f", k=8) for s in range(2)]
        self.wu_s = [Rw[:, s * 6144 + 2048:s * 6144 + 4096].rearrange("p (k f) -> p k f", k=8) for s in range(2)]
        self.wd_s = [Rw[:, s * 6144 + 4096:s * 6144 + 6144].rearrange("p (c d) -> p c d", c=2) for s in range(2)]
        self.win_xg = Rw[:, 0:8192].rearrange("p (k f) -> p k f", k=8)
        self.oN = Rw[:, 8192:12288].bitcast(F32).rearrange("p (c t) -> p c t", c=4)
        self.hT = [Rh[:, s * 4096:(s + 1) * 4096].rearrange("p (c t) -> p c t", c=2) for s in range(2)]
        self.sg = [Rh[:, 8192 + s * 1024:8192 + (s + 1) * 1024].bitcast(F32) for s in range(2)]
        self.sq8 = [Rh[:, s * 4096:(s + 1) * 4096].rearrange("p (c t) -> p c t", c=8) for s in range(2)]
        self.stage = [Rgg[:, s * 2048:(s + 1) * 2048].bitcast(F32) for s in range(2)]
        self.kT = Rh[:, 0:2176]
        self.Vd = [Rh[:, 2176:4352].rearrange("p (b f) -> p b f", b=17),
                   Rh[:, 4352:6528].rearrange("p (b f) -> p b f", b=17)]
        self.yatt = Rh[:, 6528:8576].rearrange("p (c t) -> p c t", c=4)
        self.den = [Rh[:, 8576:9600].bitcast(F32), Rxr[:, 12288:13312].bitcast(F32)]
        self.qT = Rgg[:, :].rearrange("p (b g t) -> p b g t", b=16, g=4)
        self.gg = Rgg[:, :].rearrange("p (c t) -> p c t", c=4)
        self.biasb = Rxr[:, 0:8192].rearrange("p (l t h q) -> p l t h q", l=2, t=4, h=8)
        self.win_qkv = Rxr[:, 8192:14336].rearrange("p (k f) -> p k f", k=8)
        self.pT = Rxr[:, 8192:11264].rearrange("p (b q) -> p b q", b=6)
        self.asd = Rxr[:, 11264:12288].bitcast(F32)
        self.sqa = Rxr[:, 14336:16384].rearrange("p (c t) -> p c t", c=4)
        self.xr = Rxr[:, :].bitcast(F32).rearrange("p (c t) -> p c t", c=4)
        self.wo = Rwo[:, :].rearrange("p (c d) -> p c d", c=4)

        def lt(i):
            return Rh[:, i * 1024:(i + 1) * 1024].bitcast(F32)
        self.tR = [lt(0), lt(1)]
        self.tI = [lt(2), lt(3)]
        self.tS = [lt(4), lt(5)]
        self.xcb = [Rh[:, 6144 + s * 512:6144 + (s + 1) * 512] for s in range(2)]
        self.sq4 = Rh[:, 7168:9216].rearrange("p (c t) -> p c t", c=4)
        self.rsd = Rh[:, 9216:10240].bitcast(F32)

    def nb(self):
        b = self.bank
        self.bank = (self.bank + 1) % 8
        return b

    def setup(self):
        P = self.P
        for s in ("vecs", "sel", "bias", "biashl", "stg0", "stg1", "out0", "out1", "wgu0", "wgu1", "wd0", "wd1",
                  "mixw", "mixb", "winxg", "worec", "ex1a", "ex1b", "ex1c", "ex2a0", "ex2b0", "ex2c0", "ex2a1", "ex2b1", "ex2c1", "dbgsb"):
            P.dsem(s)
        P.D("sp", "vecs", out=self.vecs[:], in_=self.vecs_in, writes=["vecs"])
        P.D("sp", "sel", out=self.sel[:], in_=self.sel_in, writes=["sel"])
        P.D("sp", "bias", out=self.bF, in_=self.bias_in, writes=["bF"])
        P.I("pool", "memset", ap=self.ident[:], constant=0.0, writes=["ident"])
        P.I("pool", "affine_select", out=self.ident[:], in_=self.ident[:], pattern=[[-1, 128]],
            compare_op=ALU.not_equal, fill=1.0, base=0, channel_multiplier=1, reads=["ident"], writes=["ident"])
        P.I("pool", "tensor_copy", out=self.identb[:], in_=self.ident[:], reads=["ident"], writes=["identb"])
        P.I("pool", "memset", ap=self.onesb[:], constant=1.0, writes=["onesb"])
        P.I("dve", "tensor_copy", out=self.bH, in_=self.bF, reads=["bF"], writes=["bH"])
        P.I("dve", "tensor_tensor", out=self.bL, in0=self.bF, in1=self.bH, op=ALU.subtract, reads=["bF", "bH"], writes=["bL"])
        P.D("sp", "biashl", out=self.bias_hl.ap()[:, 0:4096], in_=self.bH, reads=["bH"], writes=["bias_hl"])
        P.D("sp", "biashl", out=self.bias_hl.ap()[:, 4096:8192], in_=self.bL, reads=["bL"], writes=["bias_hl"])

    def load_x(self):
        P = self.P
        for tt in range(NB):
            s = tt % 2
            P.D("sp", "stg%d" % s, out=self.stage[s], in_=self.x_in[tt * 128:(tt + 1) * 128, :], writes=["stage%d" % s])
            for half in range(2):
                b = self.nb()
                for j in range(4):
                    dc = half * 4 + j
                    P.I("pe", "transpose", out=self.ps[b][:, j * 128:(j + 1) * 128],
                        in_=self.stage[s][:, dc * 128:(dc + 1) * 128], identity=self.ident[:],
                        reads=["stage%d" % s, "ident"], writes=["ps%d" % b])
                src = self.ps[b][:, :].rearrange("p (c t) -> p c t", c=4)
                dst = self.xT[:, half * 4:half * 4 + 4, tt * 128:(tt + 1) * 128]
                wr = ["xT%d_%d" % (dc, tt // 4) for dc in range(half * 4, half * 4 + 4)]
                if (tt + half) % 2 == 0:
                    P.I("act", "copy", out=dst, in_=src, reads=["ps%d" % b], writes=wr)
                else:
                    P.I("dve", "tensor_copy", out=dst, in_=src, reads=["ps%d" % b], writes=wr)

    def store(self, dst_dram, resname, norm=False):
        P = self.P
        P.barrier()
        gcol = V_FINAL
        for tg in range(NTG):
            ts = slice(tg * 512, (tg + 1) * 512)
            if norm:
                self.rstd_tg(DC, [self.xT[:, dc, ts] for dc in range(DC)], ["xT%d_%d" % (dc, tg) for dc in range(DC)],
                             1.0 / D, self.sq8[tg % 2], "sq8_%d" % (tg % 2), self.sg[tg % 2], "sg%d" % (tg % 2))
                xf = self.xf[tg % 2]
                for dc in range(DC):
                    P.I("dve", "scalar_tensor_tensor", out=xf[:, dc, :], in0=self.xT[:, dc, ts],
                        scalar=self.vecs[:, gcol + dc:gcol + dc + 1], in1=self.sg[tg % 2], op0=ALU.mult, op1=ALU.mult,
                        reads=["xT%d_%d" % (dc, tg), "sg%d" % (tg % 2), "vecs"], writes=["xf%d_%d" % (tg % 2, dc)])
            for ttl in range(4):
                tt = tg * 4 + ttl
                s = tt % 2
                for half in range(2):
                    b = self.nb()
                    for j in range(4):
                        dc = half * 4 + j
                        if norm:
                            src = self.xf[tg % 2][:, dc, ttl * 128:(ttl + 1) * 128]
                            rd = "xf%d_%d" % (tg % 2, dc)
                        else:
                            src = self.xT[:, dc, tt * 128:(tt + 1) * 128]
                            rd = "xT%d_%d" % (dc, tg)
                        P.I("pe", "transpose", out=self.ps[b][:, j * 128:(j + 1) * 128], in_=src, identity=self.ident[:],
                            reads=[rd, "ident"], writes=["ps%d" % b])
                    dst = self.stage[s][:, half * 512:(half + 1) * 512]
                    if half == 0:
                        P.I("act", "copy", out=dst, in_=self.ps[b][:, :], reads=["ps%d" % b], writes=["stage%d_%d" % (s, half)])
                    else:
                        P.I("dve", "tensor_copy", out=dst, in_=self.ps[b][:, :], reads=["ps%d" % b], writes=["stage%d_%d" % (s, half)])
                P.D("sp", "out%d" % s, out=dst_dram[tt * 128:(tt + 1) * 128, :], in_=self.stage[s],
                    reads=["stage%d_0" % s, "stage%d_1" % s], writes=[resname])
        P.barrier()

    def dbg_sb(self, name, ap, shape, dt, reads):
        d = self.nc.dram_tensor(name, list(shape), dt, kind="ExternalOutput").ap()
        self.dbg.append(name)
        self.P.D("sp", "dbgsb", out=d, in_=ap, reads=reads, writes=[name])

    def dump(self, name):
        d = self.nc.dram_tensor(name, [T, D], F32, kind="ExternalOutput").ap()
        self.dbg.append(name)
        self.store(d, name, norm=False)

    def rstd_tg(self, nch, srcs, srcnames, inv_n, sqbuf, sqname, outbuf, outname, bank=None):
        P = self.P
        for c in range(nch):
            P.I("act", "activation", out=sqbuf[:, c, :], in_=srcs[c], func=AF.Square,
                reads=[srcnames[c]], writes=["%s_%d" % (sqname, c)])
        b = self.nb() if bank is None else bank
        for c in range(nch):
            P.I("pe", "matmul", out=self.ps[b][:, :], lhsT=self.onesb[:], rhs=sqbuf[:, c, :], start=(c == 0), stop=(c == nch - 1),
                reads=["%s_%d" % (sqname, c), "onesb"], writes=["ps%d" % b])
        P.I("act", "activation", out=outbuf, in_=self.ps[b][:, :], func=AF.Ln, scale=inv_n, bias=EPS,
            reads=["ps%d" % b], writes=[outname])
        P.I("act", "activation", out=outbuf, in_=outbuf, func=AF.Exp, scale=-0.5, reads=[outname], writes=[outname])

    def rmsnorm_xn(self, gcol):
        P = self.P
        for tg in range(NTG):
            s = tg % 2
            ts = slice(tg * 512, (tg + 1) * 512)
            self.rstd_tg(DC, [self.xT[:, dc, ts] for dc in range(DC)], ["xT%d_%d" % (dc, tg) for dc in range(DC)],
                         1.0 / D, self.sq8[s], "sq8_%d" % s, self.sg[s], "sg%d" % s)
            for dc in range(DC):
                P.I("dve", "scalar_tensor_tensor", out=self.xn[:, dc, ts], in0=self.xT[:, dc, ts],
                    scalar=self.vecs[:, gcol + dc:gcol + dc + 1], in1=self.sg[s], op0=ALU.mult, op1=ALU.mult,
                    reads=["xT%d_%d" % (dc, tg), "sg%d" % s, "vecs"], writes=["xn%d_%d" % (dc, tg)])

    def ffn_load(self, l, k, gi, part):
        P = self.P
        s = gi % 2
        f0 = gi * 256
        if part == "gu":
            wg = self.w[(l, "wg", k)].rearrange("(kc p) f -> p kc f", p=128)[:, :, f0:f0 + 256]
            wu = self.w[(l, "wu", k)].rearrange("(kc p) f -> p kc f", p=128)[:, :, f0:f0 + 256]
            P.D("pool", "wgu%d" % s, out=self.wg_s[s], in_=wg, writes=["wg%d" % s])
            P.D("pool", "wgu%d" % s, out=self.wu_s[s], in_=wu, writes=["wu%d" % s])
        else:
            wd = self.w[(l, "wd", k)].rearrange("(fc p) d -> p fc d", p=128)[:, gi * 2:gi * 2 + 2, :]
            P.D("pool", "wd%d" % s, out=self.wd_s[s], in_=wd, writes=["wd%d" % s])

    def ffn_prefetch(self, l, k):
        if (l, k) in self.ffn_pref:
            return
        self.ffn_pref.add((l, k))
        for gi in (0, 1):
            self.ffn_load(l, k, gi, "gu")
            self.ffn_load(l, k, gi, "d")

    def ffn_gu(self, gi, tgs=range(NTG)):
        P = self.P
        s = gi % 2
        for fc in range(2):
            fs = slice(fc * 128, (fc + 1) * 128)
            for tg in tgs:
                ts = slice(tg * 512, (tg + 1) * 512)
                bg = self.gub
                bu = self.gub + 1
                self.gub = (self.gub + 2) % 4
                for kc in range(DC):
                    P.I("pe", "matmul", out=self.ps[bg][:, :], lhsT=self.wg_s[s][:, kc, fs], rhs=self.xn[:, kc, ts],
                        start=(kc == 0), stop=(kc == DC - 1), reads=["wg%d" % s, "xn%d_%d" % (kc, tg)], writes=["ps%d" % bg])
                for kc in range(DC):
                    P.I("pe", "matmul", out=self.ps[bu][:, :], lhsT=self.wu_s[s][:, kc, fs], rhs=self.xn[:, kc, ts],
                        start=(kc == 0), stop=(kc == DC - 1), reads=["wu%d" % s, "xn%d_%d" % (kc, tg)], writes=["ps%d" % bu])
                sgi = self.sgi
                self.sgi ^= 1
                P.I("act", "activation", out=self.sg[sgi], in_=self.ps[bg][:, :], func=AF.Silu,
                    reads=["ps%d" % bg], writes=["sg%d" % sgi])
                P.I("dve", "tensor_tensor", out=self.hT[s][:, fc, ts], in0=self.ps[bu][:, :], in1=self.sg[sgi], op=ALU.mult,
                    reads=["ps%d" % bu, "sg%d" % sgi], writes=["hT%d_%d_%d" % (s, fc, tg)])

    def ffn_d(self, gi):
        P = self.P
        s = gi % 2
        for dc in range(DC):
            ds_ = slice(dc * 128, (dc + 1) * 128)
            for tg in range(NTG):
                ts = slice(tg * 512, (tg + 1) * 512)
                b = 4 + self.db
                self.db = (self.db + 1) % 4
                for fc in range(2):
                    P.I("pe", "matmul", out=self.ps[b][:, :], lhsT=self.wd_s[s][:, fc, ds_], rhs=self.hT[s][:, fc, ts],
                        start=(fc == 0), stop=(fc == 1), reads=["wd%d" % s, "hT%d_%d_%d" % (s, fc, tg)], writes=["ps%d" % b])
                P.I("dve", "scalar_tensor_tensor", out=self.xT[:, dc, ts], in0=self.ps[b][:, :], scalar=0.5,
                    in1=self.xT[:, dc, ts], op0=ALU.mult, op1=ALU.add,
                    reads=["ps%d" % b, "xT%d_%d" % (dc, tg)], writes=["xT%d_%d" % (dc, tg)])

    def ffn(self, l, k):
        P = self.P
        self.gub = 0
        self.db = 0
        self.sgi = 0
        self.ffn_prefetch(l, k)
        if k == 1 and "mx" in self.phases and getattr(self, "mx_pref_done", None) != l:
            self.mixer_prefetch(l)
        self.rmsnorm_xn(l * VL + (V_F1G if k == 1 else V_F2G))
        for tg in range(NTG):
            self.ffn_gu(0, [tg])
            self.ffn_gu(1, [tg])
        for gi in range(FG):
            if gi + 2 < FG:
                self.ffn_load(l, k, gi + 2, "gu")
            self.ffn_d(gi)
            if gi + 2 < FG:
                self.ffn_load(l, k, gi + 2, "d")
                self.ffn_gu(gi + 2)
        P.barrier()

    def mixer_prefetch(self, l):
        P = self.P
        win = self.w[(l, "win")].rearrange("(kc p) f -> p kc f", p=128)
        P.D("sp", "mixb", out=self.Rxr[:, 0:8192], in_=self.bias_hl.ap(), reads=["bias_hl"], writes=["biasb"])
        for g in range(4):
            for kv in range(2):
                c0 = 1024 + kv * 256 + g * 64
                P.D("pool", "mixw", out=self.win_qkv[:, :, g * 128 + kv * 64:g * 128 + kv * 64 + 64], in_=win[:, :, c0:c0 + 64],
                    writes=["win_qkv"])
        P.D("pool", "mixw", out=self.win_qkv[:, :, 512:768], in_=win[:, :, 1536:1792], writes=["win_qkv"])
        P.D("pool", "mixw", out=self.wo, in_=self.w[(l, "wout")][512:1024, :].rearrange("(c p) d -> p c d", p=128), writes=["wo"])
        P.D("pool", "mixw", out=self.lruw[:, :, :], in_=self.w[(l, "lru")].rearrange("p (i m) -> p i m", i=16), writes=["lruw"])

    def evac(self, i, out, in_, reads, writes, scale=None):
        P = self.P
        if i % 2 == 0:
            if scale is None:
                P.I("act", "copy", out=out, in_=in_, reads=reads, writes=writes)
            else:
                P.I("act", "mul", out=out, in_=in_, mul=scale, reads=reads, writes=writes)
        else:
            if scale is None:
                P.I("dve", "tensor_copy", out=out, in_=in_, reads=reads, writes=writes)
            else:
                P.I("dve", "tensor_scalar", out=out, in0=in_, scalar1=scale, scalar2=None, op0=ALU.mult, reads=reads, writes=writes)

    def mixer(self, l):
        P = self.P
        o = l * VL
        dv = self.dv
        vecs = self.vecs
        if not ("f1" in self.phases):
            self.mixer_prefetch(l)
        z = dv[:, 40:48]
        p = dv[:, 48:56]
        P.I("act", "activation", out=z, in_=vecs[:, o + V_LAM:o + V_LAM + 8], func=AF.Exp, scale=-1.0, reads=["vecs"], writes=["dvz"])
        P.I("dve", "tensor_scalar", out=p, in0=z, scalar1=-1.0 / 8, scalar2=1.0 / 7, op0=ALU.mult, op1=ALU.add, reads=["dvz"], writes=["dvp"])
        for n in (6, 5, 4, 3, 2, 1):
            P.I("dve", "tensor_tensor", out=p, in0=p, in1=z, op=ALU.mult, reads=["dvp", "dvz"], writes=["dvp"])
            P.I("dve", "tensor_scalar", out=p, in0=p, scalar1=-1.0, scalar2=1.0 / n, op0=ALU.mult, op1=ALU.add, reads=["dvp"], writes=["dvp"])
        P.I("dve", "tensor_tensor", out=p, in0=p, in1=z, op=ALU.mult, reads=["dvp", "dvz"], writes=["dvp"])
        P.I("dve", "tensor_scalar", out=dv[:, 0:8], in0=p, scalar1=-8.0, scalar2=None, op0=ALU.mult, reads=["dvp"], writes=["dvkk"])
        P.I("dve", "tensor_scalar", out=dv[:, 8:16], in0=p, scalar1=-4.0, scalar2=None, op0=ALU.mult, reads=["dvp"], writes=["dvkk"])
        P.I("dve", "tensor_scalar", out=dv[:, 16:24], in0=vecs[:, o + V_BA:o + V_BA + 8], scalar1=0.5, scalar2=None, op0=ALU.mult, reads=["vecs"], writes=["dvkk"])
        P.I("dve", "tensor_scalar", out=dv[:, 24:32], in0=vecs[:, o + V_BX:o + V_BX + 8], scalar1=0.5, scalar2=None, op0=ALU.mult, reads=["vecs"], writes=["dvkk"])
        P.I("act", "activation", out=dv[:, 32:40], in_=vecs[:, o + V_SINK:o + V_SINK + 8], func=AF.Exp, reads=["vecs"], writes=["dvkk"])
        win = self.w[(l, "win")].rearrange("(kc p) f -> p kc f", p=128)
        P.D("pool", "winxg", out=self.win_xg, in_=win[:, :, 0:1024], writes=["win_xg"])
        if self.mx_stop == "pre":
            P.barrier()
            return
        self.rmsnorm_xn(o + V_MIXG)
        if self.mx_stop == "norm":
            P.barrier()
            return
        ei = 0
        for tg in range(NTG):
            ts = slice(tg * 512, (tg + 1) * 512)
            for g in range(4):
                b = self.nb()
                for kc in range(DC):
                    lh = self.win_qkv[:, kc, g * 128:(g + 1) * 128]
                    P.I("pe", "matmul", out=self.ps[b][:, :], lhsT=lh, rhs=self.xn[:, kc, ts], start=(kc == 0), stop=(kc == DC - 1),
                        reads=["win_qkv", "xn%d_%d" % (kc, tg)], writes=["ps%d" % b])
                sl = (g % 2) * 2 + g // 2
                self.evac(ei, self.qT[:, tg * 4:(tg + 1) * 4, sl, :], self.ps[b][:, :].rearrange("p (b t) -> p b t", b=4),
                          ["ps%d" % b], ["qT%d_%d" % (g, tg)], scale=0.125)
                ei += 1
            b = self.nb()
            for kc in range(DC):
                P.I("pe", "matmul", out=self.ps[b][:, :], lhsT=self.win_qkv[:, kc, 512:640], rhs=self.xn[:, kc, ts],
                    start=(kc == 0), stop=(kc == DC - 1), reads=["win_qkv", "xn%d_%d" % (kc, tg)], writes=["ps%d" % b])
            self.evac(ei, self.kT[:, ts], self.ps[b][:, :], ["ps%d" % b], ["kT%d" % tg])
            ei += 1
            b = self.nb()
            for bl in range(4):
                blk = tg * 4 + bl
                for kc in range(DC):
                    P.I("pe", "matmul", out=self.ps[b][:, bl * 128:(bl + 1) * 128], lhsT=self.xn[:, kc, blk * 128:(blk + 1) * 128],
                        rhs=self.win_qkv[:, kc, 640:768], start=(kc == 0), stop=(kc == DC - 1),
                        reads=["win_qkv", "xn%d_%d" % (kc, tg)], writes=["ps%d" % b])
            src = self.ps[b][:, :].rearrange("p (b f) -> p b f", b=4)
            bs = slice(tg * 4, tg * 4 + 4)
            self.evac(0, self.Vd[0][:, bs, 0:64], src[:, :, 0:64], ["ps%d" % b], ["Vd0a%d" % tg])
            self.evac(0, self.Vd[0][:, bs, 64:128], src[:, :, 0:64], ["ps%d" % b], ["Vd0b%d" % tg])
            self.evac(0, self.Vd[1][:, bs, 0:64], src[:, :, 64:128], ["ps%d" % b], ["Vd1a%d" % tg])
            self.evac(0, self.Vd[1][:, bs, 64:128], src[:, :, 64:128], ["ps%d" % b], ["Vd1b%d" % tg])
        if self.mx_stop in ("v", "v1"):
            P.barrier()
            return
        b = self.nb()
        for cc in range(4):
            for kc in range(DC):
                P.I("pe", "matmul", out=self.ps[b][:, cc * 2:cc * 2 + 2], lhsT=self.win_xg[:, kc, cc * 128:(cc + 1) * 128],
                    rhs=self.xn[:, kc, 2046:2048], start=(kc == 0), stop=(kc == DC - 1),
                    reads=["win_xg", "xn%d_3" % kc], writes=["ps%d" % b])
        P.I("dve", "tensor_copy", out=self.xrh[:, :], in_=self.ps[b][:, 0:8], reads=["ps%d" % b], writes=["xrh"])
        if self.mx_stop == "p1":
            P.barrier()
            return
        e1 = self.ex1_in.ap()
        P.D("sp", "ex1a", out=e1[:, 0:64], in_=self.kT[:, 1920:2048].bitcast(F32), reads=["kT3"], writes=["ex1_in"])
        P.D("sp", "ex1a", out=e1[:, 64:96], in_=self.Vd[0][:, 15, 0:64].bitcast(F32), reads=["Vd0a3"], writes=["ex1_in"])
        P.D("sp", "ex1a", out=e1[:, 96:128], in_=self.Vd[1][:, 15, 0:64].bitcast(F32), reads=["Vd1a3"], writes=["ex1_in"])
        P.D("sp", "ex1a", out=e1[:, 128:136], in_=self.xrh[:, :], reads=["xrh"], writes=["ex1_in"])
        P.dma("pool", "ex1c", lambda e: e.collective_compute(
            "AllGather", ALU.bypass, replica_groups=[[0, 1], [2, 3], [4, 5], [6, 7]],
            ins=[self.ex1_in.ap().opt()], outs=[self.ex1_out.ap().opt()]), reads=["ex1_in"], writes=["ex1_out"], inc=1)
        self.attention(l)
        P.barrier()
        if self.mx_stop == "p2":
            return
        P.I("dve", "memset", ap=self.xr[:, :, 0:2], constant=0.0, writes=["xrpad"])
        P.I("dve", "tensor_copy", out=self.xr[:, :, 2050:2052], in_=self.xrhalo[:, :, ::-1], reads=["xrhalo"], writes=["xrhal"])
        ei = 0
        for cc in range(4):
            for tg in range(NTG):
                ts = slice(tg * 512, (tg + 1) * 512)
                b = self.nb()
                for kc in range(DC):
                    P.I("pe", "matmul", out=self.ps[b][:, :], lhsT=self.win_xg[:, kc, cc * 128:(cc + 1) * 128], rhs=self.xn[:, kc, ts],
                        start=(kc == 0), stop=(kc == DC - 1), reads=["win_xg", "xn%d_%d" % (kc, tg)], writes=["ps%d" % b])
                self.evac(1, self.xr[:, cc, 2 + tg * 512:2 + (tg + 1) * 512], self.ps[b][:, :], ["ps%d" % b], ["xr%d_%d" % (cc, tg)])
                b = self.nb()
                for kc in range(DC):
                    P.I("pe", "matmul", out=self.ps[b][:, :], lhsT=self.win_xg[:, kc, 512 + cc * 128:512 + (cc + 1) * 128], rhs=self.xn[:, kc, ts],
                        start=(kc == 0), stop=(kc == DC - 1), reads=["win_xg", "xn%d_%d" % (kc, tg)], writes=["ps%d" % b])
                P.I("act", "activation", out=self.gg[:, cc, ts], in_=self.ps[b][:, :], func=AF.Gelu, reads=["ps%d" % b], writes=["gg%d_%d" % (cc, tg)])
        P.barrier()
        if self.mx_stop == "p3":
            return
        if "f2" in self.phases:
            self.ffn_prefetch(l, 2)
        P.D("pool", "worec", out=self.wo, in_=self.w[(l, "wout")][0:512, :].rearrange("(c p) d -> p c d", p=128), writes=["wo"])
        self.lru(l)
        P.barrier()

    def ex1_receive(self):
        P = self.P
        P.D("sp", "ex1b", out=self.ex1s[:, :, :], in_=self.ex1_out.ap().rearrange("(r p) w -> p r w", p=128),
            reads=["ex1_out"], writes=["ex1s"])
        s0 = self.sel[:, 0:1]
        s1 = self.sel[:, 1:2]
        e0b = self.ex1s[:, 0, 0:128].bitcast(BF16)
        e1b = self.ex1s[:, 1, 0:128].bitcast(BF16)
        etb = self.ex1t[:, 0:128].bitcast(BF16)
        P.I("dve", "tensor_scalar", out=etb, in0=e1b, scalar1=s1, scalar2=None, op0=ALU.mult,
            reads=["ex1s", "sel"], writes=["ex1t"])
        P.I("dve", "scalar_tensor_tensor", out=self.kT[:, 2048:2176], in0=e0b[:, 0:128], scalar=s0, in1=etb[:, 0:128],
            op0=ALU.mult, op1=ALU.add, reads=["ex1s", "ex1t", "sel"], writes=["kT4"])
        P.I("dve", "scalar_tensor_tensor", out=self.vhalo[:, :], in0=e0b[:, 128:256], scalar=s0, in1=etb[:, 128:256],
            op0=ALU.mult, op1=ALU.add, reads=["ex1s", "ex1t", "sel"], writes=["vhalo"])
        for kv in range(2):
            for hf, nm in ((0, "a"), (1, "b")):
                P.I("dve", "tensor_copy", out=self.Vd[kv][:, 16, hf * 64:(hf + 1) * 64], in_=self.vhalo[:, kv * 64:(kv + 1) * 64],
                    reads=["vhalo"], writes=["Vd%d%s4" % (kv, nm)])
        xh0 = self.ex1s[:, 0, 128:136]
        xh1 = self.ex1s[:, 1, 128:136]
        xht = self.ex1t[:, 128:136]
        P.I("dve", "tensor_scalar", out=xht, in0=xh1, scalar1=s1, scalar2=None, op0=ALU.mult, reads=["ex1s", "sel"], writes=["ex1tx"])
        P.I("dve", "scalar_tensor_tensor", out=self.xrhalo[:, :, :].rearrange("p c t -> p (c t)"), in0=xh0, scalar=s0, in1=xht,
            op0=ALU.mult, op1=ALU.add, reads=["ex1s", "ex1tx", "sel"], writes=["xrhalo"])

    def attention(self, l):
        P = self.P
        o = l * VL
        psL = (0, 1, 2)
        esink = self.dv[:, 32:40]
        it = 0
        for n in range(NB):
            tg = n // 4
            nq = n % 4
            if n == 8:
                self.ex1_receive()
            for kvh in range(2):
                ks = slice(kvh * 64, (kvh + 1) * 64)
                kbs = []
                if n > 0:
                    kbs.append((0, n - 1))
                kbs.append((1, n))
                kbs.append((2, n + 1) if n < NB - 1 else (3, 16))
                pset = it % 2
                bO = 3 + pset
                bS = 5
                it += 1
                for i, (tb, blk) in enumerate(kbs):
                    b = psL[i]
                    kname = "kT%d" % (blk // 4) if blk < 16 else "kT4"
                    P.I("pe", "matmul", out=self.ps[b][:, :], lhsT=self.kT[ks, blk * 128:(blk + 1) * 128],
                        rhs=self.Rgg[ks, n * 512:(n + 1) * 512], start=True, stop=False,
                        reads=[kname] + ["qT%d_%d" % (g, tg) for g in range(4)], writes=["ps%d" % b])
                    P.I("pe", "matmul", out=self.ps[b][:, :], lhsT=self.identb[:], rhs=self.Rxr[:, tb * 1024 + kvh * 512:tb * 1024 + kvh * 512 + 512],
                        start=False, stop=True, reads=["identb", "biasb"], writes=["ps%d" % b])
                    P.I("act", "activation", out=self.pT[:, pset * 3 + i, :], in_=self.ps[b][:, :], func=AF.Exp,
                        reads=["ps%d" % b], writes=["pT%d_%d" % (pset, i)])
                psO = self.ps[bO]
                nk = len(kbs)
                for i, (tb, blk) in enumerate(kbs):
                    pv = self.pT[:, pset * 3 + i, :]
                    bn = blk // 4 if blk < 16 else 4
                    rd = ["pT%d_%d" % (pset, i)]
                    P.I("pe", "matmul", out=psO[:, :], lhsT=self.Vd[kvh][:, blk, :], rhs=pv,
                        start=(i == 0), stop=(i == nk - 1), reads=rd + ["Vd%da%d" % (kvh, bn), "Vd%db%d" % (kvh, bn)], writes=["ps%d" % bO])
                for i, (tb, blk) in enumerate(kbs):
                    pv = self.pT[:, pset * 3 + i, :]
                    P.I("pe", "matmul", out=self.ps[bS][:, :], lhsT=self.onesb[:], rhs=pv,
                        start=(i == 0), stop=(i == nk - 1), reads=["pT%d_%d" % (pset, i), "onesb"], writes=["ps%d" % bS])
                dbuf = self.den[pset]
                dn = "den%d" % pset
                den = dbuf.rearrange("p (g t) -> p g t", g=4)
                P.I("dve", "tensor_tensor", out=den, in0=self.ps[bS][:, :].rearrange("p (g t) -> p g t", g=4),
                    in1=esink[:, kvh * 4:(kvh + 1) * 4].unsqueeze(2).broadcast_to([128, 4, 128]), op=ALU.add,
                    reads=["ps%d" % bS, "dvkk"], writes=[dn])
                P.I("act", "activation", out=dbuf, in_=dbuf, func=AF.Ln, reads=[dn], writes=[dn])
                P.I("act", "activation", out=dbuf, in_=dbuf, func=AF.Exp, scale=-1.0, reads=[dn], writes=[dn])
                on = self.oN[:, kvh * 2:kvh * 2 + 2, nq * 128:(nq + 1) * 128]
                P.I("dve", "tensor_tensor", out=on[0:64], in0=psO[0:64, 0:256].rearrange("p (j t) -> p j t", j=2),
                    in1=dbuf[0:64, 0:256].rearrange("p (j t) -> p j t", j=2), op=ALU.mult,
                    reads=["ps%d" % bO, dn], writes=["oNe%d_%d" % (kvh, nq)])
                P.I("dve", "tensor_tensor", out=on[64:128], in0=psO[64:128, 256:512].rearrange("p (j t) -> p j t", j=2),
                    in1=dbuf[64:128, 256:512].rearrange("p (j t) -> p j t", j=2), op=ALU.mult,
                    reads=["ps%d" % bO, dn], writes=["oNo%d_%d" % (kvh, nq)])
            if nq == 3 and tg == 0 and self.debug == 3:
                self.dbg_sb("dbg_oN", self.Rw[:, 8192:12288], [128, 4096], BF16,
                            ["oNe%d_%d" % (k_, q_) for k_ in range(2) for q_ in range(4)] + ["oNo%d_%d" % (k_, q_) for k_ in range(2) for q_ in range(4)])
                self.dbg_sb("dbg_qT", self.Rgg[:, :], [128, 8192], BF16, ["qT%d_%d" % (g_, t_) for g_ in range(4) for t_ in range(4)])
                self.dbg_sb("dbg_kv", self.Rh[:, 0:6528], [128, 6528], BF16, ["kT%d" % i for i in range(5)])
                self.dbg_sb("dbg_pT", self.Rxr[:, 8192:11264], [128, 3072], BF16, ["pT%d_%d" % (a_, b_) for a_ in range(2) for b_ in range(3)])
            if nq == 3:
                ts = slice(tg * 512, (tg + 1) * 512)
                onames = [["oNe%d_%d" % (c // 2, q) for q in range(4)] + ["oNo%d_%d" % (c // 2, q) for q in range(4)] for c in range(4)]
                P_ = self.P
                for c in range(4):
                    P_.I("act", "activation", out=self.sqa[:, c, :], in_=self.oN[:, c, :], func=AF.Square, reads=onames[c], writes=["sqa%d" % c])
                b = 6
                for c in range(4):
                    P_.I("pe", "matmul", out=self.ps[b][:, :], lhsT=self.onesb[:], rhs=self.sqa[:, c, :], start=(c == 0), stop=(c == 3),
                         reads=["sqa%d" % c, "onesb"], writes=["ps%d" % b])
                P_.I("act", "activation", out=self.asd, in_=self.ps[b][:, :], func=AF.Ln, scale=1.0 / 512, bias=EPS, reads=["ps%d" % b], writes=["asd"])
                P_.I("act", "activation", out=self.asd, in_=self.asd, func=AF.Exp, scale=-0.5, reads=["asd"], writes=["asd"])
                for c in range(4):
                    P_.I("dve", "scalar_tensor_tensor", out=self.yatt[:, c, :], in0=self.oN[:, c, :],
                         scalar=self.vecs[:, o + V_ATTG + c:o + V_ATTG + c + 1], in1=self.asd, op0=ALU.mult, op1=ALU.mult,
                         reads=onames[c] + ["asd", "vecs"], writes=["yatt%d" % c])
                for dc in range(DC):
                    b = 6 + (dc + 1) % 2
                    for c in range(4):
                        P_.I("pe", "matmul", out=self.ps[b][:, :], lhsT=self.wo[:, c, dc * 128:(dc + 1) * 128], rhs=self.yatt[:, c, :],
                             start=(c == 0), stop=(c == 3), reads=["wo", "yatt%d" % c], writes=["ps%d" % b])
                    P_.I("dve", "tensor_tensor", out=self.xT[:, dc, ts], in0=self.ps[b][:, :], in1=self.xT[:, dc, ts], op=ALU.add,
                         reads=["ps%d" % b, "xT%d_%d" % (dc, tg)], writes=["xT%d_%d" % (dc, tg)])

    def lru_T(self, l, di, cc, tt, ui):
        P = self.P
        dv = self.dv
        s = ui % 2
        ts = slice(tt * 512, (tt + 1) * 512)
        xc = self.xc[:, cc, ts]
        xcn = "xc%d_%d" % (cc, tt)
        col = di * 4 + cc
        P.I("act", "copy", out=self.xcb[s], in_=xc, reads=[xcn], writes=["xcb%d" % s])
        bR = (ui % 2) * 2
        bI = bR + 1
        P.I("pe", "matmul", out=self.ps[bR][:, :], lhsT=self.lruw[:, (di * 2 + 0) * 4 + cc, :], rhs=self.xcb[s], start=True, stop=True,
            reads=["lruw", "xcb%d" % s], writes=["ps%d" % bR])
        P.I("pe", "matmul", out=self.ps[bI][:, :], lhsT=self.lruw[:, (di * 2 + 1) * 4 + cc, :], rhs=self.xcb[s], start=True, stop=True,
            reads=["lruw", "xcb%d" % s], writes=["ps%d" % bI])
        tR, tI, tS = self.tR[s], self.tI[s], self.tS[s]
        P.I("act", "activation", out=tR, in_=self.ps[bR][:, :], func=AF.Tanh, scale=0.5, bias=dv[:, 16 + col:17 + col],
            reads=["ps%d" % bR, "dvkk"], writes=["tR%d" % s])
        P.I("act", "activation", out=tI, in_=self.ps[bI][:, :], func=AF.Tanh, scale=0.5, bias=dv[:, 24 + col:25 + col],
            reads=["ps%d" % bI, "dvkk"], writes=["tI%d" % s])
        P.I("act", "activation", out=tS, in_=tR, func=AF.Exp, scale=dv[:, col:col + 1], bias=dv[:, col:col + 1],
            reads=["tR%d" % s, "dvkk"], writes=["tS%d" % s])
        P.I("act", "activation", out=tR, in_=tR, func=AF.Exp, scale=dv[:, 8 + col:9 + col], bias=dv[:, 8 + col:9 + col],
            reads=["tR%d" % s, "dvkk"], writes=["tR%d" % s])
        return s

    def lru_S(self, s):
        self.P.I("act", "activation", out=self.tS[s], in_=self.tS[s], func=AF.Sqrt, scale=-0.25, bias=0.25,
                 reads=["tS%d" % s], writes=["tS%d" % s])

    def lru_U(self, s, cc, tt):
        P = self.P
        xc = self.xc[:, cc, tt * 512:(tt + 1) * 512]
        tI, tS = self.tI[s], self.tS[s]
        P.I("dve", "scalar_tensor_tensor", out=tI, in0=tI, scalar=1.0, in1=xc, op0=ALU.add, op1=ALU.mult,
            reads=["tI%d" % s, "xc%d_%d" % (cc, tt)], writes=["tI%d" % s])
        P.I("dve", "tensor_tensor", out=tI, in0=tI, in1=tS, op=ALU.mult, reads=["tI%d" % s, "tS%d" % s], writes=["tI%d" % s])

    def conv_tile(self, l, cc, tt):
        P = self.P
        o = l * VL
        vecs = self.vecs
        t0 = tt * 512
        out = self.xc[:, cc, t0:t0 + 512]
        rd = ["xr%d_%d" % (cc, tt), "vecs"]
        rd.append("xr%d_%d" % (cc, tt - 1) if tt > 0 else "xrpad")
        rd.append("xr%d_%d" % (cc, tt + 1) if tt < NTG - 1 else "xrhal")
        wn = "xc%d_%d" % (cc, tt)
        wc = o + V_CONV + cc * 5
        bcol = vecs[:, o + V_CONVB + cc:o + V_CONVB + cc + 1]
        if cc < 4:
            P.I("dve", "tensor_scalar", out=out, in0=self.xr[:, cc, t0:t0 + 512], scalar1=vecs[:, wc:wc + 1],
                scalar2=bcol, op0=ALU.mult, op1=ALU.add, reads=rd, writes=[wn])
            for j in range(1, 5):
                P.I("dve", "scalar_tensor_tensor", out=out, in0=self.xr[:, cc, t0 + j:t0 + j + 512], scalar=vecs[:, wc + j:wc + j + 1],
                    in1=out, op0=ALU.mult, op1=ALU.add, reads=rd + [wn], writes=[wn])
        else:
            tmp = self.rsd
            P.I("pool", "tensor_scalar", out=out, in0=self.xr[:, cc, t0:t0 + 512], scalar1=vecs[:, wc:wc + 1],
                scalar2=bcol, op0=ALU.mult, op1=ALU.add, reads=rd, writes=[wn])
            for j in range(1, 5):
                P.I("pool", "tensor_scalar", out=tmp, in0=self.xr[:, cc, t0 + j:t0 + j + 512], scalar1=vecs[:, wc + j:wc + j + 1],
                    scalar2=0.0, op0=ALU.mult, op1=ALU.add, reads=rd, writes=["rsd"])
                P.I("pool", "tensor_tensor", out=out, in0=out, in1=tmp, op=ALU.add, reads=[wn, "rsd"], writes=[wn])

    def lru(self, l):
        P = self.P
        o = l * VL
        vecs = self.vecs
        groups = [[0, 1], [2, 3], [4, 5], [6, 7]]
        for cc in (0, 1):
            self.conv_tile(l, cc, 0)
        ui = 0
        for cp in range(2):
            ccs = (2 * cp, 2 * cp + 1)
            for tt in range(NTG):
                ss = []
                for cc in ccs:
                    ss.append(self.lru_T(l, 0, cc, tt, ui))
                    ui += 1
                for s in ss:
                    self.lru_S(s)
                for s, cc in zip(ss, ccs):
                    self.lru_U(s, cc, tt)
                    if tt + 1 < NTG:
                        self.conv_tile(l, cc, tt + 1)
                    elif cp == 0:
                        self.conv_tile(l, cc + 2, 0)
                    t0 = tt * 512
                    hn = "xr%d_%d" % (cc, tt)
                    init = 0.0 if tt == 0 else self.xr[:, cc, 2 + t0 - 1:2 + t0]
                    rd = ["tR%d" % s, "tI%d" % s] + ([] if tt == 0 else ["xr%d_%d" % (cc, tt - 1)])
                    P.I("dve", "tensor_tensor_scan", out=self.xr[:, cc, 2 + t0:2 + t0 + 512], data0=self.tR[s], data1=self.tI[s],
                        initial=init, op0=ALU.mult, op1=ALU.add, reads=rd, writes=[hn])
            hae = self.hAend[:, 2 * cp:2 * cp + 2]
            P.I("dve", "tensor_copy", out=hae, in_=self.xr[:, 2 * cp:2 * cp + 2, 2049], reads=["xr%d_3" % cc for cc in ccs], writes=["hAend%d" % cp])
            P.D("sp", "ex2a%d" % cp, out=self.ex2_in[cp].ap(), in_=hae, reads=["hAend%d" % cp], writes=["ex2_in%d" % cp])
            P.dma("pool", "ex2c%d" % cp, (lambda cp_: lambda e: e.collective_compute(
                "AllGather", ALU.bypass, replica_groups=groups,
                ins=[self.ex2_in[cp_].ap().opt()], outs=[self.ex2_out[cp_].ap().opt()]))(cp),
                reads=["ex2_in%d" % cp], writes=["ex2_out%d" % cp], inc=1)
        for cp in range(2):
            ccs = (2 * cp, 2 * cp + 1)
            cs = slice(2 * cp, 2 * cp + 2)
            P.D("sp", "ex2b%d" % cp, out=self.ex2s[:, :, cs], in_=self.ex2_out[cp].ap().rearrange("(r p) w -> p r w", p=128),
                reads=["ex2_out%d" % cp], writes=["ex2s%d" % cp])
            P.I("dve", "tensor_scalar", out=self.ex2t[:, cs], in0=self.ex2s[:, 1, cs], scalar1=self.sel[:, 1:2], scalar2=None, op0=ALU.mult,
                reads=["ex2s%d" % cp, "sel"], writes=["ex2t%d" % cp])
            P.I("dve", "scalar_tensor_tensor", out=self.hinit[:, cs], in0=self.ex2s[:, 0, cs], scalar=self.sel[:, 0:1], in1=self.ex2t[:, cs],
                op0=ALU.mult, op1=ALU.add, reads=["ex2s%d" % cp, "ex2t%d" % cp, "sel"], writes=["hinit%d" % cp])
            for tt in range(NTG - 1, -1, -1):
                t0 = tt * 512
                ts = slice(t0, t0 + 512)
                ss = []
                for cc in ccs:
                    ss.append(self.lru_T(l, 1, cc, tt, ui))
                    ui += 1
                for s in ss:
                    self.lru_S(s)
                for s, cc in zip(ss, ccs):
                    self.lru_U(s, cc, tt)
                    xcn = "xc%d_%d" % (cc, tt)
                    hn = "xr%d_%d" % (cc, tt)
                    if tt == NTG - 1:
                        init = self.hinit[:, cc:cc + 1]
                        rd = ["hinit%d" % cp]
                    else:
                        init = self.xc[:, cc, t0 + 512:t0 + 513]
                        rd = ["xc%d_%d" % (cc, tt + 1)]
                    P.I("dve", "tensor_tensor_scan", out=self.xc[:, cc, ts][:, ::-1], data0=self.tR[s][:, ::-1], data1=self.tI[s][:, ::-1],
                        initial=init, op0=ALU.mult, op1=ALU.add, reads=["tR%d" % s, "tI%d" % s] + rd, writes=[xcn])
                    hA = self.xr[:, cc, 2 + t0:2 + t0 + 512]
                    P.I("dve", "tensor_tensor", out=hA, in0=hA, in1=self.xc[:, cc, ts], op=ALU.add, reads=[hn, xcn], writes=[hn])
                    P.I("dve", "tensor_tensor", out=hA, in0=hA, in1=self.gg[:, cc, ts], op=ALU.mult, reads=[hn, "gg%d_%d" % (cc, tt)], writes=[hn])
                if cp == 0:
                    continue
                ysrc = [self.xr[:, cc, 2 + t0:2 + t0 + 512] for cc in range(4)]
                ynames = ["xr%d_%d" % (cc, tt) for cc in range(4)]
                self.rstd_tg(4, ysrc, ynames, 1.0 / 512, self.sq4, "sq4", self.rsd, "rsd", bank=4)
                for cc in range(4):
                    yb = self.yrecb[:, cc, tt * 1024 + 512:tt * 1024 + 1024]
                    P.I("dve", "scalar_tensor_tensor", out=yb, in0=ysrc[cc], scalar=vecs[:, o + V_RECG + cc:o + V_RECG + cc + 1], in1=self.rsd,
                        op0=ALU.mult, op1=ALU.mult, reads=[ynames[cc], "rsd", "vecs"], writes=["yrb%d_%d" % (cc, tt)])
                for dc in range(DC):
                    b = 5 + dc % 3
                    for cc in range(4):
                        yb = self.yrecb[:, cc, tt * 1024 + 512:tt * 1024 + 1024]
                        P.I("pe", "matmul", out=self.ps[b][:, :], lhsT=self.wo[:, cc, dc * 128:(dc + 1) * 128], rhs=yb,
                            start=(cc == 0), stop=(cc == 3), reads=["wo", "yrb%d_%d" % (cc, tt)], writes=["ps%d" % b])
                    P.I("dve", "tensor_tensor", out=self.xT[:, dc, ts], in0=self.ps[b][:, :], in1=self.xT[:, dc, ts], op=ALU.add,
                        reads=["ps%d" % b, "xT%d_%d" % (dc, tt)], writes=["xT%d_%d" % (dc, tt)])


_HPERM = [0, 2, 1, 3, 4, 6, 5, 7]
_N_BUCKETS = 32
_MAX_DIST = 128


def _t5_bucket(rel):
    half = _N_BUCKETS // 2
    max_exact = half // 2
    ret = (rel > 0).astype(np.int64) * half
    n = np.abs(rel)
    n_f = np.maximum(n, 1).astype(np.float32)
    large = max_exact + (np.log(n_f / np.float32(max_exact)) / np.float32(math.log(_MAX_DIST / max_exact))
                         * np.float32(half - max_exact)).astype(np.int32)
    large = np.minimum(large, half - 1)
    return ret + np.where(n < max_exact, n, large)


def _bias_tables(rel_bias, flip):
    j = np.arange(128)[:, None]
    t = np.arange(128)[None, :]
    out = np.empty((128, 4, 8, 128), np.float32)
    rels = [(-128 + j - t), (j - t), (128 + j - t), (255 - j - t)]
    for ti, rel in enumerate(rels):
        rel_true = -rel if flip else rel
        b = _t5_bucket(rel_true)
        tab = rel_bias[b]
        tab = np.transpose(tab, (0, 2, 1))[:, _HPERM, :]
        mask = (np.abs(rel) <= 128)[:, None, :]
        out[:, ti] = np.where(mask, tab, np.float32(-1e30))
    return out.reshape(128, 4096)


def _vecs(inp, r):
    v = np.zeros((128, NV), np.float32)

    def chunks(a, n):
        return np.ascontiguousarray(a.reshape(n, 128).T)
    for l in range(DEPTH):
        o = l * VL
        v[:, o + V_F1G:o + V_F1G + 8] = chunks(inp["ffn1_norm"][l], 8)
        v[:, o + V_MIXG:o + V_MIXG + 8] = chunks(inp["mix_norm"][l], 8)
        v[:, o + V_F2G:o + V_F2G + 8] = chunks(inp["ffn2_norm"][l], 8)
        cw = inp["conv_w"][l]
        w5 = np.zeros((5, 512), np.float32)
        if r == 0:
            w5[0:4] = cw
        else:
            w5[1:5] = cw[::-1]
        for cc in range(4):
            v[:, o + V_CONV + cc * 5:o + V_CONV + cc * 5 + 5] = w5[:, cc * 128:(cc + 1) * 128].T
        v[:, o + V_CONVB:o + V_CONVB + 4] = chunks(inp["conv_b"][l], 4)
        dirs = (0, 1) if r == 0 else (1, 0)
        for di, dsrc in enumerate(dirs):
            v[:, o + V_BA + di * 4:o + V_BA + di * 4 + 4] = chunks(inp["lru_b_a"][l, dsrc], 4)
            v[:, o + V_BX + di * 4:o + V_BX + di * 4 + 4] = chunks(inp["lru_b_x"][l, dsrc], 4)
            v[:, o + V_LAM + di * 4:o + V_LAM + di * 4 + 4] = chunks(inp["lru_lambda"][l, dsrc], 4)
        v[:, o + V_RECG:o + V_RECG + 4] = chunks(inp["lru_out_norm"][l], 4)
        v[:, o + V_ATTG:o + V_ATTG + 4] = chunks(inp["attn_out_norm"][l], 4)
        v[:, o + V_SINK:o + V_SINK + 8] = inp["attn_sink"][l][_HPERM][None, :]
    v[:, V_FINAL:V_FINAL + 8] = chunks(inp["final_norm"], 8)
    return v


def _lru_w(inp, l, r):
    out = np.zeros((128, 16, 128), np.float32)
    dirs = (0, 1) if r == 0 else (1, 0)
    for di, dsrc in enumerate(dirs):
        for ki, key in enumerate(("lru_w_a", "lru_w_x")):
            wsrc = inp[key][l, dsrc]
            for cc in range(4):
                idx = (di * 2 + ki) * 4 + cc
                out[0:64, idx, 0:64] = wsrc[2 * cc]
                out[64:128, idx, 64:128] = wsrc[2 * cc + 1]
    return out.reshape(128, 2048)


_CACHE = {}


def _get_nc(layers, last, debug, phases):
    key = (tuple(layers), last, debug, tuple(phases))
    if key not in _CACHE:
        _CACHE[key] = K(layers, last, debug, phases)
        _CACHE[key].build()
    return _CACHE[key]


def _core_inputs(inp, xs, layers, names):
    f32 = lambda a: np.ascontiguousarray(a, dtype=np.float32)
    shared = {}
    for r in (0, 1):
        shared[("vecs", r)] = _vecs(inp, r)
        shared[("bias", r)] = _bias_tables(f32(inp["rel_bias"]), r == 1)
        shared[("sel", r)] = np.tile(np.array([[0.0, 1.0]] if r == 0 else [[1.0, 0.0]], np.float32), (128, 1))
        for l in layers:
            if ("lru_w_%d" % l) in names:
                shared[("lru_w_%d" % l, r)] = _lru_w(inp, l, r)
    wmap = {"f1_wg": "ffn1_w_gate", "f1_wu": "ffn1_w_up", "f1_wd": "ffn1_w_down", "f2_wg": "ffn2_w_gate",
            "f2_wu": "ffn2_w_up", "f2_wd": "ffn2_w_down", "w_in": "w_in", "w_out": "w_out"}
    maps = []
    for c in range(NCORES):
        r = c % 2
        m = {}
        for n in names:
            if n == "x_in":
                m[n] = xs[c]
            elif (n, r) in shared:
                m[n] = shared[(n, r)]
            else:
                base, l = n.rsplit("_", 1)
                m[n] = f32(inp[wmap[base]][int(l)])
        maps.append(m)
    return maps


def _shard_x(x):
    xs = []
    for c in range(NCORES):
        b, r = c // 2, c % 2
        if r == 0:
            xs.append(np.ascontiguousarray(x[b, 0:T]))
        else:
            xs.append(np.ascontiguousarray(x[b, T:2 * T][::-1]))
    return xs


def _unshard(ys):
    out = np.empty((4, 2 * T, D), np.float32)
    for c in range(NCORES):
        b, r = c // 2, c % 2
        if r == 0:
            out[b, 0:T] = ys[c]
        else:
            out[b, T:2 * T] = ys[c][::-1]
    return out


def run_layers(inp, xs, layers, last, debug=False, phases=("f1", "mx", "f2")):
    k = _get_nc(layers, last, debug, phases)
    maps = _core_inputs(inp, xs, layers, k.in_names)
    res = run_bass_kernel_spmd(k.nc, maps, core_ids=list(range(NCORES)))
    return res.results, k


LAUNCH_GROUPS = [[0, 1, 2, 3]]


def kernel(**inputs):
    inp = {k: np.asarray(v) for k, v in inputs.items()}
    xs = _shard_x(np.ascontiguousarray(inp["x"], dtype=np.float32))
    for gi, layers in enumerate(LAUNCH_GROUPS):
        last = gi == len(LAUNCH_GROUPS) - 1
        results, _ = run_layers(inp, xs, layers, last)
        xs = [results[c]["y_out"] for c in range(NCORES)]
    return _unshard(xs)
```

```python
import math
from contextlib import ExitStack
import numpy as np
import concourse.bass as bass
import concourse.mybir as mybir
from concourse.bass_utils import run_bass_kernel_spmd

F32 = mybir.dt.float32
BF16 = mybir.dt.bfloat16
AF = mybir.ActivationFunctionType
ALU = mybir.AluOpType
AX = mybir.AxisListType

NCORES = 8
DEPTH = 4
T = 2048
D = 1024
DC = 8
DFF = 2816
FG = 11
DIN = 1792
NTG = 4
NB = 16
EPS = 1e-6
ENGS = ("pe", "act", "dve", "pool", "sp")


class _Ins:
    __slots__ = ("eng", "fn", "waits", "signal", "dma_sem", "ctr", "is_dma", "inc", "idx", "epoch")
    _epoch = 0
    _n = 0

    def __init__(self, eng, fn, is_dma=False, dma_sem=None, inc=16):
        self.eng = eng
        self.fn = fn
        self.waits = []
        self.signal = False
        self.dma_sem = dma_sem
        self.ctr = None
        self.is_dma = is_dma
        self.inc = inc
        _Ins._n += 1
        self.idx = _Ins._n
        self.epoch = _Ins._epoch


class Prog:
    def __init__(self, nc, stack):
        self.nc = nc
        self.stack = stack
        self.ins = []
        self.res = {}
        self.dma_tot = {}
        self.dma_sems = {}
        _Ins._epoch = 0
        self.eng_sems = {(e, 0): stack.enter_context(nc.semaphore("ctr_%s_0" % e)) for e in ENGS}
        self.last = {e: None for e in ENGS}

    def new_epoch(self):
        _Ins._epoch += 1
        for e in ENGS:
            self.eng_sems[(e, _Ins._epoch)] = self.stack.enter_context(self.nc.semaphore("ctr_%s_%d" % (e, _Ins._epoch)))

    def sbuf(self, name, shape, dt):
        return self.stack.enter_context(self.nc.sbuf_tensor(name, list(shape), dt))

    def psum(self, name, shape, dt=F32):
        return self.stack.enter_context(self.nc.psum_tensor(name, list(shape), dt))

    def dsem(self, name):
        if name not in self.dma_sems:
            self.dma_sems[name] = self.stack.enter_context(self.nc.semaphore("d_" + name))
            self.dma_tot[name] = 0
        return name

    def _deps(self, ins, reads, writes):
        evs = []
        for r in reads:
            st = self.res.setdefault(r, [[], []])
            evs.extend(st[0])
        for w in writes:
            st = self.res.setdefault(w, [[], []])
            for ev in st[0] + st[1]:
                if ev[0] == "dma" or ins.is_dma or ev[1].eng != ins.eng:
                    evs.append(ev)
        best = {}
        for ev in evs:
            if ev[0] == "dma":
                best[("d", ev[1])] = ("dma", ev[1], self.dma_tot[ev[1]])
            else:
                key = ("e", ev[1].eng, ev[1].epoch)
                if key not in best or ev[1].idx > best[key][1].idx:
                    best[key] = ev
        ins.waits = list(best.values())

    def _commit(self, ev, reads, writes):
        for r in reads:
            self.res[r][1].append(ev)
        for w in writes:
            self.res[w] = [[ev], []]

    def op(self, eng, fn, reads=(), writes=()):
        ins = _Ins(eng, fn)
        self._deps(ins, reads, writes)
        self.ins.append(ins)
        self._commit(("eng", ins), reads, writes)
        self.last[eng] = ins
        return ins

    def dma(self, q, sem, fn, reads=(), writes=(), inc=16):
        ins = _Ins(q, fn, is_dma=True, dma_sem=sem, inc=inc)
        self._deps(ins, reads, writes)
        self.dma_tot[sem] += inc
        ev = ("dma", sem, self.dma_tot[sem])
        self.ins.append(ins)
        self._commit(ev, reads, writes)
        return ev

    def I(self, eng, name, reads=(), writes=(), **kw):
        return self.op(eng, lambda e: getattr(e, name)(**kw), reads, writes)

    def D(self, q, sem, reads=(), writes=(), **kw):
        return self.dma(q, sem, lambda e: e.dma_start(**kw), reads, writes)

    def wait_all(self, eng, resources):
        ins = _Ins(eng, None)
        self._deps(ins, list(resources), [])
        self.ins.append(ins)

    def barrier(self, skip_sems=()):
        lasts = dict(self.last)
        dmas = [("dma", s, v) for s, v in self.dma_tot.items() if v > 0 and s not in skip_sems]
        for e in ENGS:
            ins = _Ins(e, None)
            ins.waits = [("eng", l) for e2, l in lasts.items() if l is not None and e2 != e] + dmas
            self.ins.append(ins)

    def emit(self):
        nc = self.nc
        for ins in self.ins:
            for ev in ins.waits:
                if ev[0] == "eng":
                    ev[1].signal = True
        cnt = {}
        for ins in self.ins:
            if ins.signal and not ins.is_dma:
                k_ = (ins.eng, ins.epoch)
                cnt[k_] = cnt.get(k_, 0) + 1
                ins.ctr = cnt[k_]
        self.counts = cnt
        per = {e: [i for i in self.ins if i.eng == e] for e in ENGS}

        def run(eng_name, eh):
            known = {}
            for ins in per[eng_name]:
                need = {}
                for ev in ins.waits:
                    if ev[0] == "eng":
                        key = ("e", ev[1].eng, ev[1].epoch)
                        val = ev[1].ctr
                    else:
                        key = ("d", ev[1])
                        val = ev[2]
                    if val > need.get(key, 0):
                        need[key] = val
                for key, val in need.items():
                    if known.get(key, 0) >= val:
                        continue
                    known[key] = val
                    sem = self.eng_sems[(key[1], key[2])] if key[0] == "e" else self.dma_sems[key[1]]
                    eh.wait_ge(sem, val)
                if ins.fn is None:
                    continue
                r = ins.fn(eh)
                if ins.is_dma:
                    if ins.inc == 16:
                        r.then_inc(self.dma_sems[ins.dma_sem], 16)
                    else:
                        r.then_inc(self.dma_sems[ins.dma_sem])
                elif ins.signal:
                    r.then_inc(self.eng_sems[(eng_name, ins.epoch)], 1)

        with nc.Block() as block:
            @block.tensor
            def _(e):
                run("pe", e)

            @block.scalar
            def _(e):
                run("act", e)

            @block.vector
            def _(e):
                run("dve", e)

            @block.gpsimd
            def _(e):
                run("pool", e)

            @block.sync
            def _(e):
                run("sp", e)


V_F1G, V_MIXG, V_F2G = 0, 8, 16
V_CONV = 24
V_CONVB = 44
V_BA = 48
V_BX = 56
V_LAM = 64
V_RECG = 72
V_ATTG = 76
V_SINK = 84
VL = 92
V_FINAL = DEPTH * VL
NV = V_FINAL + 8


V_ATTG4 = V_ATTG


class K:
    def __init__(self, layers, last, debug=False, phases=("f1", "mx", "f2")):
        self.layers = list(layers)
        self.last = last
        self.debug = debug
        self.phases = phases
        self.nc = bass.Bass("TRN2", target_bir_lowering=False)
        self.dbg = []
        self.bank = 0
        self.ffn_pref = set()
        self.in_names = []
        import os
        self.mx_stop = os.environ.get("MXSTOP", "")

    def dram_in(self, name, shape, dt=F32):
        self.in_names.append(name)
        return self.nc.dram_tensor(name, list(shape), dt, kind="ExternalInput").ap()

    def build(self):
        nc = self.nc
        self.x_in = self.dram_in("x_in", [T, D])
        self.vecs_in = self.dram_in("vecs", [128, NV])
        self.sel_in = self.dram_in("sel", [128, 2])
        self.bias_in = self.dram_in("bias", [128, 4096])
        self.w = {}
        for l in self.layers:
            if "f1" in self.phases:
                self.w[(l, "wg", 1)] = self.dram_in("f1_wg_%d" % l, [D, DFF])
                self.w[(l, "wu", 1)] = self.dram_in("f1_wu_%d" % l, [D, DFF])
                self.w[(l, "wd", 1)] = self.dram_in("f1_wd_%d" % l, [DFF, D])
            if "f2" in self.phases:
                self.w[(l, "wg", 2)] = self.dram_in("f2_wg_%d" % l, [D, DFF])
                self.w[(l, "wu", 2)] = self.dram_in("f2_wu_%d" % l, [D, DFF])
                self.w[(l, "wd", 2)] = self.dram_in("f2_wd_%d" % l, [DFF, D])
            if "mx" in self.phases:
                self.w[(l, "win")] = self.dram_in("w_in_%d" % l, [D, DIN])
                self.w[(l, "wout")] = self.dram_in("w_out_%d" % l, [D, D])
                self.w[(l, "lru")] = self.dram_in("lru_w_%d" % l, [128, 2048])
        self.y_out = nc.dram_tensor("y_out", [T, D], F32, kind="ExternalOutput").ap()
        self.bias_hl = nc.dram_tensor("bias_hl", [128, 8192], BF16)
        self.ex1_in = nc.dram_tensor("ex1_in", [128, 136], F32)
        self.ex1_out = nc.dram_tensor("ex1_out", [256, 136], F32)
        self.ex2_in = [nc.dram_tensor("ex2_in%d" % i, [128, 2], F32) for i in range(2)]
        self.ex2_out = [nc.dram_tensor("ex2_out%d" % i, [256, 2], F32) for i in range(2)]
        with ExitStack() as st:
            self.P = Prog(nc, st)
            self.alloc()
            self.setup()
            if "f1" in self.phases:
                self.ffn_prefetch(self.layers[0], 1)
            self.load_x()
            if self.debug:
                self.dump("dbg_x0")
            for li_, l in enumerate(self.layers):
                if li_ > 0:
                    self.P.new_epoch()
                if "f1" in self.phases:
                    self.ffn(l, 1)
                    if self.debug:
                        self.dump("dbg_f1_%d" % l)
                if "mx" in self.phases:
                    self.mixer(l)
                    if self.debug:
                        self.dump("dbg_mx_%d" % l)
                if "f2" in self.phases:
                    self.ffn(l, 2)
                    if self.debug:
                        self.dump("dbg_f2_%d" % l)
            self.store(self.y_out, "y_out", norm=self.last)
            self.P.wait_all("sp", ["y_out"] + self.dbg)
            self.P.emit()
        return nc

    def alloc(self):
        P = self.P
        self.xT = P.sbuf("xT", [128, DC, T], F32)
        self.Rxn = P.sbuf("Rxn", [128, 16384], BF16)
        self.Rw = P.sbuf("Rw", [128, 12288], BF16)
        self.Rh = P.sbuf("Rh", [128, 10240], BF16)
        self.Rxr = P.sbuf("Rxr", [128, 16416], BF16)
        self.Rgg = P.sbuf("Rgg", [128, 8192], BF16)
        self.Rwo = P.sbuf("Rwo", [128, 4096], BF16)
        self.vecs = P.sbuf("vecs_s", [128, NV], F32)
        self.dv = P.sbuf("dv", [128, 64], F32)
        self.sel = P.sbuf("sel_s", [128, 2], F32)
        self.ident = P.sbuf("ident", [128, 128], F32)
        self.identb = P.sbuf("identb", [128, 128], BF16)
        self.onesb = P.sbuf("onesb", [128, 128], BF16)
        self.lruw = P.sbuf("lruw", [128, 16, 128], BF16)
        self.xrh = P.sbuf("xrh", [128, 8], F32)
        self.xrhalo = P.sbuf("xrhalo", [128, 4, 2], F32)
        self.hinit = P.sbuf("hinit", [128, 4], F32)
        self.ex1s = P.sbuf("ex1s", [128, 2, 136], F32)
        self.ex1t = P.sbuf("ex1t", [128, 136], F32)
        self.ex2s = P.sbuf("ex2s", [128, 2, 4], F32)
        self.ex2t = P.sbuf("ex2t", [128, 4], F32)
        self.hAend = P.sbuf("hAend", [128, 4], F32)
        self.vhalo = P.sbuf("vhalo", [128, 128], BF16)
        self.ps = [P.psum("ps%d" % i, [128, 512], F32) for i in range(8)]
        Rxn, Rw, Rh, Rxr, Rgg, Rwo = self.Rxn, self.Rw, self.Rh, self.Rxr, self.Rgg, self.Rwo
        self.xn = Rxn[:, :].rearrange("p (c t) -> p c t", c=DC)
        self.xc = Rxn[:, :].bitcast(F32).rearrange("p (c t) -> p c t", c=4)
        self.yrecb = Rxn[:, :].rearrange("p (c t) -> p c t", c=4)
        self.xf = [Rxn[:, s * 8192:(s + 1) * 8192].bitcast(F32).rearrange("p (c t) -> p c t", c=8) for s in range(2)]
        self.bF = Rxn[:, 0:8192].bitcast(F32)
        self.bH = Rxn[:, 8192:12288]
        self.bL = Rxn[:, 12288:16384]
        self.wg_s = [Rw[:, s * 6144:s * 6144 + 2048].rearrange("p (k f) -> p k f", k=8) for s in range(2)]
        self.wu_s = [Rw[:, s * 6144 + 2048:s * 6144 + 4096].rearrange("p (k f) -> p k f", k=8) for s in range(2)]
        self.wd_s = [Rw[:, s * 6144 + 4096:s * 6144 + 6144].rearrange("p (c d) -> p c d", c=2) for s in range(2)]
        self.win_xg = Rw[:, 0:8192].rearrange("p (k f) -> p k f", k=8)
        self.oN = Rw[:, 8192:12288].bitcast(F32).rearrange("p (c t) -> p c t", c=4)
        self.hT = [Rh[:, s * 4096:(s + 1) * 4096].rearrange("p (c t) -> p c t", c=2) for s in range(2)]
        self.sg = [Rh[:, 8192 + s * 1024:8192 + (s + 1) * 1024].bitcast(F32) for s in range(2)]
        self.sq8 = [Rh[:, s * 4096:(s + 1) * 4096].rearrange("p (c t) -> p c t", c=8) for s in range(2)]
        self.stage = [Rgg[:, s * 2048:(s + 1) * 2048].bitcast(F32) for s in range(2)]
        self.kT = Rh[:, 0:2176]
        self.Vd = [Rh[:, 2176:4352].rearrange("p (b f) -> p b f", b=17),
                   Rh[:, 4352:6528].rearrange("p (b f) -> p b f", b=17)]
        self.yatt = Rh[:, 6528:8576].rearrange("p (c t) -> p c t", c=4)
        self.den = [Rh[:, 8576:9600].bitcast(F32), Rxr[:, 12288:13312].bitcast(F32)]
        self.qT = Rgg[:, :].rearrange("p (b g t) -> p b g t", b=16, g=4)
        self.gg = Rgg[:, :].rearrange("p (c t) -> p c t", c=4)
        self.biasb = Rxr[:, 0:8192].rearrange("p (l t h q) -> p l t h q", l=2, t=4, h=8)
        self.win_qkv = Rxr[:, 8192:14336].rearrange("p (k f) -> p k f", k=8)
        self.pT = Rxr[:, 8192:11264].rearrange("p (b q) -> p b q", b=6)
        self.asd = Rxr[:, 11264:12288].bitcast(F32)
        self.sqa = Rxr[:, 14336:16384].rearrange("p (c t) -> p c t", c=4)
        self.xr = Rxr[:, :].bitcast(F32).rearrange("p (c t) -> p c t", c=4)
        self.wo = Rwo[:, :].rearrange("p (c d) -> p c d", c=4)

        def lt(i):
            return Rh[:, i * 1024:(i + 1) * 1024].bitcast(F32)
        self.tR = [lt(0), lt(1)]
        self.tI = [lt(2), lt(3)]
        self.tS = [lt(4), lt(5)]
        self.xcb = [Rh[:, 6144 + s * 512:6144 + (s + 1) * 512] for s in range(2)]
        self.sq4 = Rh[:, 7168:9216].rearrange("p (c t) -> p c t", c=4)
        self.rsd = Rh[:, 9216:10240].bitcast(F32)

    def nb(self):
        b = self.bank
        self.bank = (self.bank + 1) % 8
        return b

    def setup(self):
        P = self.P
        for s in ("vecs", "sel", "bias", "biashl", "stg0", "stg1", "out0", "out1", "wgu0", "wgu1", "wd0", "wd1",
                  "mixw", "mixb", "winxg", "worec", "ex1a", "ex1b", "ex1c", "ex2a0", "ex2b0", "ex2c0", "ex2a1", "ex2b1", "ex2c1", "dbgsb"):
            P.dsem(s)
        P.D("sp", "vecs", out=self.vecs[:], in_=self.vecs_in, writes=["vecs"])
        P.D("sp", "sel", out=self.sel[:], in_=self.sel_in, writes=["sel"])
        P.D("sp", "bias", out=self.bF, in_=self.bias_in, writes=["bF"])
        P.I("pool", "memset", ap=self.ident[:], constant=0.0, writes=["ident"])
        P.I("pool", "affine_select", out=self.ident[:], in_=self.ident[:], pattern=[[-1, 128]],
            compare_op=ALU.not_equal, fill=1.0, base=0, channel_multiplier=1, reads=["ident"], writes=["ident"])
        P.I("pool", "tensor_copy", out=self.identb[:], in_=self.ident[:], reads=["ident"], writes=["identb"])
        P.I("pool", "memset", ap=self.onesb[:], constant=1.0, writes=["onesb"])
        P.I("dve", "tensor_copy", out=self.bH, in_=self.bF, reads=["bF"], writes=["bH"])
        P.I("dve", "tensor_tensor", out=self.bL, in0=self.bF, in1=self.bH, op=ALU.subtract, reads=["bF", "bH"], writes=["bL"])
        P.D("sp", "biashl", out=self.bias_hl.ap()[:, 0:4096], in_=self.bH, reads=["bH"], writes=["bias_hl"])
        P.D("sp", "biashl", out=self.bias_hl.ap()[:, 4096:8192], in_=self.bL, reads=["bL"], writes=["bias_hl"])
        P.barrier()

    def load_x(self):
        P = self.P
        for tt in range(NB):
            s = tt % 2
            P.D("sp", "stg%d" % s, out=self.stage[s], in_=self.x_in[tt * 128:(tt + 1) * 128, :], writes=["stage%d" % s])
            for half in range(2):
                b = self.nb()
                for j in range(4):
                    dc = half * 4 + j
                    P.I("pe", "transpose", out=self.ps[b][:, j * 128:(j + 1) * 128],
                        in_=self.stage[s][:, dc * 128:(dc + 1) * 128], identity=self.ident[:],
                        reads=["stage%d" % s, "ident"], writes=["ps%d" % b])
                src = self.ps[b][:, :].rearrange("p (c t) -> p c t", c=4)
                dst = self.xT[:, half * 4:half * 4 + 4, tt * 128:(tt + 1) * 128]
                wr = ["xT%d_%d" % (dc, tt // 4) for dc in range(half * 4, half * 4 + 4)]
                if (tt + half) % 2 == 0:
                    P.I("act", "copy", out=dst, in_=src, reads=["ps%d" % b], writes=wr)
                else:
                    P.I("dve", "tensor_copy", out=dst, in_=src, reads=["ps%d" % b], writes=wr)

    def store(self, dst_dram, resname, norm=False):
        P = self.P
        P.barrier()
        gcol = V_FINAL
        for tg in range(NTG):
            ts = slice(tg * 512, (tg + 1) * 512)
            if norm:
                self.rstd_tg(DC, [self.xT[:, dc, ts] for dc in range(DC)], ["xT%d_%d" % (dc, tg) for dc in range(DC)],
                             1.0 / D, self.sq8[tg % 2], "sq8_%d" % (tg % 2), self.sg[tg % 2], "sg%d" % (tg % 2))
                xf = self.xf[tg % 2]
                for dc in range(DC):
                    P.I("dve", "scalar_tensor_tensor", out=xf[:, dc, :], in0=self.xT[:, dc, ts],
                        scalar=self.vecs[:, gcol + dc:gcol + dc + 1], in1=self.sg[tg % 2], op0=ALU.mult, op1=ALU.mult,
                        reads=["xT%d_%d" % (dc, tg), "sg%d" % (tg % 2), "vecs"], writes=["xf%d_%d" % (tg % 2, dc)])
            for ttl in range(4):
                tt = tg * 4 + ttl
                s = tt % 2
                for half in range(2):
                    b = self.nb()
                    for j in range(4):
                        dc = half * 4 + j
                        if norm:
                            src = self.xf[tg % 2][:, dc, ttl * 128:(ttl + 1) * 128]
                            rd = "xf%d_%d" % (tg % 2, dc)
                        else:
                            src = self.xT[:, dc, tt * 128:(tt + 1) * 128]
                            rd = "xT%d_%d" % (dc, tg)
                        P.I("pe", "transpose", out=self.ps[b][:, j * 128:(j + 1) * 128], in_=src, identity=self.ident[:],
                            reads=[rd, "ident"], writes=["ps%d" % b])
                    dst = self.stage[s][:, half * 512:(half + 1) * 512]
                    if half == 0:
                        P.I("act", "copy", out=dst, in_=self.ps[b][:, :], reads=["ps%d" % b], writes=["stage%d_%d" % (s, half)])
                    else:
                        P.I("dve", "tensor_copy", out=dst, in_=self.ps[b][:, :], reads=["ps%d" % b], writes=["stage%d_%d" % (s, half)])
                P.D("sp", "out%d" % s, out=dst_dram[tt * 128:(tt + 1) * 128, :], in_=self.stage[s],
                    reads=["stage%d_0" % s, "stage%d_1" % s], writes=[resname])
        P.barrier()

    def dbg_sb(self, name, ap, shape, dt, reads):
        d = self.nc.dram_tensor(name, list(shape), dt, kind="ExternalOutput").ap()
        self.dbg.append(name)
        self.P.D("sp", "dbgsb", out=d, in_=ap, reads=reads, writes=[name])

    def dump(self, name):
        d = self.nc.dram_tensor(name, [T, D], F32, kind="ExternalOutput").ap()
        self.dbg.append(name)
        self.store(d, name, norm=False)

    def rstd_tg(self, nch, srcs, srcnames, inv_n, sqbuf, sqname, outbuf, outname, bank=None):
        P = self.P
        for c in range(nch):
            P.I("act", "activation", out=sqbuf[:, c, :], in_=srcs[c], func=AF.Square,
                reads=[srcnames[c]], writes=["%s_%d" % (sqname, c)])
        b = self.nb() if bank is None else bank
        for c in range(nch):
            P.I("pe", "matmul", out=self.ps[b][:, :], lhsT=self.onesb[:], rhs=sqbuf[:, c, :], start=(c == 0), stop=(c == nch - 1),
                reads=["%s_%d" % (sqname, c), "onesb"], writes=["ps%d" % b])
        P.I("act", "activation", out=outbuf, in_=self.ps[b][:, :], func=AF.Ln, scale=inv_n, bias=EPS,
            reads=["ps%d" % b], writes=[outname])
        P.I("act", "activation", out=outbuf, in_=outbuf, func=AF.Exp, scale=-0.5, reads=[outname], writes=[outname])

    def rmsnorm_xn(self, gcol):
        P = self.P
        for tg in range(NTG):
            s = tg % 2
            ts = slice(tg * 512, (tg + 1) * 512)
            self.rstd_tg(DC, [self.xT[:, dc, ts] for dc in range(DC)], ["xT%d_%d" % (dc, tg) for dc in range(DC)],
                         1.0 / D, self.sq8[s], "sq8_%d" % s, self.sg[s], "sg%d" % s)
            for dc in range(DC):
                P.I("dve", "scalar_tensor_tensor", out=self.xn[:, dc, ts], in0=self.xT[:, dc, ts],
                    scalar=self.vecs[:, gcol + dc:gcol + dc + 1], in1=self.sg[s], op0=ALU.mult, op1=ALU.mult,
                    reads=["xT%d_%d" % (dc, tg), "sg%d" % s, "vecs"], writes=["xn%d_%d" % (dc, tg)])

    def ffn_load(self, l, k, gi, part):
        P = self.P
        s = gi % 2
        f0 = gi * 256
        if part == "gu":
            wg = self.w[(l, "wg", k)].rearrange("(kc p) f -> p kc f", p=128)[:, :, f0:f0 + 256]
            wu = self.w[(l, "wu", k)].rearrange("(kc p) f -> p kc f", p=128)[:, :, f0:f0 + 256]
            P.D("pool", "wgu%d" % s, out=self.wg_s[s], in_=wg, writes=["wg%d" % s])
            P.D("pool", "wgu%d" % s, out=self.wu_s[s], in_=wu, writes=["wu%d" % s])
        else:
            wd = self.w[(l, "wd", k)].rearrange("(fc p) d -> p fc d", p=128)[:, gi * 2:gi * 2 + 2, :]
            P.D("pool", "wd%d" % s, out=self.wd_s[s], in_=wd, writes=["wd%d" % s])

    def ffn_prefetch(self, l, k):
        if (l, k) in self.ffn_pref:
            return
        self.ffn_pref.add((l, k))
        for gi in (0, 1):
            self.ffn_load(l, k, gi, "gu")
            self.ffn_load(l, k, gi, "d")

    def ffn_gu(self, gi, tgs=range(NTG)):
        P = self.P
        s = gi % 2
        for fc in range(2):
            fs = slice(fc * 128, (fc + 1) * 128)
            for tg in tgs:
                ts = slice(tg * 512, (tg + 1) * 512)
                bg = self.gub
                bu = self.gub + 1
                self.gub = (self.gub + 2) % 4
                for kc in range(DC):
                    P.I("pe", "matmul", out=self.ps[bg][:, :], lhsT=self.wg_s[s][:, kc, fs], rhs=self.xn[:, kc, ts],
                        start=(kc == 0), stop=(kc == DC - 1), reads=["wg%d" % s, "xn%d_%d" % (kc, tg)], writes=["ps%d" % bg])
                for kc in range(DC):
                    P.I("pe", "matmul", out=self.ps[bu][:, :], lhsT=self.wu_s[s][:, kc, fs], rhs=self.xn[:, kc, ts],
                        start=(kc == 0), stop=(kc == DC - 1), reads=["wu%d" % s, "xn%d_%d" % (kc, tg)], writes=["ps%d" % bu])
                sgi = self.sgi
                self.sgi ^= 1
                P.I("act", "activation", out=self.sg[sgi], in_=self.ps[bg][:, :], func=AF.Silu,
                    reads=["ps%d" % bg], writes=["sg%d" % sgi])
                P.I("dve", "tensor_tensor", out=self.hT[s][:, fc, ts], in0=self.ps[bu][:, :], in1=self.sg[sgi], op=ALU.mult,
                    reads=["ps%d" % bu, "sg%d" % sgi], writes=["hT%d_%d_%d" % (s, fc, tg)])

    def ffn_d(self, gi):
        P = self.P
        s = gi % 2
        for dc in range(DC):
            ds_ = slice(dc * 128, (dc + 1) * 128)
            for tg in range(NTG):
                ts = slice(tg * 512, (tg + 1) * 512)
                b = 4 + self.db
                self.db = (self.db + 1) % 4
                for fc in range(2):
                    P.I("pe", "matmul", out=self.ps[b][:, :], lhsT=self.wd_s[s][:, fc, ds_], rhs=self.hT[s][:, fc, ts],
                        start=(fc == 0), stop=(fc == 1), reads=["wd%d" % s, "hT%d_%d_%d" % (s, fc, tg)], writes=["ps%d" % b])
                P.I("dve", "scalar_tensor_tensor", out=self.xT[:, dc, ts], in0=self.ps[b][:, :], scalar=0.5,
                    in1=self.xT[:, dc, ts], op0=ALU.mult, op1=ALU.add,
                    reads=["ps%d" % b, "xT%d_%d" % (dc, tg)], writes=["xT%d_%d" % (dc, tg)])

    def ffn(self, l, k):
        P = self.P
        self.gub = 0
        self.db = 0
        self.sgi = 0
        self.ffn_prefetch(l, k)
        if k == 1 and "mx" in self.phases:
            self.mixer_prefetch(l)
        self.rmsnorm_xn(l * VL + (V_F1G if k == 1 else V_F2G))
        for tg in range(NTG):
            self.ffn_gu(0, [tg])
            self.ffn_gu(1, [tg])
        for gi in range(FG):
            if gi + 2 < FG:
                self.ffn_load(l, k, gi + 2, "gu")
            self.ffn_d(gi)
            if gi + 2 < FG:
                self.ffn_load(l, k, gi + 2, "d")
                self.ffn_gu(gi + 2)
        P.barrier()

    def mixer_prefetch(self, l):
        P = self.P
        win = self.w[(l, "win")].rearrange("(kc p) f -> p kc f", p=128)
        P.D("sp", "mixb", out=self.Rxr[:, 0:8192], in_=self.bias_hl.ap(), reads=["bias_hl"], writes=["biasb"])
        for g in range(4):
            for kv in range(2):
                c0 = 1024 + kv * 256 + g * 64
                P.D("pool", "mixw", out=self.win_qkv[:, :, g * 128 + kv * 64:g * 128 + kv * 64 + 64], in_=win[:, :, c0:c0 + 64],
                    writes=["win_qkv"])
        P.D("pool", "mixw", out=self.win_qkv[:, :, 512:768], in_=win[:, :, 1536:1792], writes=["win_qkv"])
        P.D("pool", "mixw", out=self.wo, in_=self.w[(l, "wout")][512:1024, :].rearrange("(c p) d -> p c d", p=128), writes=["wo"])
        P.D("pool", "mixw", out=self.lruw[:, :, :], in_=self.w[(l, "lru")].rearrange("p (i m) -> p i m", i=16), writes=["lruw"])

    def evac(self, i, out, in_, reads, writes, scale=None):
        P = self.P
        if i % 2 == 0:
            if scale is None:
                P.I("act", "copy", out=out, in_=in_, reads=reads, writes=writes)
            else:
                P.I("act", "mul", out=out, in_=in_, mul=scale, reads=reads, writes=writes)
        else:
            if scale is None:
                P.I("dve", "tensor_copy", out=out, in_=in_, reads=reads, writes=writes)
            else:
                P.I("dve", "tensor_scalar", out=out, in0=in_, scalar1=scale, scalar2=None, op0=ALU.mult, reads=reads, writes=writes)

    def mixer(self, l):
        P = self.P
        o = l * VL
        dv = self.dv
        vecs = self.vecs
        if not ("f1" in self.phases):
            self.mixer_prefetch(l)
        z = dv[:, 40:48]
        p = dv[:, 48:56]
        P.I("act", "activation", out=z, in_=vecs[:, o + V_LAM:o + V_LAM + 8], func=AF.Exp, scale=-1.0, reads=["vecs"], writes=["dvz"])
        P.I("dve", "tensor_scalar", out=p, in0=z, scalar1=-1.0 / 8, scalar2=1.0 / 7, op0=ALU.mult, op1=ALU.add, reads=["dvz"], writes=["dvp"])
        for n in (6, 5, 4, 3, 2, 1):
            P.I("dve", "tensor_tensor", out=p, in0=p, in1=z, op=ALU.mult, reads=["dvp", "dvz"], writes=["dvp"])
            P.I("dve", "tensor_scalar", out=p, in0=p, scalar1=-1.0, scalar2=1.0 / n, op0=ALU.mult, op1=ALU.add, reads=["dvp"], writes=["dvp"])
        P.I("dve", "tensor_tensor", out=p, in0=p, in1=z, op=ALU.mult, reads=["dvp", "dvz"], writes=["dvp"])
        P.I("dve", "tensor_scalar", out=dv[:, 0:8], in0=p, scalar1=-8.0, scalar2=None, op0=ALU.mult, reads=["dvp"], writes=["dvkk"])
        P.I("dve", "tensor_scalar", out=dv[:, 8:16], in0=p, scalar1=-4.0, scalar2=None, op0=ALU.mult, reads=["dvp"], writes=["dvkk"])
        P.I("dve", "tensor_scalar", out=dv[:, 16:24], in0=vecs[:, o + V_BA:o + V_BA + 8], scalar1=0.5, scalar2=None, op0=ALU.mult, reads=["vecs"], writes=["dvkk"])
        P.I("dve", "tensor_scalar", out=dv[:, 24:32], in0=vecs[:, o + V_BX:o + V_BX + 8], scalar1=0.5, scalar2=None, op0=ALU.mult, reads=["vecs"], writes=["dvkk"])
        P.I("act", "activation", out=dv[:, 32:40], in_=vecs[:, o + V_SINK:o + V_SINK + 8], func=AF.Exp, reads=["vecs"], writes=["dvkk"])
        win = self.w[(l, "win")].rearrange("(kc p) f -> p kc f", p=128)
        P.D("pool", "winxg", out=self.win_xg, in_=win[:, :, 0:1024], writes=["win_xg"])
        if self.mx_stop == "pre":
            P.barrier()
            return
        self.rmsnorm_xn(o + V_MIXG)
        if self.mx_stop == "norm":
            P.barrier()
            return
        ei = 0
        for tg in range(NTG):
            ts = slice(tg * 512, (tg + 1) * 512)
            for g in range(4):
                b = self.nb()
                for kc in range(DC):
                    lh = self.win_qkv[:, kc, g * 128:(g + 1) * 128]
                    P.I("pe", "matmul", out=self.ps[b][:, :], lhsT=lh, rhs=self.xn[:, kc, ts], start=(kc == 0), stop=(kc == DC - 1),
                        reads=["win_qkv", "xn%d_%d" % (kc, tg)], writes=["ps%d" % b])
                sl = (g % 2) * 2 + g // 2
                self.evac(ei, self.qT[:, tg * 4:(tg + 1) * 4, sl, :], self.ps[b][:, :].rearrange("p (b t) -> p b t", b=4),
                          ["ps%d" % b], ["qT%d_%d" % (g, tg)], scale=0.125)
                ei += 1
            b = self.nb()
            for kc in range(DC):
                P.I("pe", "matmul", out=self.ps[b][:, :], lhsT=self.win_qkv[:, kc, 512:640], rhs=self.xn[:, kc, ts],
                    start=(kc == 0), stop=(kc == DC - 1), reads=["win_qkv", "xn%d_%d" % (kc, tg)], writes=["ps%d" % b])
            self.evac(ei, self.kT[:, ts], self.ps[b][:, :], ["ps%d" % b], ["kT%d" % tg])
            ei += 1
            b = self.nb()
            for bl in range(4):
                blk = tg * 4 + bl
                for kc in range(DC):
                    P.I("pe", "matmul", out=self.ps[b][:, bl * 128:(bl + 1) * 128], lhsT=self.xn[:, kc, blk * 128:(blk + 1) * 128],
                        rhs=self.win_qkv[:, kc, 640:768], start=(kc == 0), stop=(kc == DC - 1),
                        reads=["win_qkv", "xn%d_%d" % (kc, tg)], writes=["ps%d" % b])
            src = self.ps[b][:, :].rearrange("p (b f) -> p b f", b=4)
            bs = slice(tg * 4, tg * 4 + 4)
            self.evac(0, self.Vd[0][:, bs, 0:64], src[:, :, 0:64], ["ps%d" % b], ["Vd0a%d" % tg])
            self.evac(0, self.Vd[0][:, bs, 64:128], src[:, :, 0:64], ["ps%d" % b], ["Vd0b%d" % tg])
            self.evac(0, self.Vd[1][:, bs, 0:64], src[:, :, 64:128], ["ps%d" % b], ["Vd1a%d" % tg])
            self.evac(0, self.Vd[1][:, bs, 64:128], src[:, :, 64:128], ["ps%d" % b], ["Vd1b%d" % tg])
        if self.mx_stop in ("v", "v1"):
            P.barrier()
            return
        b = self.nb()
        for cc in range(4):
            for kc in range(DC):
                P.I("pe", "matmul", out=self.ps[b][:, cc * 2:cc * 2 + 2], lhsT=self.win_xg[:, kc, cc * 128:(cc + 1) * 128],
                    rhs=self.xn[:, kc, 2046:2048], start=(kc == 0), stop=(kc == DC - 1),
                    reads=["win_xg", "xn%d_3" % kc], writes=["ps%d" % b])
        P.I("dve", "tensor_copy", out=self.xrh[:, :], in_=self.ps[b][:, 0:8], reads=["ps%d" % b], writes=["xrh"])
        if self.mx_stop == "p1":
            P.barrier()
            return
        e1 = self.ex1_in.ap()
        P.D("sp", "ex1a", out=e1[:, 0:64], in_=self.kT[:, 1920:2048].bitcast(F32), reads=["kT3"], writes=["ex1_in"])
        P.D("sp", "ex1a", out=e1[:, 64:96], in_=self.Vd[0][:, 15, 0:64].bitcast(F32), reads=["Vd0a3"], writes=["ex1_in"])
        P.D("sp", "ex1a", out=e1[:, 96:128], in_=self.Vd[1][:, 15, 0:64].bitcast(F32), reads=["Vd1a3"], writes=["ex1_in"])
        P.D("sp", "ex1a", out=e1[:, 128:136], in_=self.xrh[:, :], reads=["xrh"], writes=["ex1_in"])
        P.dma("pool", "ex1c", lambda e: e.collective_compute(
            "AllGather", ALU.bypass, replica_groups=[[0, 1], [2, 3], [4, 5], [6, 7]],
            ins=[self.ex1_in.ap().opt()], outs=[self.ex1_out.ap().opt()]), reads=["ex1_in"], writes=["ex1_out"], inc=1)
        self.attention(l)
        P.barrier()
        if self.mx_stop == "p2":
            return
        P.I("dve", "memset", ap=self.xr[:, :, 0:2], constant=0.0, writes=["xrpad"])
        P.I("dve", "tensor_copy", out=self.xr[:, :, 2050:2052], in_=self.xrhalo[:, :, ::-1], reads=["xrhalo"], writes=["xrhal"])
        ei = 0
        for cc in range(4):
            for tg in range(NTG):
                ts = slice(tg * 512, (tg + 1) * 512)
                b = self.nb()
                for kc in range(DC):
                    P.I("pe", "matmul", out=self.ps[b][:, :], lhsT=self.win_xg[:, kc, cc * 128:(cc + 1) * 128], rhs=self.xn[:, kc, ts],
                        start=(kc == 0), stop=(kc == DC - 1), reads=["win_xg", "xn%d_%d" % (kc, tg)], writes=["ps%d" % b])
                self.evac(1, self.xr[:, cc, 2 + tg * 512:2 + (tg + 1) * 512], self.ps[b][:, :], ["ps%d" % b], ["xr%d_%d" % (cc, tg)])
                b = self.nb()
                for kc in range(DC):
                    P.I("pe", "matmul", out=self.ps[b][:, :], lhsT=self.win_xg[:, kc, 512 + cc * 128:512 + (cc + 1) * 128], rhs=self.xn[:, kc, ts],
                        start=(kc == 0), stop=(kc == DC - 1), reads=["win_xg", "xn%d_%d" % (kc, tg)], writes=["ps%d" % b])
                P.I("act", "activation", out=self.gg[:, cc, ts], in_=self.ps[b][:, :], func=AF.Gelu, reads=["ps%d" % b], writes=["gg%d_%d" % (cc, tg)])
        P.barrier()
        if self.mx_stop == "p3":
            return
        if "f2" in self.phases:
            self.ffn_prefetch(l, 2)
        P.D("pool", "worec", out=self.wo, in_=self.w[(l, "wout")][0:512, :].rearrange("(c p) d -> p c d", p=128), writes=["wo"])
        self.lru(l)
        P.barrier()

    def ex1_receive(self):
        P = self.P
        P.D("sp", "ex1b", out=self.ex1s[:, :, :], in_=self.ex1_out.ap().rearrange("(r p) w -> p r w", p=128),
            reads=["ex1_out"], writes=["ex1s"])
        s0 = self.sel[:, 0:1]
        s1 = self.sel[:, 1:2]
        e0b = self.ex1s[:, 0, 0:128].bitcast(BF16)
        e1b = self.ex1s[:, 1, 0:128].bitcast(BF16)
        etb = self.ex1t[:, 0:128].bitcast(BF16)
        P.I("dve", "tensor_scalar", out=etb, in0=e1b, scalar1=s1, scalar2=None, op0=ALU.mult,
            reads=["ex1s", "sel"], writes=["ex1t"])
        P.I("dve", "scalar_tensor_tensor", out=self.kT[:, 2048:2176], in0=e0b[:, 0:128], scalar=s0, in1=etb[:, 0:128],
            op0=ALU.mult, op1=ALU.add, reads=["ex1s", "ex1t", "sel"], writes=["kT4"])
        P.I("dve", "scalar_tensor_tensor", out=self.vhalo[:, :], in0=e0b[:, 128:256], scalar=s0, in1=etb[:, 128:256],
            op0=ALU.mult, op1=ALU.add, reads=["ex1s", "ex1t", "sel"], writes=["vhalo"])
        for kv in range(2):
            for hf, nm in ((0, "a"), (1, "b")):
                P.I("dve", "tensor_copy", out=self.Vd[kv][:, 16, hf * 64:(hf + 1) * 64], in_=self.vhalo[:, kv * 64:(kv + 1) * 64],
                    reads=["vhalo"], writes=["Vd%d%s4" % (kv, nm)])
        xh0 = self.ex1s[:, 0, 128:136]
        xh1 = self.ex1s[:, 1, 128:136]
        xht = self.ex1t[:, 128:136]
        P.I("dve", "tensor_scalar", out=xht, in0=xh1, scalar1=s1, scalar2=None, op0=ALU.mult, reads=["ex1s", "sel"], writes=["ex1tx"])
        P.I("dve", "scalar_tensor_tensor", out=self.xrhalo[:, :, :].rearrange("p c t -> p (c t)"), in0=xh0, scalar=s0, in1=xht,
            op0=ALU.mult, op1=ALU.add, reads=["ex1s", "ex1tx", "sel"], writes=["xrhalo"])

    def attention(self, l):
        P = self.P
        o = l * VL
        psL = (0, 1, 2)
        esink = self.dv[:, 32:40]
        its = [(n, kvh) for n in range(NB) for kvh in range(2)]

        def kblocks(n):
            kbs = []
            if n > 0:
                kbs.append((0, n - 1))
            kbs.append((1, n))
            kbs.append((2, n + 1) if n < NB - 1 else (3, 16))
            return kbs

        def stage_a(it):
            n, kvh = its[it]
            tg = n // 4
            ks = slice(kvh * 64, (kvh + 1) * 64)
            kbs = kblocks(n)
            pset = it % 2
            for i, (tb, blk) in enumerate(kbs):
                b = psL[i]
                kname = "kT%d" % (blk // 4) if blk < 16 else "kT4"
                P.I("pe", "matmul", out=self.ps[b][:, :], lhsT=self.kT[ks, blk * 128:(blk + 1) * 128],
                    rhs=self.Rgg[ks, n * 512:(n + 1) * 512], start=True, stop=False,
                    reads=[kname] + ["qT%d_%d" % (g, tg) for g in range(4)], writes=["ps%d" % b])
                P.I("pe", "matmul", out=self.ps[b][:, :], lhsT=self.identb[:], rhs=self.Rxr[:, tb * 1024 + kvh * 512:tb * 1024 + kvh * 512 + 512],
                    start=False, stop=True, reads=["identb", "biasb"], writes=["ps%d" % b])
                P.I("act", "activation", out=self.pT[:, pset * 3 + i, :], in_=self.ps[b][:, :], func=AF.Exp,
                    reads=["ps%d" % b], writes=["pT%d_%d" % (pset, i)])

        def stage_b(it):
            n, kvh = its[it]
            tg = n // 4
            nq = n % 4
            ks = slice(kvh * 64, (kvh + 1) * 64)
            kbs = kblocks(n)
            pset = it % 2
            bO = 3 + pset
            bS = 5
            psO = self.ps[bO]
            nk = len(kbs)
            for i, (tb, blk) in enumerate(kbs):
                pv = self.pT[:, pset * 3 + i, :]
                bn = blk // 4 if blk < 16 else 4
                rd = ["pT%d_%d" % (pset, i)]
                P.I("pe", "matmul", out=psO[:, :], lhsT=self.Vd[kvh][:, blk, :], rhs=pv,
                    start=(i == 0), stop=(i == nk - 1), reads=rd + ["Vd%da%d" % (kvh, bn), "Vd%db%d" % (kvh, bn)], writes=["ps%d" % bO])
            for i, (tb, blk) in enumerate(kbs):
                pv = self.pT[:, pset * 3 + i, :]
                P.I("pe", "matmul", out=self.ps[bS][:, :], lhsT=self.onesb[:], rhs=pv,
                    start=(i == 0), stop=(i == nk - 1), reads=["pT%d_%d" % (pset, i), "onesb"], writes=["ps%d" % bS])
            dbuf = self.den[pset]
            dn = "den%d" % pset
            den = dbuf.rearrange("p (g t) -> p g t", g=4)
            P.I("dve", "tensor_tensor", out=den, in0=self.ps[bS][:, :].rearrange("p (g t) -> p g t", g=4),
                in1=esink[:, kvh * 4:(kvh + 1) * 4].unsqueeze(2).broadcast_to([128, 4, 128]), op=ALU.add,
                reads=["ps%d" % bS, "dvkk"], writes=[dn])
            P.I("act", "activation", out=dbuf, in_=dbuf, func=AF.Ln, reads=[dn], writes=[dn])
            P.I("act", "activation", out=dbuf, in_=dbuf, func=AF.Exp, scale=-1.0, reads=[dn], writes=[dn])
            on = self.oN[:, kvh * 2:kvh * 2 + 2, nq * 128:(nq + 1) * 128]
            P.I("dve", "tensor_tensor", out=on[0:64], in0=psO[0:64, 0:256].rearrange("p (j t) -> p j t", j=2),
                in1=dbuf[0:64, 0:256].rearrange("p (j t) -> p j t", j=2), op=ALU.mult,
                reads=["ps%d" % bO, dn], writes=["oNe%d_%d" % (kvh, nq)])
            P.I("dve", "tensor_tensor", out=on[64:128], in0=psO[64:128, 256:512].rearrange("p (j t) -> p j t", j=2),
                in1=dbuf[64:128, 256:512].rearrange("p (j t) -> p j t", j=2), op=ALU.mult,
                reads=["ps%d" % bO, dn], writes=["oNo%d_%d" % (kvh, nq)])

        stage_a(0)
        for it in range(len(its)):
            n, kvh = its[it]
            tg = n // 4
            nq = n % 4
            if n == 8 and kvh == 0:
                self.ex1_receive()
            if it + 1 < len(its):
                stage_a(it + 1)
            stage_b(it)
            if kvh == 0:
                continue
            if nq == 3 and tg == 0 and self.debug == 3:
                self.dbg_sb("dbg_oN", self.Rw[:, 8192:12288], [128, 4096], BF16,
                            ["oNe%d_%d" % (k_, q_) for k_ in range(2) for q_ in range(4)] + ["oNo%d_%d" % (k_, q_) for k_ in range(2) for q_ in range(4)])
                self.dbg_sb("dbg_qT", self.Rgg[:, :], [128, 8192], BF16, ["qT%d_%d" % (g_, t_) for g_ in range(4) for t_ in range(4)])
                self.dbg_sb("dbg_kv", self.Rh[:, 0:6528], [128, 6528], BF16, ["kT%d" % i for i in range(5)])
                self.dbg_sb("dbg_pT", self.Rxr[:, 8192:11264], [128, 3072], BF16, ["pT%d_%d" % (a_, b_) for a_ in range(2) for b_ in range(3)])
            if nq == 3:
                ts = slice(tg * 512, (tg + 1) * 512)
                onames = [["oNe%d_%d" % (c // 2, q) for q in range(4)] + ["oNo%d_%d" % (c // 2, q) for q in range(4)] for c in range(4)]
                P_ = self.P
                for c in range(4):
                    P_.I("act", "activation", out=self.sqa[:, c, :], in_=self.oN[:, c, :], func=AF.Square, reads=onames[c], writes=["sqa%d" % c])
                b = 6
                for c in range(4):
                    P_.I("pe", "matmul", out=self.ps[b][:, :], lhsT=self.onesb[:], rhs=self.sqa[:, c, :], start=(c == 0), stop=(c == 3),
                         reads=["sqa%d" % c, "onesb"], writes=["ps%d" % b])
                P_.I("act", "activation", out=self.asd, in_=self.ps[b][:, :], func=AF.Ln, scale=1.0 / 512, bias=EPS, reads=["ps%d" % b], writes=["asd"])
                P_.I("act", "activation", out=self.asd, in_=self.asd, func=AF.Exp, scale=-0.5, reads=["asd"], writes=["asd"])
                for c in range(4):
                    P_.I("dve", "scalar_tensor_tensor", out=self.yatt[:, c, :], in0=self.oN[:, c, :],
                         scalar=self.vecs[:, o + V_ATTG + c:o + V_ATTG + c + 1], in1=self.asd, op0=ALU.mult, op1=ALU.mult,
                         reads=onames[c] + ["asd", "vecs"], writes=["yatt%d" % c])
                for dc in range(DC):
                    b = 6 + (dc + 1) % 2
                    for c in range(4):
                        P_.I("pe", "matmul", out=self.ps[b][:, :], lhsT=self.wo[:, c, dc * 128:(dc + 1) * 128], rhs=self.yatt[:, c, :],
                             start=(c == 0), stop=(c == 3), reads=["wo", "yatt%d" % c], writes=["ps%d" % b])
                    P_.I("dve", "tensor_tensor", out=self.xT[:, dc, ts], in0=self.ps[b][:, :], in1=self.xT[:, dc, ts], op=ALU.add,
                         reads=["ps%d" % b, "xT%d_%d" % (dc, tg)], writes=["xT%d_%d" % (dc, tg)])

    def lru_T(self, l, di, cc, tt, ui):
        P = self.P
        dv = self.dv
        s = ui % 2
        ts = slice(tt * 512, (tt + 1) * 512)
        xc = self.xc[:, cc, ts]
        xcn = "xc%d_%d" % (cc, tt)
        col = di * 4 + cc
        P.I("act", "copy", out=self.xcb[s], in_=xc, reads=[xcn], writes=["xcb%d" % s])
        bR = (ui % 2) * 2
        bI = bR + 1
        P.I("pe", "matmul", out=self.ps[bR][:, :], lhsT=self.lruw[:, (di * 2 + 0) * 4 + cc, :], rhs=self.xcb[s], start=True, stop=True,
            reads=["lruw", "xcb%d" % s], writes=["ps%d" % bR])
        P.I("pe", "matmul", out=self.ps[bI][:, :], lhsT=self.lruw[:, (di * 2 + 1) * 4 + cc, :], rhs=self.xcb[s], start=True, stop=True,
            reads=["lruw", "xcb%d" % s], writes=["ps%d" % bI])
        tR, tI, tS = self.tR[s], self.tI[s], self.tS[s]
        P.I("act", "activation", out=tR, in_=self.ps[bR][:, :], func=AF.Tanh, scale=0.5, bias=dv[:, 16 + col:17 + col],
            reads=["ps%d" % bR, "dvkk"], writes=["tR%d" % s])
        P.I("act", "activation", out=tI, in_=self.ps[bI][:, :], func=AF.Tanh, scale=0.5, bias=dv[:, 24 + col:25 + col],
            reads=["ps%d" % bI, "dvkk"], writes=["tI%d" % s])
        P.I("act", "activation", out=tS, in_=tR, func=AF.Exp, scale=dv[:, col:col + 1], bias=dv[:, col:col + 1],
            reads=["tR%d" % s, "dvkk"], writes=["tS%d" % s])
        P.I("act", "activation", out=tR, in_=tR, func=AF.Exp, scale=dv[:, 8 + col:9 + col], bias=dv[:, 8 + col:9 + col],
            reads=["tR%d" % s, "dvkk"], writes=["tR%d" % s])
        return s

    def lru_S(self, s):
        self.P.I("act", "activation", out=self.tS[s], in_=self.tS[s], func=AF.Sqrt, scale=-0.25, bias=0.25,
                 reads=["tS%d" % s], writes=["tS%d" % s])

    def lru_U(self, s, cc, tt):
        P = self.P
        xc = self.xc[:, cc, tt * 512:(tt + 1) * 512]
        tI, tS = self.tI[s], self.tS[s]
        P.I("dve", "scalar_tensor_tensor", out=tI, in0=tI, scalar=1.0, in1=xc, op0=ALU.add, op1=ALU.mult,
            reads=["tI%d" % s, "xc%d_%d" % (cc, tt)], writes=["tI%d" % s])
        P.I("dve", "tensor_tensor", out=tI, in0=tI, in1=tS, op=ALU.mult, reads=["tI%d" % s, "tS%d" % s], writes=["tI%d" % s])

    def conv_tile(self, l, cc, tt):
        P = self.P
        o = l * VL
        vecs = self.vecs
        t0 = tt * 512
        out = self.xc[:, cc, t0:t0 + 512]
        rd = ["xr%d_%d" % (cc, tt), "vecs"]
        rd.append("xr%d_%d" % (cc, tt - 1) if tt > 0 else "xrpad")
        rd.append("xr%d_%d" % (cc, tt + 1) if tt < NTG - 1 else "xrhal")
        wn = "xc%d_%d" % (cc, tt)
        wc = o + V_CONV + cc * 5
        bcol = vecs[:, o + V_CONVB + cc:o + V_CONVB + cc + 1]
        if cc < 4:
            P.I("dve", "tensor_scalar", out=out, in0=self.xr[:, cc, t0:t0 + 512], scalar1=vecs[:, wc:wc + 1],
                scalar2=bcol, op0=ALU.mult, op1=ALU.add, reads=rd, writes=[wn])
            for j in range(1, 5):
                P.I("dve", "scalar_tensor_tensor", out=out, in0=self.xr[:, cc, t0 + j:t0 + j + 512], scalar=vecs[:, wc + j:wc + j + 1],
                    in1=out, op0=ALU.mult, op1=ALU.add, reads=rd + [wn], writes=[wn])
        else:
            tmp = self.rsd
            P.I("pool", "tensor_scalar", out=out, in0=self.xr[:, cc, t0:t0 + 512], scalar1=vecs[:, wc:wc + 1],
                scalar2=bcol, op0=ALU.mult, op1=ALU.add, reads=rd, writes=[wn])
            for j in range(1, 5):
                P.I("pool", "tensor_scalar", out=tmp, in0=self.xr[:, cc, t0 + j:t0 + j + 512], scalar1=vecs[:, wc + j:wc + j + 1],
                    scalar2=0.0, op0=ALU.mult, op1=ALU.add, reads=rd, writes=["rsd"])
                P.I("pool", "tensor_tensor", out=out, in0=out, in1=tmp, op=ALU.add, reads=[wn, "rsd"], writes=[wn])

    def lru(self, l):
        P = self.P
        o = l * VL
        vecs = self.vecs
        groups = [[0, 1], [2, 3], [4, 5], [6, 7]]
        for cc in (0, 1):
            self.conv_tile(l, cc, 0)
        ui = 0
        for cp in range(2):
            ccs = (2 * cp, 2 * cp + 1)
            for tt in range(NTG):
                ss = []
                for cc in ccs:
                    ss.append(self.lru_T(l, 0, cc, tt, ui))
                    ui += 1
                for s in ss:
                    self.lru_S(s)
                for s, cc in zip(ss, ccs):
                    self.lru_U(s, cc, tt)
                    if tt + 1 < NTG:
                        self.conv_tile(l, cc, tt + 1)
                    elif cp == 0:
                        self.conv_tile(l, cc + 2, 0)
                    t0 = tt * 512
                    hn = "xr%d_%d" % (cc, tt)
                    init = 0.0 if tt == 0 else self.xr[:, cc, 2 + t0 - 1:2 + t0]
                    rd = ["tR%d" % s, "tI%d" % s] + ([] if tt == 0 else ["xr%d_%d" % (cc, tt - 1)])
                    P.I("dve", "tensor_tensor_scan", out=self.xr[:, cc, 2 + t0:2 + t0 + 512], data0=self.tR[s], data1=self.tI[s],
                        initial=init, op0=ALU.mult, op1=ALU.add, reads=rd, writes=[hn])
            hae = self.hAend[:, 2 * cp:2 * cp + 2]
            P.I("dve", "tensor_copy", out=hae, in_=self.xr[:, 2 * cp:2 * cp + 2, 2049], reads=["xr%d_3" % cc for cc in ccs], writes=["hAend%d" % cp])
            P.D("sp", "ex2a%d" % cp, out=self.ex2_in[cp].ap(), in_=hae, reads=["hAend%d" % cp], writes=["ex2_in%d" % cp])
            P.dma("pool", "ex2c%d" % cp, (lambda cp_: lambda e: e.collective_compute(
                "AllGather", ALU.bypass, replica_groups=groups,
                ins=[self.ex2_in[cp_].ap().opt()], outs=[self.ex2_out[cp_].ap().opt()]))(cp),
                reads=["ex2_in%d" % cp], writes=["ex2_out%d" % cp], inc=1)
        for cp in range(2):
            ccs = (2 * cp, 2 * cp + 1)
            cs = slice(2 * cp, 2 * cp + 2)
            P.D("sp", "ex2b%d" % cp, out=self.ex2s[:, :, cs], in_=self.ex2_out[cp].ap().rearrange("(r p) w -> p r w", p=128),
                reads=["ex2_out%d" % cp], writes=["ex2s%d" % cp])
            P.I("dve", "tensor_scalar", out=self.ex2t[:, cs], in0=self.ex2s[:, 1, cs], scalar1=self.sel[:, 1:2], scalar2=None, op0=ALU.mult,
                reads=["ex2s%d" % cp, "sel"], writes=["ex2t%d" % cp])
            P.I("dve", "scalar_tensor_tensor", out=self.hinit[:, cs], in0=self.ex2s[:, 0, cs], scalar=self.sel[:, 0:1], in1=self.ex2t[:, cs],
                op0=ALU.mult, op1=ALU.add, reads=["ex2s%d" % cp, "ex2t%d" % cp, "sel"], writes=["hinit%d" % cp])
            for tt in range(NTG - 1, -1, -1):
                t0 = tt * 512
                ts = slice(t0, t0 + 512)
                ss = []
                for cc in ccs:
                    ss.append(self.lru_T(l, 1, cc, tt, ui))
                    ui += 1
                for s in ss:
                    self.lru_S(s)
                for s, cc in zip(ss, ccs):
                    self.lru_U(s, cc, tt)
                    xcn = "xc%d_%d" % (cc, tt)
                    hn = "xr%d_%d" % (cc, tt)
                    if tt == NTG - 1:
                        init = self.hinit[:, cc:cc + 1]
                        rd = ["hinit%d" % cp]
                    else:
                        init = self.xc[:, cc, t0 + 512:t0 + 513]
                        rd = ["xc%d_%d" % (cc, tt + 1)]
                    P.I("dve", "tensor_tensor_scan", out=self.xc[:, cc, ts][:, ::-1], data0=self.tR[s][:, ::-1], data1=self.tI[s][:, ::-1],
                        initial=init, op0=ALU.mult, op1=ALU.add, reads=["tR%d" % s, "tI%d" % s] + rd, writes=[xcn])
                    hA = self.xr[:, cc, 2 + t0:2 + t0 + 512]
                    P.I("dve", "tensor_tensor", out=hA, in0=hA, in1=self.xc[:, cc, ts], op=ALU.add, reads=[hn, xcn], writes=[hn])
                    P.I("dve", "tensor_tensor", out=hA, in0=hA, in1=self.gg[:, cc, ts], op=ALU.mult, reads=[hn, "gg%d_%d" % (cc, tt)], writes=[hn])
                if cp == 0:
                    continue
                ysrc = [self.xr[:, cc, 2 + t0:2 + t0 + 512] for cc in range(4)]
                ynames = ["xr%d_%d" % (cc, tt) for cc in range(4)]
                self.rstd_tg(4, ysrc, ynames, 1.0 / 512, self.sq4, "sq4", self.rsd, "rsd", bank=4)
                for cc in range(4):
                    yb = self.yrecb[:, cc, tt * 1024 + 512:tt * 1024 + 1024]
                    P.I("dve", "scalar_tensor_tensor", out=yb, in0=ysrc[cc], scalar=vecs[:, o + V_RECG + cc:o + V_RECG + cc + 1], in1=self.rsd,
                        op0=ALU.mult, op1=ALU.mult, reads=[ynames[cc], "rsd", "vecs"], writes=["yrb%d_%d" % (cc, tt)])
                for dc in range(DC):
                    b = 5 + dc % 3
                    for cc in range(4):
                        yb = self.yrecb[:, cc, tt * 1024 + 512:tt * 1024 + 1024]
                        P.I("pe", "matmul", out=self.ps[b][:, :], lhsT=self.wo[:, cc, dc * 128:(dc + 1) * 128], rhs=yb,
                            start=(cc == 0), stop=(cc == 3), reads=["wo", "yrb%d_%d" % (cc, tt)], writes=["ps%d" % b])
                    P.I("dve", "tensor_tensor", out=self.xT[:, dc, ts], in0=self.ps[b][:, :], in1=self.xT[:, dc, ts], op=ALU.add,
                        reads=["ps%d" % b, "xT%d_%d" % (dc, tt)], writes=["xT%d_%d" % (dc, tt)])


_HPERM = [0, 2, 1, 3, 4, 6, 5, 7]
_N_BUCKETS = 32
_MAX_DIST = 128


def _t5_bucket(rel):
    half = _N_BUCKETS // 2
    max_exact = half // 2
    ret = (rel > 0).astype(np.int64) * half
    n = np.abs(rel)
    n_f = np.maximum(n, 1).astype(np.float32)
    large = max_exact + (np.log(n_f / np.float32(max_exact)) / np.float32(math.log(_MAX_DIST / max_exact))
                         * np.float32(half - max_exact)).astype(np.int32)
    large = np.minimum(large, half - 1)
    return ret + np.where(n < max_exact, n, large)


def _bias_tables(rel_bias, flip):
    j = np.arange(128)[:, None]
    t = np.arange(128)[None, :]
    out = np.empty((128, 4, 8, 128), np.float32)
    rels = [(-128 + j - t), (j - t), (128 + j - t), (255 - j - t)]
    for ti, rel in enumerate(rels):
        rel_true = -rel if flip else rel
        b = _t5_bucket(rel_true)
        tab = rel_bias[b]
        tab = np.transpose(tab, (0, 2, 1))[:, _HPERM, :]
        mask = (np.abs(rel) <= 128)[:, None, :]
        out[:, ti] = np.where(mask, tab, np.float32(-1e30))
    return out.reshape(128, 4096)


def _vecs(inp, r):
    v = np.zeros((128, NV), np.float32)

    def chunks(a, n):
        return np.ascontiguousarray(a.reshape(n, 128).T)
    for l in range(DEPTH):
        o = l * VL
        v[:, o + V_F1G:o + V_F1G + 8] = chunks(inp["ffn1_norm"][l], 8)
        v[:, o + V_MIXG:o + V_MIXG + 8] = chunks(inp["mix_norm"][l], 8)
        v[:, o + V_F2G:o + V_F2G + 8] = chunks(inp["ffn2_norm"][l], 8)
        cw = inp["conv_w"][l]
        w5 = np.zeros((5, 512), np.float32)
        if r == 0:
            w5[0:4] = cw
        else:
            w5[1:5] = cw[::-1]
        for cc in range(4):
            v[:, o + V_CONV + cc * 5:o + V_CONV + cc * 5 + 5] = w5[:, cc * 128:(cc + 1) * 128].T
        v[:, o + V_CONVB:o + V_CONVB + 4] = chunks(inp["conv_b"][l], 4)
        dirs = (0, 1) if r == 0 else (1, 0)
        for di, dsrc in enumerate(dirs):
            v[:, o + V_BA + di * 4:o + V_BA + di * 4 + 4] = chunks(inp["lru_b_a"][l, dsrc], 4)
            v[:, o + V_BX + di * 4:o + V_BX + di * 4 + 4] = chunks(inp["lru_b_x"][l, dsrc], 4)
            v[:, o + V_LAM + di * 4:o + V_LAM + di * 4 + 4] = chunks(inp["lru_lambda"][l, dsrc], 4)
        v[:, o + V_RECG:o + V_RECG + 4] = chunks(inp["lru_out_norm"][l], 4)
        v[:, o + V_ATTG:o + V_ATTG + 4] = chunks(inp["attn_out_norm"][l], 4)
        v[:, o + V_SINK:o + V_SINK + 8] = inp["attn_sink"][l][_HPERM][None, :]
    v[:, V_FINAL:V_FINAL + 8] = chunks(inp["final_norm"], 8)
    return v


def _lru_w(inp, l, r):
    out = np.zeros((128, 16, 128), np.float32)
    dirs = (0, 1) if r == 0 else (1, 0)
    for di, dsrc in enumerate(dirs):
        for ki, key in enumerate(("lru_w_a", "lru_w_x")):
            wsrc = inp[key][l, dsrc]
            for cc in range(4):
                idx = (di * 2 + ki) * 4 + cc
                out[0:64, idx, 0:64] = wsrc[2 * cc]
                out[64:128, idx, 64:128] = wsrc[2 * cc + 1]
    return out.reshape(128, 2048)


_CACHE = {}


def _get_nc(layers, last, debug, phases):
    key = (tuple(layers), last, debug, tuple(phases))
    if key not in _CACHE:
        _CACHE[key] = K(layers, last, debug, phases)
        _CACHE[key].build()
    return _CACHE[key]


def _core_inputs(inp, xs, layers, names):
    f32 = lambda a: np.ascontiguousarray(a, dtype=np.float32)
    shared = {}
    for r in (0, 1):
        shared[("vecs", r)] = _vecs(inp, r)
        shared[("bias", r)] = _bias_tables(f32(inp["rel_bias"]), r == 1)
        shared[("sel", r)] = np.tile(np.array([[0.0, 1.0]] if r == 0 else [[1.0, 0.0]], np.float32), (128, 1))
        for l in layers:
            if ("lru_w_%d" % l) in names:
                shared[("lru_w_%d" % l, r)] = _lru_w(inp, l, r)
    wmap = {"f1_wg": "ffn1_w_gate", "f1_wu": "ffn1_w_up", "f1_wd": "ffn1_w_down", "f2_wg": "ffn2_w_gate",
            "f2_wu": "ffn2_w_up", "f2_wd": "ffn2_w_down", "w_in": "w_in", "w_out": "w_out"}
    maps = []
    for c in range(NCORES):
        r = c % 2
        m = {}
        for n in names:
            if n == "x_in":
                m[n] = xs[c]
            elif (n, r) in shared:
                m[n] = shared[(n, r)]
            else:
                base, l = n.rsplit("_", 1)
                m[n] = f32(inp[wmap[base]][int(l)])
        maps.append(m)
    return maps


def _shard_x(x):
    xs = []
    for c in range(NCORES):
        b, r = c // 2, c % 2
        if r == 0:
            xs.append(np.ascontiguousarray(x[b, 0:T]))
        else:
            xs.append(np.ascontiguousarray(x[b, T:2 * T][::-1]))
    return xs


def _unshard(ys):
    out = np.empty((4, 2 * T, D), np.float32)
    for c in range(NCORES):
        b, r = c // 2, c % 2
        if r == 0:
            out[b, 0:T] = ys[c]
        else:
            out[b, T:2 * T] = ys[c][::-1]
    return out


def run_layers(inp, xs, layers, last, debug=False, phases=("f1", "mx", "f2")):
    k = _get_nc(layers, last, debug, phases)
    maps = _core_inputs(inp, xs, layers, k.in_names)
    res = run_bass_kernel_spmd(k.nc, maps, core_ids=list(range(NCORES)))
    return res.results, k


LAUNCH_GROUPS = [[0, 1, 2, 3]]


def kernel(**inputs):
    inp = {k: np.asarray(v) for k, v in inputs.items()}
    xs = _shard_x(np.ascontiguousarray(inp["x"], dtype=np.float32))
    for gi, layers in enumerate(LAUNCH_GROUPS):
        last = gi == len(LAUNCH_GROUPS) - 1
        results, _ = run_layers(inp, xs, layers, last)
        xs = [results[c]["y_out"] for c in range(NCORES)]
    return _unshard(xs)
```
